# Optimizing a Trainium2 kernel written in Bass

```python
import math
import jax
import jax.numpy as jnp
from jax import lax
import numpy as np

D_MODEL = 1024
BATCH = 16
SEQ = 2048
DEPTH = 2
DEC_BATCH = 32
DEC_SEQ = 8
PAST_LEN = 16384
PAGE_SIZE = 128

EPS = 1e-6
ROPE_THETA = 500000.0
NEG = -1e30
BIG = 1e9
N_AB = (DEPTH + 1) // 2
N_NSA = DEPTH // 2
SCONV_DIM = D_MODEL
SCONV_W = 3
SSD_HEADS = 16
SSD_HD = 64
SSD_DIM = SSD_HEADS * SSD_HD
SSD_GROUPS = 4
SSD_N = 128
SSD_CONV_W = 4
SSD_CONV_DIM = SSD_DIM + 2 * SSD_GROUPS * SSD_N
SSD_CHUNK = 64
IN0_DIM = 3 * SCONV_DIM + SSD_DIM + SSD_CONV_DIM + SSD_HEADS
NSA_HEADS = 16
NSA_KV = 4
NSA_HD = 64
NSA_HPG = NSA_HEADS // NSA_KV
ROT_DIM = NSA_HD // 4
CMP_BLOCK = 32
CMP_STRIDE = CMP_BLOCK // 2
SLC_BLOCK = 64
SLC_TOP = 16
WINDOW = 512
NSA_Q_CHUNK = 32
KV_W = NSA_KV * NSA_HD
IN1_DIM = NSA_HEADS * NSA_HD + 6 * KV_W + 3 * NSA_HEADS
N_MEM = 256
MEM_HEADS = 4
MEM_HD = 128
D_FF = -(-8 * D_MODEL // (3 * 256)) * 256

kernel_name = 'hybrid_shortconv_ssd_nsa_memory_step'


def rmsnorm(x, g):
    xf = x.astype(jnp.float32)
    y = xf * lax.rsqrt(jnp.mean(xf * xf, axis=-1, keepdims=True) + EPS)
    return (y * g.astype(jnp.float32)).astype(x.dtype)


def rope_partial(x, pos):
    half = ROT_DIM // 2
    inv = ROPE_THETA ** (-jnp.arange(half, dtype=jnp.float32) / half)
    ang = pos.astype(jnp.float32)[:, None] * inv[None, :]
    cos = jnp.cos(ang)[:, None, :]
    sin = jnp.sin(ang)[:, None, :]
    xr = x[..., :ROT_DIM].astype(jnp.float32)
    x1, x2 = xr[..., :half], xr[..., half:]
    rot = jnp.concatenate([x1 * cos - x2 * sin, x2 * cos + x1 * sin], axis=-1)
    return jnp.concatenate([rot.astype(x.dtype), x[..., ROT_DIM:]], axis=-1)


def causal_dwconv(u, buf, w):
    width = w.shape[0]
    T = u.shape[1]
    full = jnp.concatenate([buf.astype(u.dtype), u], axis=1)
    y = full[:, 0:T] * w[0]
    for k in range(1, width):
        y = y + full[:, k:k + T] * w[k]
    return y, full[:, T:]


def swiglu(h, wg, wu, wd):
    return (jax.nn.silu(h @ wg) * (h @ wu)) @ wd


def ssd_scan(x, dt, A, Bm, Cm, h0):
    Bsz, T = x.shape[:2]
    G, Hg, P, N = SSD_GROUPS, SSD_HEADS // SSD_GROUPS, SSD_HD, SSD_N
    Lc = SSD_CHUNK if T % SSD_CHUNK == 0 else T
    nc = T // Lc
    f32 = jnp.float32
    xf = x.astype(f32).reshape(Bsz, nc, Lc, G, Hg, P)
    dtc = dt.reshape(Bsz, nc, Lc, G, Hg)
    Bc = Bm.astype(f32).reshape(Bsz, nc, Lc, G, N)
    Cc = Cm.astype(f32).reshape(Bsz, nc, Lc, G, N)
    cum = jnp.cumsum(dtc * A.reshape(G, Hg), axis=2)
    seg = cum[:, :, :, None] - cum[:, :, None, :]
    causal = jnp.tril(jnp.ones((Lc, Lc), dtype=bool))[:, :, None, None]
    Lmat = jnp.exp(jnp.where(causal, seg, -jnp.inf))
    xdt = xf * dtc[..., None]
    CB = jnp.einsum('bcign,bcjgn->bcijg', Cc, Bc)
    y_diag = jnp.einsum('bcijg,bcijgh,bcjghp->bcighp', CB, Lmat, xdt)
    decay_to_end = jnp.exp(cum[:, :, -1:] - cum)
    chunk_states = jnp.einsum('bcjgn,bcjgh,bcjghp->bcghpn', Bc, decay_to_end, xdt)
    chunk_decay = jnp.exp(cum[:, :, -1])

    def step(h, inp):
        st, dec = inp
        return h * dec[..., None, None] + st, h

    h_final, h_starts = lax.scan(step, h0.astype(f32).reshape(Bsz, G, Hg, P, N),
                                 (jnp.moveaxis(chunk_states, 1, 0), jnp.moveaxis(chunk_decay, 1, 0)))
    h_starts = jnp.moveaxis(h_starts, 0, 1)
    y_off = jnp.einsum('bcign,bcghpn,bcigh->bcighp', Cc, h_starts, jnp.exp(cum))
    y = (y_diag + y_off).reshape(Bsz, T, SSD_HEADS, P)
    return y, h_final.reshape(Bsz, SSD_HEADS, P, N)


def mixer_ab(h, sconv_buf, ssd_conv_buf, ssd_h0, w_in, sconv_w, ssd_conv_w, ssd_conv_b,
             dt_bias, a_log, d_skip, ssd_norm, w_out):
    Bsz, T, _ = h.shape
    f32 = jnp.float32
    proj = h @ w_in
    s_h, s_b, s_c, z, xbc, dt_raw = jnp.split(
        proj, [SCONV_DIM, 2 * SCONV_DIM, 3 * SCONV_DIM, 3 * SCONV_DIM + SSD_DIM,
               3 * SCONV_DIM + SSD_DIM + SSD_CONV_DIM], axis=-1)
    conv_u, new_sconv = causal_dwconv(s_c * s_h, sconv_buf, sconv_w)
    y_a = s_b * conv_u
    xbc_c, new_ssd_conv = causal_dwconv(xbc, ssd_conv_buf, ssd_conv_w)
    xbc_c = jax.nn.silu(xbc_c + ssd_conv_b)
    xs, Bm, Cm = jnp.split(xbc_c, [SSD_DIM, SSD_DIM + SSD_GROUPS * SSD_N], axis=-1)
    xs = xs.reshape(Bsz, T, SSD_HEADS, SSD_HD)
    dt = jax.nn.softplus(dt_raw.astype(f32) + dt_bias.astype(f32))
    A = -jnp.exp(a_log.astype(f32))
    y_ssd, h_new = ssd_scan(xs, dt, A, Bm.reshape(Bsz, T, SSD_GROUPS, SSD_N),
                            Cm.reshape(Bsz, T, SSD_GROUPS, SSD_N), ssd_h0)
    y_ssd = y_ssd + d_skip.astype(f32)[:, None] * xs.astype(f32)
    yg = (y_ssd.reshape(Bsz, T, SSD_DIM) * jax.nn.silu(z.astype(f32))).reshape(Bsz, T, SSD_GROUPS, -1)
    yg = yg * lax.rsqrt(jnp.mean(yg * yg, axis=-1, keepdims=True) + EPS)
    y_b = (yg.reshape(Bsz, T, SSD_DIM) * ssd_norm.astype(f32)).astype(h.dtype)
    y = jnp.concatenate([y_a, y_b], axis=-1) @ w_out
    return y, new_sconv, new_ssd_conv, h_new.astype(ssd_h0.dtype)


def compress_blocks(rows, pe, w1, w2):
    Bsz, L = rows.shape[:2]
    nh = L // CMP_STRIDE
    halves = rows[:, :nh * CMP_STRIDE].reshape(Bsz, nh, CMP_STRIDE, NSA_KV, NSA_HD)
    a = jnp.einsum('bnjgd,jde->bnge', halves + pe[:CMP_STRIDE, None, :], w1[:CMP_STRIDE])
    b = jnp.einsum('bnjgd,jde->bnge', halves + pe[CMP_STRIDE:, None, :], w1[CMP_STRIDE:])
    return jax.nn.silu(a[:, :-1] + b[:, 1:]) @ w2


def block_overlap(n_cmp, n_slc):
    cs = jnp.arange(n_cmp)[:, None] * CMP_STRIDE
    ss = jnp.arange(n_slc)[None, :] * SLC_BLOCK
    ov = jnp.minimum(cs + CMP_BLOCK, ss + SLC_BLOCK) - jnp.maximum(cs, ss)
    return jnp.clip(ov, 0, None).astype(jnp.float32) / CMP_BLOCK


def nsa_attend(q, pos_q, gates, kc, vc, pos_c, m_ov, ks_blk, vs_blk, win_rows, pos_w):
    Bsz = q.shape[0]
    f32 = jnp.float32
    qf = q.astype(f32) * NSA_HD ** -0.5
    ok_c = (pos_c[None, :] <= pos_q[:, None])[None, :, None, None, :]
    s_c = jnp.where(ok_c, jnp.einsum('btghd,bngd->btghn', qf, kc.astype(f32)), NEG)
    p_c = jax.nn.softmax(s_c, axis=-1) * ok_c
    o_c = jnp.einsum('btghn,bngd->btghd', p_c, vc.astype(f32))
    n_slc = m_ov.shape[1]
    imp = jnp.einsum('btghn,ns->btgs', p_c, m_ov)
    blk = jnp.arange(n_slc)[None, :]
    cur = (pos_q // SLC_BLOCK)[:, None]
    imp = jnp.where(((blk == cur) | (blk == 0))[None, :, None, :], BIG, imp)
    imp = jnp.where((blk > cur)[None, :, None, :], NEG, imp)
    top_v, top_i = lax.top_k(imp, min(SLC_TOP, n_slc))
    bi = jnp.arange(Bsz)[:, None, None, None]
    gi = jnp.arange(NSA_KV)[None, None, :, None]
    k_sel = ks_blk[bi, gi, top_i].astype(f32)
    v_sel = vs_blk[bi, gi, top_i].astype(f32)
    key_pos = top_i[..., None] * SLC_BLOCK + jnp.arange(SLC_BLOCK)
    ok_s = ((top_v > 0.5 * NEG)[..., None] & (key_pos <= pos_q[None, :, None, None, None]))[:, :, :, None]
    s_s = jnp.where(ok_s, jnp.einsum('btghd,btgnsd->btghns', qf, k_sel), NEG)
    sh = s_s.shape
    p_s = jax.nn.softmax(s_s.reshape(sh[:4] + (-1,)), axis=-1).reshape(sh)
    o_s = jnp.einsum('btghns,btgnsd->btghd', p_s, v_sel)
    dpos = pos_q[:, None] - pos_w[None, :]
    ok_w = ((dpos >= 0) & (dpos < WINDOW) & (pos_w[None, :] >= 0))[None, :, None, None, :]
    s_w = jnp.where(ok_w, jnp.einsum('btghd,bsgd->btghs', qf, win_rows[:, :, 0].astype(f32)), NEG)
    p_w = jax.nn.softmax(s_w, axis=-1)
    o_w = jnp.einsum('btghs,bsgd->btghd', p_w, win_rows[:, :, 1].astype(f32))
    return gates[..., 0:1] * o_c + gates[..., 1:2] * o_s + gates[..., 2:3] * o_w


def mixer_nsa(h, pos, past_rows, win_buf, w_in, q_norm, k_norm, cmp_pe, cmp_w1, cmp_w2, w_out):
    Bsz, T, _ = h.shape
    G, Hg, D = NSA_KV, NSA_HPG, NSA_HD
    proj = h @ w_in
    q, kv, gate_raw = jnp.split(proj, [NSA_HEADS * D, NSA_HEADS * D + 6 * KV_W], axis=-1)
    q = rope_partial(rmsnorm(q.reshape(Bsz, T, NSA_HEADS, D), q_norm), pos)
    kv = kv.reshape(Bsz, T, 6, G, D)
    k_slc = rope_partial(rmsnorm(kv[:, :, 2], k_norm[1]), pos)
    k_win = rope_partial(rmsnorm(kv[:, :, 4], k_norm[2]), pos)
    new_rows = jnp.stack([kv[:, :, 0], kv[:, :, 1], k_slc, kv[:, :, 3]], axis=2)
    new_win = jnp.stack([k_win, kv[:, :, 5]], axis=2)
    if past_rows is None:
        rows = new_rows
        win_rows = jnp.concatenate([jnp.zeros((Bsz, WINDOW, 2, G, D), new_win.dtype), new_win], axis=1)
        new_win_buf = new_win[:, T - min(WINDOW, T):]
    else:
        rows = jnp.concatenate([past_rows.astype(new_rows.dtype), new_rows], axis=1)
        wb = win_buf.shape[1]
        win_rows = jnp.concatenate([win_buf.astype(new_win.dtype), new_win], axis=1)
        win_pos = past_rows.shape[1] - wb + jnp.arange(wb + T)
        new_win_buf = win_rows[:, T:]
    L = rows.shape[1]
    kc = compress_blocks(rows[:, :, 0], cmp_pe[0], cmp_w1[0], cmp_w2[0])
    vc = compress_blocks(rows[:, :, 1], cmp_pe[1], cmp_w1[1], cmp_w2[1])
    n_cmp = kc.shape[1]
    pos_c = jnp.arange(n_cmp) * CMP_STRIDE + CMP_BLOCK - 1
    kc = rope_partial(rmsnorm(kc, k_norm[0]), pos_c)
    n_slc = -(-L // SLC_BLOCK)
    pad = n_slc * SLC_BLOCK - L

    def to_blocks(r):
        r = jnp.pad(r, ((0, 0), (0, pad), (0, 0), (0, 0)))
        return r.reshape(Bsz, n_slc, SLC_BLOCK, G, D).transpose(0, 3, 1, 2, 4)

    ks_blk = to_blocks(rows[:, :, 2])
    vs_blk = to_blocks(rows[:, :, 3])
    m_ov = block_overlap(n_cmp, n_slc)
    q = q.reshape(Bsz, T, G, Hg, D)
    gates = jax.nn.sigmoid(gate_raw.astype(jnp.float32)).reshape(Bsz, T, G, Hg, 3)
    if past_rows is None:
        qc = NSA_Q_CHUNK

        def chunk(c):
            s = c * qc
            return nsa_attend(lax.dynamic_slice_in_dim(q, s, qc, 1), s + jnp.arange(qc),
                              lax.dynamic_slice_in_dim(gates, s, qc, 1), kc, vc, pos_c, m_ov,
                              ks_blk, vs_blk, lax.dynamic_slice_in_dim(win_rows, s, qc + WINDOW, 1),
                              s - WINDOW + jnp.arange(qc + WINDOW))

        o = lax.map(chunk, jnp.arange(T // qc))
        o = jnp.moveaxis(o, 0, 1).reshape(Bsz, T, NSA_HEADS * D)
    else:
        o = nsa_attend(q, pos, gates, kc, vc, pos_c, m_ov, ks_blk, vs_blk, win_rows, win_pos)
        o = o.reshape(Bsz, T, NSA_HEADS * D)
    y = o.astype(h.dtype) @ w_out
    return y, new_rows, new_win_buf


def memory_kv(mem, g_src, w_kv, k_norm):
    Bsz, M, _ = mem.shape
    kv = (rmsnorm(mem, g_src) @ w_kv).reshape(Bsz, M, 2, MEM_HEADS, MEM_HD)
    return jnp.stack([rmsnorm(kv[:, :, 0], k_norm), kv[:, :, 1]], axis=2)


def mem_attend(h, kv, w_q, q_norm, w_o):
    Bsz, T, _ = h.shape
    f32 = jnp.float32
    q = rmsnorm((h @ w_q).reshape(Bsz, T, MEM_HEADS, MEM_HD), q_norm)
    s = jnp.einsum('bthd,bmhd->bhtm', q.astype(f32), kv[:, :, 0].astype(f32)) * MEM_HD ** -0.5
    p = jax.nn.softmax(s, axis=-1)
    o = jnp.einsum('bhtm,bmhd->bthd', p, kv[:, :, 1].astype(f32))
    return o.reshape(Bsz, T, MEM_HEADS * MEM_HD).astype(h.dtype) @ w_o


def run_group(x, pos, mem_kvs, sconv_st, ssd_conv_st, ssd_st, nsa_past, nsa_win, p):
    new_sconv, new_ssdc, new_ssd, new_rows, new_win = [], [], [], [], []
    for layer in range(DEPTH):
        j = layer // 2
        h = rmsnorm(x, p['norm_mix'][layer])
        if layer % 2 == 0:
            y, s_a, s_b, s_c = mixer_ab(h, sconv_st[j], ssd_conv_st[j], ssd_st[j], p['ab_w_in'][j],
                                        p['ab_sconv_w'][j], p['ab_ssd_conv_w'][j], p['ab_ssd_conv_b'][j],
                                        p['ab_dt_bias'][j], p['ab_a_log'][j], p['ab_d'][j],
                                        p['ab_ssd_norm'][j], p['ab_w_out'][j])
            new_sconv.append(s_a)
            new_ssdc.append(s_b)
            new_ssd.append(s_c)
        else:
            past_j = None if nsa_past is None else nsa_past[:, :, j]
            win_j = None if nsa_win is None else nsa_win[j]
            y, rows, win = mixer_nsa(h, pos, past_j, win_j, p['nsa_w_in'][j], p['nsa_q_norm'][j],
                                     p['nsa_k_norm'][j], p['nsa_cmp_pe'][j], p['nsa_cmp_w1'][j],
                                     p['nsa_cmp_w2'][j], p['nsa_w_out'][j])
            new_rows.append(rows)
            new_win.append(win)
        x = x + y
        x = x + mem_attend(rmsnorm(x, p['norm_mem'][layer]), mem_kvs[layer], p['mem_w_q'][layer],
                           p['mem_q_norm'][layer], p['mem_w_o'][layer])
        x = x + swiglu(rmsnorm(x, p['norm_ffn'][layer]), p['ffn_w_gate'][layer], p['ffn_w_up'][layer],
                       p['ffn_w_down'][layer])
    return (x, jnp.stack(new_sconv), jnp.stack(new_ssdc), jnp.stack(new_ssd),
            jnp.stack(new_rows, axis=2), jnp.stack(new_win))


def setup_inputs(seed: int = 0) -> dict:
    key = jax.random.key(seed)
    ks = iter(jax.random.split(key, 64))
    f32 = jnp.float32

    def nrm(shape, scale):
        return jax.random.normal(next(ks), shape, f32) * scale

    def gain(shape):
        return 1.0 + nrm(shape, 0.05)

    D = D_MODEL
    n_pages = PAST_LEN // PAGE_SIZE
    n_pool = (DEC_BATCH * n_pages * 5) // 4
    w_buf = min(WINDOW, PAST_LEN)
    page_table = jax.random.permutation(next(ks), n_pool)[:DEC_BATCH * n_pages]
    page_table = page_table.reshape(DEC_BATCH, n_pages).astype(jnp.int32)
    dt0 = jnp.exp(jax.random.uniform(next(ks), (N_AB, SSD_HEADS), f32, math.log(1e-3), math.log(1e-1)))
    dt_bias = dt0 + jnp.log(-jnp.expm1(-dt0))
    a_log = jnp.log(jax.random.uniform(next(ks), (N_AB, SSD_HEADS), f32, 1.0, 16.0))
    return {
        'x_prompt': nrm((BATCH, SEQ, D), 1.0),
        'x_sample': nrm((DEC_BATCH, DEC_SEQ, D), 1.0),
        'mem_prompt': nrm((BATCH, N_MEM, D), 1.0),
        'state_sconv': nrm((N_AB, DEC_BATCH, SCONV_W - 1, SCONV_DIM), 1.0),
        'state_ssd_conv': nrm((N_AB, DEC_BATCH, SSD_CONV_W - 1, SSD_CONV_DIM), 1.0),
        'state_ssd': nrm((N_AB, DEC_BATCH, SSD_HEADS, SSD_HD, SSD_N), 0.1),
        'cache_nsa_kv': nrm((n_pool, PAGE_SIZE, N_NSA, 4, NSA_KV, NSA_HD), 1.0),
        'cache_nsa_win': nrm((N_NSA, DEC_BATCH, w_buf, 2, NSA_KV, NSA_HD), 1.0),
        'cache_mem_kv': nrm((DEPTH, DEC_BATCH, N_MEM, 2, MEM_HEADS, MEM_HD), 1.0),
        'page_table': page_table,
        'norm_mix': gain((DEPTH, D)),
        'norm_mem': gain((DEPTH, D)),
        'norm_memsrc': gain((DEPTH, D)),
        'norm_ffn': gain((DEPTH, D)),
        'ab_w_in': nrm((N_AB, D, IN0_DIM), D ** -0.5),
        'ab_sconv_w': nrm((N_AB, SCONV_W, SCONV_DIM), SCONV_W ** -0.5),
        'ab_ssd_conv_w': nrm((N_AB, SSD_CONV_W, SSD_CONV_DIM), SSD_CONV_W ** -0.5),
        'ab_ssd_conv_b': nrm((N_AB, SSD_CONV_DIM), 0.02),
        'ab_dt_bias': dt_bias,
        'ab_a_log': a_log,
        'ab_d': gain((N_AB, SSD_HEADS)),
        'ab_ssd_norm': gain((N_AB, SSD_DIM)),
        'ab_w_out': nrm((N_AB, SCONV_DIM + SSD_DIM, D), (SCONV_DIM + SSD_DIM) ** -0.5),
        'nsa_w_in': nrm((N_NSA, D, IN1_DIM), D ** -0.5),
        'nsa_q_norm': gain((N_NSA, NSA_HD)),
        'nsa_k_norm': gain((N_NSA, 3, NSA_HD)),
        'nsa_cmp_pe': nrm((N_NSA, 2, CMP_BLOCK, NSA_HD), 0.02),
        'nsa_cmp_w1': nrm((N_NSA, 2, CMP_BLOCK, NSA_HD, NSA_HD), (CMP_BLOCK * NSA_HD) ** -0.5),
        'nsa_cmp_w2': nrm((N_NSA, 2, NSA_HD, NSA_HD), NSA_HD ** -0.5),
        'nsa_w_out': nrm((N_NSA, NSA_HEADS * NSA_HD, D), (NSA_HEADS * NSA_HD) ** -0.5),
        'mem_w_q': nrm((DEPTH, D, MEM_HEADS * MEM_HD), D ** -0.5),
        'mem_q_norm': gain((DEPTH, MEM_HD)),
        'mem_w_kv': nrm((DEPTH, D, 2 * MEM_HEADS * MEM_HD), D ** -0.5),
        'mem_k_norm': gain((DEPTH, MEM_HD)),
        'mem_w_o': nrm((DEPTH, MEM_HEADS * MEM_HD, D), (MEM_HEADS * MEM_HD) ** -0.5),
        'ffn_w_gate': nrm((DEPTH, D, D_FF), D ** -0.5),
        'ffn_w_up': nrm((DEPTH, D, D_FF), D ** -0.5),
        'ffn_w_down': nrm((DEPTH, D_FF, D), D_FF ** -0.5),
    }


def reference(x_prompt, x_sample, mem_prompt, state_sconv, state_ssd_conv, state_ssd, cache_nsa_kv,
              cache_nsa_win, cache_mem_kv, page_table, norm_mix, norm_mem, norm_memsrc, norm_ffn,
              ab_w_in, ab_sconv_w, ab_ssd_conv_w, ab_ssd_conv_b, ab_dt_bias, ab_a_log, ab_d, ab_ssd_norm,
              ab_w_out, nsa_w_in, nsa_q_norm, nsa_k_norm, nsa_cmp_pe, nsa_cmp_w1, nsa_cmp_w2, nsa_w_out,
              mem_w_q, mem_q_norm, mem_w_kv, mem_k_norm, mem_w_o, ffn_w_gate, ffn_w_up, ffn_w_down):
    p = dict(norm_mix=norm_mix, norm_mem=norm_mem, norm_ffn=norm_ffn, ab_w_in=ab_w_in,
             ab_sconv_w=ab_sconv_w, ab_ssd_conv_w=ab_ssd_conv_w, ab_ssd_conv_b=ab_ssd_conv_b,
             ab_dt_bias=ab_dt_bias, ab_a_log=ab_a_log, ab_d=ab_d, ab_ssd_norm=ab_ssd_norm,
             ab_w_out=ab_w_out, nsa_w_in=nsa_w_in, nsa_q_norm=nsa_q_norm, nsa_k_norm=nsa_k_norm,
             nsa_cmp_pe=nsa_cmp_pe, nsa_cmp_w1=nsa_cmp_w1, nsa_cmp_w2=nsa_cmp_w2, nsa_w_out=nsa_w_out,
             mem_w_q=mem_w_q, mem_q_norm=mem_q_norm, mem_w_o=mem_w_o,
             ffn_w_gate=ffn_w_gate, ffn_w_up=ffn_w_up, ffn_w_down=ffn_w_down)
    Bp, T = x_prompt.shape[:2]
    Bs, Ts = x_sample.shape[:2]
    dt = x_prompt.dtype
    p_mem_kv = jnp.stack([memory_kv(mem_prompt, norm_memsrc[l], mem_w_kv[l], mem_k_norm[l])
                          for l in range(DEPTH)], axis=0)
    y_prompt, p_sconv, p_ssdc, p_ssd, p_rows, p_win = run_group(
        x_prompt, jnp.arange(T), p_mem_kv,
        jnp.zeros((N_AB, Bp, SCONV_W - 1, SCONV_DIM), dt),
        jnp.zeros((N_AB, Bp, SSD_CONV_W - 1, SSD_CONV_DIM), dt),
        jnp.zeros((N_AB, Bp, SSD_HEADS, SSD_HD, SSD_N), dt),
        None, None, p)
    n_pages = page_table.shape[1]
    past = cache_nsa_kv[page_table].reshape(Bs, n_pages * cache_nsa_kv.shape[1], N_NSA, 4, NSA_KV, NSA_HD)
    y_sample, s_sconv, s_ssdc, s_ssd, s_rows, s_win = run_group(
        x_sample, past.shape[1] + jnp.arange(Ts), cache_mem_kv, state_sconv, state_ssd_conv, state_ssd,
        past, cache_nsa_win, p)
    return (y_prompt, y_sample, p_sconv, p_ssdc, p_ssd, p_rows, p_win, p_mem_kv,
            s_sconv, s_ssdc, s_ssd, s_rows, s_win)
```

```python
import math
from contextlib import ExitStack

import numpy as np
import concourse.bass as bass
import concourse.mybir as mybir
from concourse.bass_utils import run_bass_kernel_spmd

F32 = mybir.dt.float32
BF16 = mybir.dt.bfloat16
I32 = mybir.dt.int32
ALU = mybir.AluOpType
AF = mybir.ActivationFunctionType
AX = mybir.AxisListType

NCORES = 8
D = 1024
SEQ = 2048
NPB = 2
NSB = 4
TS = 8
PAST = 16384
NPAGES = 128
EPS = 1e-6
IN0 = 6160
IN1 = 2608
DFF = 2816
NEGM = -30000.0
THETA = 500000.0


class Buf:
    __slots__ = ("name", "w", "r", "excl")

    def __init__(self, name, excl=False):
        self.name = name
        self.w = None
        self.r = {}
        self.excl = excl


class TT:
    def __init__(self, t, name, b=None):
        self.t = t
        self.b = b if b is not None else Buf(name)

    def __getitem__(self, k):
        return self.t[k]


class Sched:
    def __init__(self, nc, es, same_engine_sync=True, n_dma_sems=10):
        self.nc = nc
        self.engs = {"pe": nc.tensor, "act": nc.scalar, "dve": nc.vector, "pool": nc.gpsimd, "sp": nc.sync}
        self.sem = {}
        self.cnt = {}
        for k in self.engs:
            self.sem[k] = es.enter_context(nc.semaphore("s_" + k))
            self.cnt[k] = 0
        self.dma_sems = {}
        self.dma_rr = {}
        for q in ("sp", "pool", "act"):
            self.dma_sems[q] = []
            for i in range(n_dma_sems):
                key = "d_%s%d" % (q, i)
                self.sem[key] = es.enter_context(nc.semaphore(key))
                self.cnt[key] = 0
                self.dma_sems[q].append(key)
            self.dma_rr[q] = 0
        self.waited = {k: {} for k in self.engs}
        self.same = same_engine_sync
        self.ninstr = 0

    def _wait(self, e, deps):
        eng = self.engs[e]
        for k, v in deps.items():
            if v <= 0:
                continue
            if k == e and (not self.same or e == "pe"):
                continue
            if self.waited[e].get(k, 0) >= v:
                continue
            eng.wait_ge(self.sem[k], v)
            self.waited[e][k] = v

    @staticmethod
    def _deps(reads, writes):
        deps = {}
        for b in reads:
            if b.w is not None and deps.get(b.w[0], 0) < b.w[1]:
                deps[b.w[0]] = b.w[1]
        for b in writes:
            if b.w is not None and deps.get(b.w[0], 0) < b.w[1]:
                deps[b.w[0]] = b.w[1]
            for k, v in b.r.items():
                if deps.get(k, 0) < v:
                    deps[k] = v
        return deps

    def op(self, e, fn, reads=(), writes=()):
        if any(b.excl for b in reads):
            writes = list(writes) + [b for b in reads if b.excl and b not in writes]
            reads = [b for b in reads if not b.excl]
        deps = self._deps(reads, writes)
        self._wait(e, deps)
        ins = fn(self.engs[e])
        self.cnt[e] += 1
        ins.then_inc(self.sem[e], 1)
        v = self.cnt[e]
        for b in reads:
            if b.r.get(e, 0) < v:
                b.r[e] = v
        for b in writes:
            b.w = (e, v)
            b.r = {}
        self.ninstr += 1
        return ins

    def dma(self, q, out, in_, reads=(), writes=(), fn=None, **kw):
        deps = self._deps(reads, writes)
        key = self.dma_sems[q][self.dma_rr[q] % len(self.dma_sems[q])]
        self.dma_rr[q] += 1
        if self.cnt[key] > 0:
            deps[key] = max(deps.get(key, 0), self.cnt[key])
        self._wait(q, deps)
        if fn is not None:
            ins = fn(self.engs[q])
        else:
            ins = self.engs[q].dma_start(out=out, in_=in_, **kw)
        self.cnt[key] += 16
        ins.then_inc(self.sem[key], 16)
        v = self.cnt[key]
        for b in reads:
            b.r[key] = v
        for b in writes:
            b.w = (key, v)
            b.r = {}
        self.ninstr += 1
        return ins

    def barrier(self):
        snap = {k: v for k, v in self.cnt.items() if v > 0}
        for e in self.engs:
            self._wait(e, dict(snap))

    def finish(self, bufs, e="sp"):
        self._wait(e, self._deps(bufs, ()))


class Prog:
    def __init__(self, do_sample=True, n_ptiles=16, debug=False, pool_rows=5120 * 128, stage=9, do_prompt=True, n_sample=NSB):
        self.pool_rows = pool_rows
        self.stage = stage
        self.do_prompt = do_prompt
        self.n_sample = n_sample
        self.same_sync = True
        self.do_sample = do_sample
        self.n_ptiles = n_ptiles
        self.debug = debug
        self.nc = bass.Bass("TRN2", target_bir_lowering=False)
        self.es = ExitStack()
        self.out_bufs = []

    def sb(self, name, shape, dt=F32):
        return TT(self.es.enter_context(self.nc.sbuf_tensor(name, list(shape), dt)), name)

    def din(self, name, shape, dt=F32):
        return self.nc.dram_tensor(name, list(shape), dt, kind="ExternalInput").ap()

    def dout(self, name, shape, dt=F32):
        ap = self.nc.dram_tensor(name, list(shape), dt, kind="ExternalOutput").ap()
        b = Buf(name)
        self.out_bufs.append(b)
        return ap, b

    def dscratch(self, name, shape, dt):
        return TT(self.nc.dram_tensor(name, list(shape), dt, kind="Internal").ap(), name)

    def fillreg(self, v):
        if not hasattr(self, "_fillregs"):
            self._fillregs = {}
        if v not in self._fillregs:
            self._fillregs[v] = self.nc.gpsimd.to_reg(float(v))
        return self._fillregs[v]

    def psum(self):
        i = self.ps_rr % len(self.psF)
        self.ps_rr += 1
        return self.psF[i]

    def psumb(self):
        i = self.psb_rr % len(self.psB)
        self.psb_rr += 1
        return self.psB[i]

    def mm(self, out, ob, lhsT, lb, rhs, rb, start=True, stop=True):
        self.S.op("pe", lambda e: e.matmul(out, lhsT, rhs, start=start, stop=stop), reads=[lb, rb], writes=[ob])

    def tr(self, out, ob, in_, ib, ident=None):
        idt = ident if ident is not None else self.ident
        n = in_.shape[0]
        self.S.op("pe", lambda e: e.transpose(out, in_, idt[0:n, 0:n]), reads=[ib, idt.b], writes=[ob])

    def act(self, out, ob, in_, ib, func, bias=None, scale=None, accum=None, rd=(), wr=()):
        kw = {}
        if bias is not None:
            kw["bias"] = bias
        if scale is not None:
            kw["scale"] = scale
        if accum is not None:
            kw["accum_out"] = accum
        self.S.op("act", lambda e: e.activation(out=out, in_=in_, func=func, **kw), reads=[ib] + list(rd),
                  writes=[ob] + list(wr))

    def tt(self, out, ob, in0, b0, in1, b1, op, eng="dve"):
        self.S.op(eng, lambda e: e.tensor_tensor(out=out, in0=in0, in1=in1, op=op), reads=[b0, b1], writes=[ob])

    def ts(self, out, ob, in0, b0, s1, op0, s2=None, op1=None, rd=(), eng="dve", accum=None):
        kw = {}
        if op1 is not None:
            kw["op1"] = op1
        if accum is not None:
            kw["accum_out"] = accum
        self.S.op(eng, lambda e: e.tensor_scalar(out=out, in0=in0, scalar1=s1, scalar2=s2, op0=op0, **kw),
                  reads=[b0] + list(rd), writes=[ob])

    def stt(self, out, ob, in0, b0, sc, in1, b1, op0, op1, rd=()):
        self.S.op("dve", lambda e: e.scalar_tensor_tensor(out=out, in0=in0, scalar=sc, in1=in1, op0=op0, op1=op1),
                  reads=[b0, b1] + list(rd), writes=[ob])

    def cp(self, out, ob, in_, ib, eng="dve"):
        if eng == "act":
            self.S.op("act", lambda e: e.copy(out, in_), reads=[ib], writes=[ob])
        else:
            self.S.op(eng, lambda e: e.tensor_copy(out, in_), reads=[ib], writes=[ob])

    def memset(self, ap, b, val, eng="pool"):
        self.S.op(eng, lambda e: e.memset(ap, val), writes=[b])

    def red(self, out, ob, in_, ib, op=ALU.add, eng="dve"):
        self.S.op(eng, lambda e: e.tensor_reduce(out=out, in_=in_, axis=AX.X, op=op), reads=[ib], writes=[ob])

    def recip(self, out, ob, in_, ib):
        self.S.op("dve", lambda e: e.reciprocal(out, in_), reads=[ib], writes=[ob])

    def rstd(self, out, ob, ss, sb_, n, inv_n):
        self.act(out, ob, ss, sb_, AF.Ln, scale=inv_n, bias=self.epsc[0:out.shape[0], 0:1], rd=[self.epsc.b])
        self.act(out, ob, out, ob, AF.Exp, scale=-0.5)

    def wblock(self, Wd, K, c0, ncols):
        i = self.w_rr % len(self.wbufs)
        self.w_rr += 1
        wb = self.wbufs[i]
        kc = K // 128
        src = Wd.t[:, c0:c0 + ncols].rearrange("(c p) n -> p c n", p=128)
        view = wb[:, 0:kc * ncols].rearrange("p (c n) -> p c n", c=kc)
        self.S.dma("sp", view, src, reads=[Wd.b], writes=[wb.b])
        return view, wb.b

    def linear_tm(self, xT, n, Wd, K, c0, ncols_total, consume):
        kc = K // 128
        off = 0
        while off < ncols_total:
            maxc = min(512, (5632 // kc) // 128 * 128)
            nb = min(maxc, ncols_total - off)
            wv, wbb = self.wblock(Wd, K, c0 + off, nb)
            ps = self.psum()
            for k in range(kc):
                self.mm(ps[0:n, 0:nb], ps.b, xT[:, k, 0:n], xT.b, wv[:, k, 0:nb], wbb, start=(k == 0), stop=(k == kc - 1))
            consume(off, nb, ps[0:n, 0:nb], ps.b)
            off += nb

    def linear_fm(self, xT, n, Wd, K, c0, nchunks, consume):
        kc = K // 128
        ch = 0
        while ch < nchunks:
            nch = min(4, nchunks - ch)
            wv, wbb = self.wblock(Wd, K, c0 + ch * 128, nch * 128)
            ps = self.psum()
            for j in range(nch):
                for k in range(kc):
                    self.mm(ps[:, j * n:(j + 1) * n], ps.b, wv[:, k, j * 128:(j + 1) * 128], wbb, xT[:, k, 0:n], xT.b,
                            start=(k == 0), stop=(k == kc - 1))
            consume(ch, nch, ps[:, 0:nch * n].rearrange("p (c t) -> p c t", c=nch), ps.b)
            ch += nch

    def rmsnorm_T(self, x, n, gcol, hT):
        junk, ss, hn = self.junk, self.ss, self.hn
        self.act(junk[0:n, :], junk.b, x[0:n, :], x.b, AF.Square, accum=ss[0:n, 0:1], wr=[ss.b])
        self.rstd(ss[0:n, 1:2], ss.b, ss[0:n, 0:1], ss.b, n, 1.0 / D)
        self.ts(hn[0:n, :], hn.b, x[0:n, :], x.b, ss[0:n, 1:2], ALU.mult, rd=[ss.b])
        self.transpose_fm(hn, n, 8, hT, gcol)

    def transpose_fm(self, src, n, nchunks, dst, gcol=None, width=128, col0=0, dview=None):
        dv = dview if dview is not None else dst[0:width, 0:nchunks, 0:n]
        per = min(8, 1024 // n)
        c = 0
        while c < nchunks:
            m = min(per, nchunks - c)
            pb = self.psumb()
            for j in range(m):
                a = col0 + (c + j) * width
                self.tr(pb[0:width, j * n:(j + 1) * n], pb.b, src[0:n, a:a + width], src.b)
            pv = pb[0:width, 0:m * n].rearrange("p (c t) -> p c t", c=m)
            if gcol is None:
                self.cp(dv[:, c:c + m, :], dst.b, pv, pb.b, eng="act")
            else:
                g1 = gcol[0:width, c:c + m].unsqueeze(2).to_broadcast([width, m, n])
                self.tt(dv[:, c:c + m, :], dst.b, pv, pb.b, g1, gcol.b, ALU.mult)
            c += m

    def build(self):
        nc = self.nc
        es = self.es
        with es:
            self.S = Sched(nc, es, same_engine_sync=self.same_sync)
            self.declare_io()
            self.alloc()
            st = self.stage
            self.setup_consts()
            if st >= 1:
                self.convert_weights()
            if st >= 2 and self.do_prompt:
                self.cur_prompt = True
                for s in range(NPB if st >= 9 else 1):
                    self.prompt_seq(s)
            if self.do_sample:
                for b in range(self.n_sample):
                    self.sample_seq(b)
            self.S.finish([b for b in self.out_bufs if b.w is not None])
        return nc

    def declare_io(self):
        d = self.din
        self.x_prompt = d("x_prompt", [NPB, SEQ, D])
        self.x_sample = d("x_sample", [NSB, TS, D])
        self.mem_prompt = d("mem_prompt", [NPB, 256, D])
        self.state_sconv = d("state_sconv", [NSB, 2, 1024])
        self.state_ssd_conv = d("state_ssd_conv", [NSB, 3, 2048])
        self.state_ssd = d("state_ssd", [NSB, 1024, 128])
        self.cache_nsa_kv = d("cache_nsa_kv", [self.pool_rows, 1024])
        self.cache_nsa_win = d("cache_nsa_win", [NSB, 512, 512])
        self.cache_mem_kv = d("cache_mem_kv", [2, NSB, 256, 1024])
        self.page_table = d("page_table", [NSB, 128], I32)
        self.norm_mix = d("norm_mix", [2, D])
        self.norm_mem = d("norm_mem", [2, D])
        self.norm_memsrc = d("norm_memsrc", [2, D])
        self.norm_ffn = d("norm_ffn", [2, D])
        self.ab_w_in = d("ab_w_in", [D, IN0])
        self.ab_sconv_w = d("ab_sconv_w", [3, 1024])
        self.ab_ssd_conv_w = d("ab_ssd_conv_w", [4, 2048])
        self.ab_ssd_conv_b = d("ab_ssd_conv_b", [1, 2048])
        self.ab_dt_bias = d("ab_dt_bias", [1, 16])
        self.ab_a_log = d("ab_a_log", [1, 16])
        self.ab_d = d("ab_d", [1, 16])
        self.ab_ssd_norm = d("ab_ssd_norm", [1, 1024])
        self.ab_w_out = d("ab_w_out", [2048, D])
        self.nsa_w_in = d("nsa_w_in", [D, IN1])
        self.nsa_q_norm = d("nsa_q_norm", [1, 64])
        self.nsa_k_norm = d("nsa_k_norm", [3, 64])
        self.nsa_cmp_pe = d("nsa_cmp_pe", [2, 32, 64])
        self.nsa_cmp_w1 = d("nsa_cmp_w1", [2, 32, 64, 64])
        self.nsa_cmp_w2 = d("nsa_cmp_w2", [2, 64, 64])
        self.nsa_w_out = d("nsa_w_out", [D, D])
        self.mem_w_q = d("mem_w_q", [2, D, 512])
        self.mem_q_norm = d("mem_q_norm", [2, 128])
        self.mem_w_kv = d("mem_w_kv", [2, D, 1024])
        self.mem_k_norm = d("mem_k_norm", [2, 128])
        self.mem_w_o = d("mem_w_o", [2, 512, D])
        self.ffn_w_gate = d("ffn_w_gate", [2, D, DFF])
        self.ffn_w_up = d("ffn_w_up", [2, D, DFF])
        self.ffn_w_down = d("ffn_w_down", [2, DFF, D])
        o = self.dout
        self.y_prompt, self.b_y_prompt = o("y_prompt", [NPB, SEQ, D])
        self.y_sample, self.b_y_sample = o("y_sample", [NSB, TS, D])
        self.p_sconv, self.b_p_sconv = o("p_sconv", [NPB, 2, 1024])
        self.p_ssd_conv, self.b_p_ssd_conv = o("p_ssd_conv", [NPB, 3, 2048])
        self.p_ssd, self.b_p_ssd = o("p_ssd", [NPB, 1024, 128])
        self.p_nsa_rows, self.b_p_nsa_rows = o("p_nsa_rows", [NPB, SEQ, 1024])
        self.p_nsa_win, self.b_p_nsa_win = o("p_nsa_win", [NPB, 512, 512])
        self.p_mem_kv, self.b_p_mem_kv = o("p_mem_kv", [2, NPB, 256, 1024])
        self.s_sconv, self.b_s_sconv = o("s_sconv", [NSB, 2, 1024])
        self.s_ssd_conv, self.b_s_ssd_conv = o("s_ssd_conv", [NSB, 3, 2048])
        self.s_ssd, self.b_s_ssd = o("s_ssd", [NSB, 1024, 128])
        self.s_nsa_rows, self.b_s_nsa_rows = o("s_nsa_rows", [NSB, TS, 1024])
        self.s_nsa_win, self.b_s_nsa_win = o("s_nsa_win", [NSB, 512, 512])
        ds = self.dscratch
        self.W_in0 = ds("W_in0", [D, IN0], BF16)
        self.W_out0 = ds("W_out0", [2048, D], BF16)
        self.W_in1 = ds("W_in1", [D, IN1], BF16)
        self.W_out1 = ds("W_out1", [D, D], BF16)
        self.W_q = [ds("W_q%d" % l, [D, 512], BF16) for l in range(2)]
        self.W_kv = [ds("W_kv%d" % l, [D, 1024], BF16) for l in range(2)]
        self.W_o = [ds("W_o%d" % l, [512, D], BF16) for l in range(2)]
        self.W_g = [ds("W_g%d" % l, [D, DFF], BF16) for l in range(2)]
        self.W_u = [ds("W_u%d" % l, [D, DFF], BF16) for l in range(2)]
        self.W_d = [ds("W_d%d" % l, [DFF, D], BF16) for l in range(2)]

    def convert_weights(self):
        S = self.S
        pairs = [(self.W_in0, self.ab_w_in), (self.W_out0, self.ab_w_out), (self.W_in1, self.nsa_w_in),
                 (self.W_out1, self.nsa_w_out)]
        for l in range(2):
            pairs += [(self.W_q[l], self.mem_w_q[l]), (self.W_kv[l], self.mem_w_kv[l]), (self.W_o[l], self.mem_w_o[l]),
                      (self.W_g[l], self.ffn_w_gate[l]), (self.W_u[l], self.ffn_w_up[l]), (self.W_d[l], self.ffn_w_down[l])]
        for dst, src in pairs:
            K, N = src.shape[0], src.shape[1]
            for r in range(0, K, 256):
                r1 = min(K, r + 256)
                for c in range(0, N, 2048):
                    c1 = min(N, c + 2048)
                    S.dma("pool", dst.t[r:r1, c:c1], src[r:r1, c:c1], writes=[dst.b])

    def alloc(self):
        nc, es, sb = self.nc, self.es, self.sb
        self.psF = [TT(es.enter_context(nc.psum_tensor("psF%d" % i, [128, 512], F32)), "psF%d" % i, Buf("psF%d" % i, True))
                    for i in range(6)]
        self.psB = [TT(es.enter_context(nc.psum_tensor("psB%d" % i, [128, 1024], BF16)), "psB%d" % i, Buf("psB%d" % i, True))
                    for i in range(2)]
        self.ps_rr = 0
        self.psb_rr = 0
        self.wbufs = [sb("wbuf%d" % i, [128, 5632], BF16) for i in range(3)]
        self.w_rr = 0
        self.ident = sb("ident", [128, 128], BF16)
        self.identf = sb("identf", [128, 128], F32)
        self.U = sb("U", [128, 128], F32)
        self.onesf = sb("onesf", [128, 128], F32)
        self.causal_neg = sb("causal_neg", [128, 128], BF16)
        self.win_neg = sb("win_neg", [128, 128], BF16)
        self.maskc = sb("maskc", [128, 128], BF16)
        self.Eall = sb("Eall", [128, 16, 128], BF16)
        self.mov = sb("mov", [128, 33], F32)
        self.itmp = sb("itmp", [128, 256], I32)
        self.piota2 = sb("piota2", [128, 1], F32)
        self.ones_bf = sb("ones_bf", [128, 128], BF16)
        self.mov33 = sb("mov33", [128, 33], BF16)
        self.epsc = sb("epsc", [128, 1], F32)
        self.g_mix = [sb("g_mix%d" % l, [128, 8]) for l in range(2)]
        self.g_mem = [sb("g_mem%d" % l, [128, 8]) for l in range(2)]
        self.g_src = [sb("g_src%d" % l, [128, 8]) for l in range(2)]
        self.g_ffn = [sb("g_ffn%d" % l, [128, 8]) for l in range(2)]
        self.g_ssdn = sb("g_ssdn", [128, 8])
        self.sconv_w = sb("sconv_w", [128, 8, 3])
        self.ssdc_w = sb("ssdc_w", [128, 16, 4])
        self.ssdc_b = sb("ssdc_b", [128, 16])
        self.dt_bias = sb("dt_bias", [128, 16])
        self.Aneg = sb("Aneg", [128, 16])
        self.Dsk = sb("Dsk", [128, 16])
        self.memq_g = [sb("memq_g%d" % l, [128, 1]) for l in range(2)]
        self.memk_g = [sb("memk_g%d" % l, [128, 128]) for l in range(2)]
        self.memk_gc = [sb("memk_gc%d" % l, [128, 1]) for l in range(2)]
        self.nsaq_g = sb("nsaq_g", [128, 64])
        self.nsak_g = sb("nsak_g", [128, 3, 64])
        self.w1 = sb("w1", [128, 2, 16, 128], BF16)
        self.w2 = sb("w2", [128, 64], BF16)
        self.peT = sb("peT", [128, 2, 16], BF16)
        self.cAB = sb("cAB", [128, 2])
        self.cosT = sb("cosT", [128, 16, 8])
        self.sinT = sb("sinT", [128, 16, 8])
        self.cosC = sb("cosC", [128, 8, 8])
        self.sinC = sb("sinC", [128, 8, 8])
        self.cosS = sb("cosS", [8, 1, 8])
        self.sinS = sb("sinS", [8, 1, 8])
        self.selFH = sb("selFH", [128, 2, 32])
        self.xres2 = sb("xres2", [128, 1024])
        self.xres = sb("xres", [128, 1024])
        self.junk = sb("junk", [128, 1024])
        self.ss = sb("ss", [128, 2])
        self.hn = sb("hn", [128, 1024], BF16)
        self.hT = sb("hT", [128, 8, 128], BF16)
        self.big = sb("big", [128, 2816], BF16)
        self.bigT = sb("bigT", [128, 22, 128], BF16)
        self.f1 = sb("f1", [128, 1024])
        self.f2 = sb("f2", [128, 1024])
        self.f3 = sb("f3", [128, 1024])
        self.sm = sb("sm", [128, 256])
        self.ubuf = sb("ubuf", [128, 8, 130])
        self.xbuf = sb("xbuf", [128, 16, 131])
        self.xcs = sb("xcs", [128, 16, 128], BF16)
        self.rmask = sb("rmask", [128, 4, 128])
        self.seg = sb("seg", [128, 4, 128])
        self.cbm = sb("cbm", [128, 4, 128])
        self.xdt = sb("xdt", [128, 1024], BF16)
        self.xdte = sb("xdte", [128, 1024], BF16)
        self.hstate = sb("hstate", [128, 1024])
        self.hstate_bf = sb("hstate_bf", [128, 1024], BF16)
        self.KT = [sb("KT%d" % l, [128, 4, 256], BF16) for l in range(2)]
        self.V1 = [sb("V1%d" % l, [128, 2, 4, 129], BF16) for l in range(2)]
        self.qT = sb("qT", [128, 4, 128], BF16)
        self.qT2 = sb("qT2", [128, 2, 16, 128], BF16)
        self.pT = sb("pT", [128, 16, 512], BF16)
        self.kcache = sb("kcache", [128, 4, 2048], BF16)
        self.VS1 = sb("VS1", [128, 16, 4, 65], BF16)
        self.VW1 = sb("VW1", [128, 5, 4, 65], BF16)
        self.Aall = sb("Aall", [128, 2, 4, 129])
        self.pre = sb("pre", [128, 4, 128], BF16)
        self.kcT = sb("kcT", [128, 4, 128], BF16)
        self.VC1 = sb("VC1", [128, 4, 97], BF16)
        self.cT = sb("cT", [128, 4, 128], BF16)
        self.selT = sb("selT", [128, 4, 128], BF16)
        self.rows = sb("rows", [128, 1024])
        self.winr = sb("winr", [128, 512])
        self.gates = sb("gates", [128, 48])
        self.imp = sb("imp", [128, 4, 32])
        self.xr = [self.xres, self.xres2]
        self.s_h = TT(self.f1[:, 0:1024].rearrange("p (c t) -> p c t", c=8), "s_h", self.f1.b)
        self.cacc = TT(self.f3[:, 0:1024].rearrange("p (c t) -> p c t", c=8), "cacc", self.f3.b)
        self.x_tm = TT(self.big[:, 1024:2048], "x_tm", self.big.b)
        self.B_tm = TT(self.big[:, 2048:2560], "B_tm", self.big.b)
        self.Mh = TT(self.pT[:, 0:4, :].rearrange("p a (b t) -> p (a b) t", b=4), "Mh", self.pT.b)
        self.oacc = TT(self.f3[:, :], "oacc", self.f3.b)

    def setup_consts(self):
        S = self.S
        P = self
        ms = self.memset

        def asel(t, pattern, op, fill, base, cm, rows=128, view=None):
            v = view if view is not None else t[0:rows]
            S.op("pool", lambda e: e.affine_select(out=v, in_=v, pattern=pattern, compare_op=op, fill=self.fillreg(fill),
                                                  base=base, channel_multiplier=cm), reads=[t.b], writes=[t.b])
        ms(self.epsc[:], self.epsc.b, EPS)
        ms(self.ident[:], self.ident.b, 1.0)
        asel(self.ident, [[-1, 128]], ALU.is_equal, 0.0, 0, 1)
        ms(self.identf[:], self.identf.b, 1.0)
        asel(self.identf, [[-1, 128]], ALU.is_equal, 0.0, 0, 1)
        ms(self.onesf[:], self.onesf.b, 1.0)
        ms(self.U[:], self.U.b, 1.0)
        asel(self.U, [[1, 128]], ALU.is_ge, 0.0, 0, -1)
        ms(self.causal_neg[:], self.causal_neg.b, 0.0)
        asel(self.causal_neg, [[1, 128]], ALU.is_ge, NEGM, 0, -1)
        ms(self.win_neg[:], self.win_neg.b, 0.0)
        asel(self.win_neg, [[-1, 128]], ALU.is_gt, NEGM, 0, 1)
        ms(self.Eall[:], self.Eall.b, -NEGM)
        asel(self.Eall, [[128, 16], [1, 128]], ALU.is_ge, 0.0, 0, -64)
        asel(self.Eall, [[-128, 16], [-1, 128]], ALU.is_ge, 0.0, 63, 64)
        ms(self.qT2[:], self.qT2.b, 0.0)
        ms(self.kcT[:], self.kcT.b, 0.0)
        ms(self.selT[:], self.selT.b, 0.0)
        def col(t, src_row):
            S.dma("pool", t[:], src_row.rearrange("(c p) -> p c", p=128), writes=[t.b])
        nc = self.nc
        with nc.allow_non_contiguous_dma("tiny constant loads"):
            for l in range(2):
                col(self.g_mix[l], self.norm_mix[l])
                col(self.g_mem[l], self.norm_mem[l])
                col(self.g_src[l], self.norm_memsrc[l])
                col(self.g_ffn[l], self.norm_ffn[l])
            col(self.g_ssdn, self.ab_ssd_norm[0])
            for k in range(3):
                S.dma("pool", self.sconv_w[:, :, k], self.ab_sconv_w[k].rearrange("(c p) -> p c", p=128), writes=[self.sconv_w.b])
            for k in range(4):
                S.dma("pool", self.ssdc_w[:, :, k], self.ab_ssd_conv_w[k].rearrange("(c p) -> p c", p=128), writes=[self.ssdc_w.b])
            S.dma("pool", self.ssdc_b[:], self.ab_ssd_conv_b[0].rearrange("(c p) -> p c", p=128), writes=[self.ssdc_b.b])
            for l in range(2):
                S.dma("pool", self.memq_g[l][:], self.mem_q_norm[l].rearrange("(p o) -> p o", o=1), writes=[self.memq_g[l].b])
                S.dma("pool", self.memk_gc[l][:], self.mem_k_norm[l].rearrange("(p o) -> p o", o=1), writes=[self.memk_gc[l].b])
            ms(self.w1[:], self.w1.b, 0.0)
            for kv in range(2):
                r0 = 64 * kv
                for ab in range(2):
                    S.dma("pool", self.peT[r0:r0 + 64, ab, :], self.nsa_cmp_pe[kv, 16 * ab:16 * ab + 16].rearrange("j d -> d j"),
                          writes=[self.peT.b])
                    S.dma("pool", self.w1[r0:r0 + 64, ab, :, r0:r0 + 64],
                          self.nsa_cmp_w1[kv, 16 * ab:16 * ab + 16].rearrange("j d e -> d j e"), writes=[self.w1.b])
                S.dma("pool", self.w2[r0:r0 + 64, :], self.nsa_cmp_w2[kv], writes=[self.w2.b])
        S.dma("pool", self.dt_bias[:], self.ab_dt_bias.partition_broadcast(128), writes=[self.dt_bias.b])
        S.dma("pool", self.Aneg[:], self.ab_a_log.partition_broadcast(128), writes=[self.Aneg.b])
        S.dma("pool", self.Dsk[:], self.ab_d.partition_broadcast(128), writes=[self.Dsk.b])
        for l in range(2):
            S.dma("pool", self.memk_g[l][:], self.mem_k_norm[l:l + 1, :].partition_broadcast(128), writes=[self.memk_g[l].b])
            self.ts(self.memq_g[l][:], self.memq_g[l].b, self.memq_g[l][:], self.memq_g[l].b, 128 ** -0.5, ALU.mult)
        S.dma("pool", self.nsaq_g[:], self.nsa_q_norm.partition_broadcast(128), writes=[self.nsaq_g.b])
        self.ts(self.nsaq_g[:], self.nsaq_g.b, self.nsaq_g[:], self.nsaq_g.b, 64 ** -0.5, ALU.mult)
        for i in range(3):
            S.dma("pool", self.nsak_g[:, i, :], self.nsa_k_norm[i:i + 1, :].partition_broadcast(128), writes=[self.nsak_g.b])
        self.act(self.Aneg[:], self.Aneg.b, self.Aneg[:], self.Aneg.b, AF.Exp)
        self.ts(self.Aneg[:], self.Aneg.b, self.Aneg[:], self.Aneg.b, -1.0, ALU.mult)
        ps = self.psum()
        for ab in range(2):
            for j in range(16):
                self.mm(ps[:, ab:ab + 1], ps.b, self.w1[:, ab, j, :], self.w1.b, self.peT[:, ab, j:j + 1], self.peT.b,
                        start=(j == 0), stop=(j == 15))
        self.cp(self.cAB[:], self.cAB.b, ps[:, 0:2], ps.b)
        t2 = self.junk
        it = self.itmp
        S.op("pool", lambda e: e.iota(it[:, 0:33], pattern=[[64, 33]], base=0, channel_multiplier=0), writes=[it.b])
        S.op("pool", lambda e: e.iota(it[:, 64:97], pattern=[[0, 33]], base=0, channel_multiplier=16), writes=[it.b])
        S.op("pool", lambda e: e.iota(it[:, 128:129], pattern=[[0, 1]], base=0, channel_multiplier=2), writes=[it.b])
        self.cp(t2[:, 0:129], t2.b, it[:, 0:129], it.b)
        self.cp(self.piota2[:], self.piota2.b, t2[:, 128:129], t2.b)
        self.ts(t2[:, 256:289], t2.b, t2[:, 0:33], t2.b, 64.0, ALU.add)
        self.stt(t2[:, 320:353], t2.b, t2[:, 64:97], t2.b, 32.0, t2[:, 256:289], t2.b, ALU.add, ALU.min)
        self.tt(t2[:, 384:417], t2.b, t2[:, 64:97], t2.b, t2[:, 0:33], t2.b, ALU.max)
        self.tt(t2[:, 448:481], t2.b, t2[:, 320:353], t2.b, t2[:, 384:417], t2.b, ALU.subtract)
        self.ts(self.mov[:], self.mov.b, t2[:, 448:481], t2.b, 0.0, ALU.max, 1.0 / 32, ALU.mult)
        self.cp(self.mov33[:], self.mov33.b, self.mov[:], self.mov.b)
        ms(self.ones_bf[:], self.ones_bf.b, 1.0)
        self.rope_table(self.cosT, self.sinT, 16, lambda e, v: e.iota(v, pattern=[[128, 16], [0, 8]], base=0, channel_multiplier=1), 128)
        self.rope_table(self.cosC, self.sinC, 8, lambda e, v: e.iota(v, pattern=[[2048, 8], [0, 8]], base=31, channel_multiplier=16), 128)
        self.rope_table(self.cosS, self.sinS, 1, lambda e, v: e.iota(v, pattern=[[0, 1], [0, 8]], base=PAST, channel_multiplier=1), 8)
        ms(self.VS1[:], self.VS1.b, 1.0)
        ms(self.VW1[:], self.VW1.b, 1.0)
        ms(self.VC1[:], self.VC1.b, 1.0)
        for l in range(2):
            ms(self.V1[l][:], self.V1[l].b, 1.0)
        for g in range(4):
            self.cp(self.VC1[:, g, 65:97], self.VC1.b, self.mov[:, 0:32], self.mov.b)

    def rope_table(self, cosT, sinT, ncol, iota_fn, rows):
        S = self.S
        t = self.f1
        v = t[0:rows, 0:ncol * 8].rearrange("p (c k) -> p c k", k=8)
        iv = self.itmp[0:rows, 0:ncol * 8].rearrange("p (c k) -> p c k", k=8)
        S.op("pool", lambda e: iota_fn(e, iv), writes=[self.itmp.b])
        self.cp(v, t.b, iv, self.itmp.b)
        inv = self.f2
        for k in range(8):
            self.memset(inv[0:rows, k:k + 1], inv.b, THETA ** (-k / 8.0))
        invb = inv[0:rows, 0:8].unsqueeze(1).to_broadcast([rows, ncol, 8])
        ang = self.f3
        av = ang[0:rows, 0:ncol * 8].rearrange("p (c k) -> p c k", k=8)
        self.tt(av, ang.b, v, t.b, invb, inv.b, ALU.mult)
        a2 = ang[0:rows, 0:ncol * 8]
        w = self.junk[0:rows, 0:ncol * 8]
        self.range_reduce(w, a2, ang.b, rows, ncol * 8, 0.0)
        self.act(sinT[0:rows].rearrange("p c k -> p (c k)"), sinT.b, w, self.junk.b, AF.Sin)
        self.range_reduce(w, a2, ang.b, rows, ncol * 8, math.pi / 2)
        self.act(cosT[0:rows].rearrange("p c k -> p (c k)"), cosT.b, w, self.junk.b, AF.Sin)

    def range_reduce(self, w, a, ab, rows, m, shift):
        jb = self.junk.b
        two_pi = 2 * math.pi
        hi = 6.28125
        lo = two_pi - hi
        t = self.junk[0:rows, 512:512 + m]
        it = self.itmp[0:rows, 0:m]
        kf = self.junk[0:rows, 768:768 + m]
        self.ts(w, jb, a, ab, shift, ALU.add)
        self.ts(t, jb, w, jb, 1.0 / two_pi, ALU.mult)
        self.cp(it, self.itmp.b, t, jb)
        self.cp(kf, jb, it, self.itmp.b)
        self.stt(w, jb, kf, jb, -hi, w, jb, ALU.mult, ALU.add)
        self.stt(w, jb, kf, jb, -lo, w, jb, ALU.mult, ALU.add)
        self.ts(t, jb, w, jb, math.pi, ALU.is_gt)
        self.stt(w, jb, t, jb, -two_pi, w, jb, ALU.mult, ALU.add)
        self.ts(t, jb, w, jb, -math.pi, ALU.is_lt)
        self.stt(w, jb, t, jb, two_pi, w, jb, ALU.mult, ALU.add)
        self.ts(w, jb, w, jb, math.pi - 1e-5, ALU.min, -(math.pi - 1e-5), ALU.max)

    def prompt_seq(self, s):
        self.mem_kv_prompt(s)
        if self.stage < 3:
            return
        self.seq_reset()
        for tau in range(0, self.n_ptiles, 2):
            self.tile_pair(s, tau)
        self.xres = self.xr[0]

    def seq_reset(self):
        ms = self.memset
        ms(self.ubuf[:], self.ubuf.b, 0.0)
        ms(self.xbuf[:], self.xbuf.b, 0.0)
        ms(self.hstate[:], self.hstate.b, 0.0)
        ms(self.hstate_bf[:], self.hstate_bf.b, 0.0)
        ms(self.Aall[:], self.Aall.b, 0.0)
        ms(self.pre[:], self.pre.b, 0.0)

    def mem_kv_prompt(self, s):
        S = self.S
        for l in range(2):
            for mc in range(2):
                x = self.xres
                S.dma("pool", x[:], self.mem_prompt[s, mc * 128:(mc + 1) * 128, :], writes=[x.b])
                sub = 9
                if sub >= 1:
                    self.rmsnorm_T(x, 128, self.g_src[l], self.hT)
                if sub < 2:
                    S.dma("pool", self.p_mem_kv[l, s, mc * 128:(mc + 1) * 128, :], x[:], reads=[x.b, self.hT.b], writes=[self.b_p_mem_kv])
                    continue
                out = self.f1

                def consume(off, nb, ps, pb, out=out, l=l, mc=mc):
                    if sub < 3:
                        self.cp(out[:, off:off + nb], out.b, ps, pb, eng="act")
                        return
                    if off == 0:
                        sq = self.junk
                        self.act(sq[:, 0:512], sq.b, ps, pb, AF.Square)
                        self.red(self.sm[:, 0:4], self.sm.b, sq[:, 0:512].rearrange("p (h d) -> p h d", h=4), sq.b)
                        self.rstd(self.sm[:, 4:8], self.sm.b, self.sm[:, 0:4], self.sm.b, 128, 1.0 / 128)
                        rb = self.sm[:, 4:8].unsqueeze(2).to_broadcast([128, 4, 128])
                        ov = out[:, 0:512].rearrange("p (h d) -> p h d", h=4)
                        self.tt(ov, out.b, ps.rearrange("p (h d) -> p h d", h=4), pb, rb, self.sm.b, ALU.mult)
                        self.cp(self.big[:, 0:512], self.big.b, out[:, 0:512], out.b, eng="act")
                        gb = self.memk_g[l][:].unsqueeze(1).to_broadcast([128, 4, 128])
                        self.tt(ov, out.b, ov, out.b, gb, self.memk_g[l].b, ALU.mult)
                    else:
                        self.cp(out[:, 512:1024], out.b, ps, pb, eng="act")
                        self.cp(self.V1[l][:, mc, :, 0:128], self.V1[l].b, ps.rearrange("p (h d) -> p h d", h=4), pb)
                self.linear_tm(self.hT, 128, self.W_kv[l], 1024, 0, 1024, consume)
                S.dma("pool", self.p_mem_kv[l, s, mc * 128:(mc + 1) * 128, :], out[:], reads=[out.b], writes=[self.b_p_mem_kv])
                if sub < 4:
                    continue
                pb = self.psumb()
                for h in range(4):
                    self.tr(pb[:, h * 128:(h + 1) * 128], pb.b, self.big[:, h * 128:(h + 1) * 128], self.big.b)
                self.ts(self.KT[l][:, :, mc * 128:(mc + 1) * 128], self.KT[l].b, pb[:, 0:512].rearrange("p (h t) -> p h t", h=4),
                        pb.b, self.memk_gc[l][:, 0:1], ALU.mult, rd=[self.memk_gc[l].b])

    def tile(self, s, tau):
        S = self.S
        n = 128
        x = self.xres
        S.dma("pool", x[:], self.x_prompt[s, tau * 128:(tau + 1) * 128, :], writes=[x.b])
        st = self.stage
        if st >= 4:
            self.layer0(s, tau, n)
        if st >= 5:
            self.mem_attn(0, n)
        if st >= 6:
            self.ffn(0, n)
        if st >= 7:
            self.layer1(s, tau, n)
        if st >= 8:
            self.mem_attn(1, n)
            self.ffn(1, n)
        S.dma("pool", self.y_prompt[s, tau * 128:(tau + 1) * 128, :], x[:], reads=[x.b], writes=[self.b_y_prompt])

    def tile_pair(self, s, tau):
        S = self.S
        n = 128
        xr = self.xr
        for i in range(2):
            S.dma("sp", xr[i][:], self.x_prompt[s, (tau + i) * 128:(tau + i + 1) * 128, :], writes=[xr[i].b])
        for i in range(2):
            self.xres = xr[i]
            self.layer0(s, tau + i, n)
            self.mem_attn(0, n)
        self.ffn_pair(0, n)
        for i in range(2):
            self.xres = xr[i]
            self.layer1(s, tau + i, n)
            self.mem_attn(1, n)
        self.ffn_pair(1, n)
        for i in range(2):
            S.dma("pool", self.y_prompt[s, (tau + i) * 128:(tau + i + 1) * 128, :], xr[i][:], reads=[xr[i].b],
                  writes=[self.b_y_prompt])

    def wrows(self, Wd, r0, m, ncols):
        i = self.w_rr % len(self.wbufs)
        self.w_rr += 1
        wb = self.wbufs[i]
        src = Wd.t[r0 * 128:(r0 + m) * 128, 0:ncols].rearrange("(c p) n -> p c n", p=128)
        view = wb[:, 0:m * ncols].rearrange("p (c n) -> p c n", c=m)
        self.S.dma("sp", view, src, reads=[Wd.b], writes=[wb.b])
        return view, wb.b

    def ffn_pair(self, l, n):
        xr = self.xr
        pTf = self.pT.t[:].rearrange("p a b -> p (a b)")
        hTs = [self.hT, TT(pTf[:, 0:1024].rearrange("p (c t) -> p c t", c=8), "hTB", self.pT.b)]
        acts = [self.big, TT(pTf[:, 1024:1024 + DFF], "actB", self.pT.b)]
        sgs = [TT(self.f1.t[:, 0:512], "sgA", self.f1.b), TT(self.f2.t[:, 0:512], "sgB", self.f2.b)]
        for i in range(2):
            self.rmsnorm_T(xr[i], n, self.g_ffn[l], hTs[i])
        off = 0
        while off < DFF:
            nb = min(512, DFF - off)
            wv, wbb = self.wblock(self.W_g[l], 1024, off, nb)
            for i in range(2):
                ps = self.psum()
                for k in range(8):
                    self.mm(ps[0:n, 0:nb], ps.b, hTs[i][:, k, 0:n], hTs[i].b, wv[:, k, 0:nb], wbb, start=(k == 0), stop=(k == 7))
                self.act(sgs[i][0:n, 0:nb], sgs[i].b, ps[0:n, 0:nb], ps.b, AF.Silu)
            wv, wbb = self.wblock(self.W_u[l], 1024, off, nb)
            for i in range(2):
                ps = self.psum()
                for k in range(8):
                    self.mm(ps[0:n, 0:nb], ps.b, hTs[i][:, k, 0:n], hTs[i].b, wv[:, k, 0:nb], wbb, start=(k == 0), stop=(k == 7))
                self.tt(acts[i][0:n, off:off + nb], acts[i].b, ps[0:n, 0:nb], ps.b, sgs[i][0:n, 0:nb], sgs[i].b, ALU.mult)
            off += nb
        po = [[self.psum(), self.psum()] for _ in range(2)]
        bT = self.bigT
        aTs = [[TT(bT.t[:, 4 * (2 * par + i):4 * (2 * par + i) + 4, :], "aT%d%d" % (par, i)) for i in range(2)] for par in range(2)]
        nk = DFF // 128
        gi = 0
        for kg in range(0, nk, 4):
            m = min(4, nk - kg)
            wv, wbb = self.wrows(self.W_d[l], kg, m, 1024)
            for i in range(2):
                aT = aTs[gi % 2][i]
                pb = self.psumb()
                for j in range(m):
                    self.tr(pb[:, j * n:(j + 1) * n], pb.b, acts[i][0:n, (kg + j) * 128:(kg + j + 1) * 128], acts[i].b)
                self.cp(aT[:, 0:m, 0:n], aT.b, pb[:, 0:m * n].rearrange("p (c t) -> p c t", c=m), pb.b, eng=("act" if i == 0 else "dve"))
                for half in range(2):
                    p_ = po[i][half]
                    for j in range(m):
                        self.mm(p_[0:n, :], p_.b, aT[:, j, 0:n], aT.b, wv[:, j, half * 512:(half + 1) * 512], wbb,
                                start=(kg == 0 and j == 0), stop=(kg + j == nk - 1))
            gi += 1
        for i in range(2):
            x = xr[i]
            for half in range(2):
                p_ = po[i][half]
                self.tt(x[0:n, half * 512:(half + 1) * 512], x.b, x[0:n, half * 512:(half + 1) * 512], x.b, p_[0:n, :], p_.b, ALU.add)

    def ffn(self, l, n):
        x = self.xres
        self.rmsnorm_T(x, n, self.g_ffn[l], self.hT)
        act = self.big
        sg = self.f1

        def c_gate(off, nb, ps, pb):
            self.act(sg[0:n, off:off + nb], sg.b, ps, pb, AF.Silu)
        off = 0
        while off < DFF:
            nb = min(512, DFF - off)
            self.linear_tm(self.hT, n, self.W_g[l], 1024, off, nb, lambda o, b, ps, pb: self.act(sg[0:n, 0:nb], sg.b, ps, pb, AF.Silu))
            self.linear_tm(self.hT, n, self.W_u[l], 1024, off, nb,
                           lambda o, b, ps, pb: self.tt(act[0:n, off:off + nb], act.b, ps, pb, sg[0:n, 0:nb], sg.b, ALU.mult))
            off += nb
        self.transpose_fm(act, n, 22, self.bigT)
        self.linear_tm(self.bigT, n, self.W_d[l], DFF, 0, 1024,
                       lambda o, b, ps, pb: self.tt(x[0:n, o:o + b], x.b, x[0:n, o:o + b], x.b, ps, pb, ALU.add))

    def mem_attn(self, l, n):
        x = self.xres
        self.rmsnorm_T(x, n, self.g_mem[l], self.hT)
        q = self.f1
        sm = self.sm

        def cq(off, nb, ps, pb):
            sq = self.junk
            self.act(sq[0:n, 0:512], sq.b, ps, pb, AF.Square)
            self.red(sm[0:n, 0:4], sm.b, sq[0:n, 0:512].rearrange("p (h d) -> p h d", h=4), sq.b)
            self.rstd(sm[0:n, 4:8], sm.b, sm[0:n, 0:4], sm.b, 128, 1.0 / 128)
            rb = sm[0:n, 4:8].unsqueeze(2).to_broadcast([n, 4, 128])
            self.tt(self.big[0:n, 0:512].rearrange("p (h d) -> p h d", h=4), self.big.b,
                    ps.rearrange("p (h d) -> p h d", h=4), pb, rb, sm.b, ALU.mult)
        self.linear_tm(self.hT, n, self.W_q[l], 1024, 0, 512, cq)
        pb = self.psumb()
        for h in range(4):
            self.tr(pb[:, h * n:(h + 1) * n], pb.b, self.big[0:n, h * 128:(h + 1) * 128], self.big.b)
        self.ts(self.qT[:, 0:4, 0:n], self.qT.b, pb[:, 0:4 * n].rearrange("p (h t) -> p h t", h=4), pb.b,
                self.memq_g[l][:, 0:1], ALU.mult, rd=[self.memq_g[l].b])
        pT = self.pT
        for mc in range(2):
            ps = self.psum()
            for h in range(4):
                self.mm(ps[:, h * n:(h + 1) * n], ps.b, self.KT[l][:, h, mc * 128:(mc + 1) * 128], self.KT[l].b,
                        self.qT[:, h, 0:n], self.qT.b)
            self.act(pT[:, mc, 0:4 * n], pT.b, ps[:, 0:4 * n], ps.b, AF.Exp)
        o = self.big
        for h in range(4):
            ps = self.psum()
            for mc in range(2):
                self.mm(ps[0:n, 0:129], ps.b, pT[:, mc, h * n:(h + 1) * n], pT.b, self.V1[l][:, mc, h, :], self.V1[l].b,
                        start=(mc == 0), stop=(mc == 1))
            self.recip(sm[0:n, 8:9], sm.b, ps[0:n, 128:129], ps.b)
            self.ts(o[0:n, h * 128:(h + 1) * 128], o.b, ps[0:n, 0:128], ps.b, sm[0:n, 8:9], ALU.mult, rd=[sm.b])
        self.transpose_fm(o, n, 4, self.bigT)
        self.linear_tm(self.bigT, n, self.W_o[l], 512, 0, 1024,
                       lambda o_, b, ps, pb: self.tt(x[0:n, o_:o_ + b], x.b, x[0:n, o_:o_ + b], x.b, ps, pb, ALU.add))

    def layer0(self, s, tau, n, last=None):
        S = self.S
        x = self.xres
        last = (tau == self.n_ptiles - 1) if last is None else last
        self.rmsnorm_T(x, n, self.g_mix[0], self.hT)
        ubuf, xbuf, s_h = self.ubuf, self.xbuf, self.s_h
        self.linear_fm(self.hT, n, self.W_in0, 1024, 0, 8,
                       lambda c, m, ps, pb: self.cp(s_h[:, c:c + m, 0:n], s_h.b, ps, pb, eng="act"))
        sbv = self.f2[:, 0:8 * n].rearrange("p (c t) -> p c t", c=8)
        self.linear_fm(self.hT, n, self.W_in0, 1024, 1024, 8,
                       lambda c, m, ps, pb: self.cp(sbv[:, c:c + m, :], self.f2.b, ps, pb, eng="act"))
        self.linear_fm(self.hT, n, self.W_in0, 1024, 2048, 8,
                       lambda c, m, ps, pb: self.tt(ubuf[:, c:c + m, 2:2 + n], ubuf.b, ps, pb, s_h[:, c:c + m, 0:n], s_h.b, ALU.mult))
        ca = self.cacc
        for k in range(3):
            wk = self.sconv_w[:, :, k:k + 1].to_broadcast([128, 8, n])
            if k == 0:
                self.tt(ca[:, 0:8, 0:n], ca.b, ubuf[:, :, 0:n], ubuf.b, wk, self.sconv_w.b, ALU.mult)
            else:
                self.tt(self.junk[:, 0:8 * n].rearrange("p (c t) -> p c t", c=8), self.junk.b, ubuf[:, :, k:k + n], ubuf.b,
                        wk, self.sconv_w.b, ALU.mult)
                self.tt(ca[:, 0:8, 0:n], ca.b, ca[:, 0:8, 0:n], ca.b, self.junk[:, 0:8 * n].rearrange("p (c t) -> p c t", c=8),
                        self.junk.b, ALU.add)
        ycat = self.bigT
        self.tt(ycat[:, 0:8, 0:n], ycat.b, ca[:, 0:8, 0:n], ca.b, sbv, self.f2.b, ALU.mult)
        if last:
            with self.nc.allow_non_contiguous_dma("tiny state out"):
                dst, db = (self.p_sconv, self.b_p_sconv) if self.cur_prompt else (self.s_sconv, self.b_s_sconv)
                for k in range(2):
                    S.dma("pool", dst[s, k].rearrange("(c p) -> p c", p=128), ubuf[:, :, n + k], reads=[ubuf.b], writes=[db])
        self.cp(self.sm[:, 0:16].rearrange("p (c k) -> p c k", k=2), self.sm.b, ubuf[:, :, n:n + 2], ubuf.b)
        self.cp(ubuf[:, :, 0:2], ubuf.b, self.sm[:, 0:16].rearrange("p (c k) -> p c k", k=2), self.sm.b)
        self.linear_fm(self.hT, n, self.W_in0, 1024, 4096, 16,
                       lambda c, m, ps, pb: self.cp(xbuf[:, c:c + m, 3:3 + n], xbuf.b, ps, pb, eng="act"))
        if last:
            with self.nc.allow_non_contiguous_dma("tiny state out"):
                dst, db = (self.p_ssd_conv, self.b_p_ssd_conv) if self.cur_prompt else (self.s_ssd_conv, self.b_s_ssd_conv)
                for k in range(3):
                    S.dma("pool", dst[s, k].rearrange("(c p) -> p c", p=128), xbuf[:, :, n + k], reads=[xbuf.b], writes=[db])
        xcs = self.xcs
        for hh in range(2):
            c0 = 8 * hh
            acc = ca[:, 0:8, 0:n]
            jv2 = self.junk[:, 0:8 * n].rearrange("p (c t) -> p c t", c=8)
            for k in range(4):
                wk2 = self.ssdc_w[:, c0:c0 + 8, k:k + 1].to_broadcast([128, 8, n])
                if k == 0:
                    self.tt(acc, ca.b, xbuf[:, c0:c0 + 8, 0:n], xbuf.b, wk2, self.ssdc_w.b, ALU.mult)
                else:
                    self.tt(jv2, self.junk.b, xbuf[:, c0:c0 + 8, k:k + n], xbuf.b, wk2, self.ssdc_w.b, ALU.mult)
                    self.tt(acc, ca.b, acc, ca.b, jv2, self.junk.b, ALU.add)
            bb = self.ssdc_b[:, c0:c0 + 8].unsqueeze(2).to_broadcast([128, 8, n])
            self.tt(acc, ca.b, acc, ca.b, bb, self.ssdc_b.b, ALU.add)
            self.act(xcs[:, c0:c0 + 8, 0:n], xcs.b, acc, ca.b, AF.Silu)
        self.cp(self.sm[:, 16:64].rearrange("p (c k) -> p c k", k=3), self.sm.b, xbuf[:, :, n:n + 3], xbuf.b)
        self.cp(xbuf[:, :, 0:3], xbuf.b, self.sm[:, 16:64].rearrange("p (c k) -> p c k", k=3), self.sm.b)
        x_tm, B_tm = self.x_tm, self.B_tm
        pb = self.psumb()
        for c in range(8):
            self.tr(pb[0:n, c * 128:(c + 1) * 128], pb.b, xcs[:, c, 0:n], xcs.b)
        self.cp(x_tm[0:n, :], x_tm.b, pb[0:n, :], pb.b, eng="act")
        pb = self.psumb()
        for c in range(4):
            self.tr(pb[0:n, c * 128:(c + 1) * 128], pb.b, xcs[:, 8 + c, 0:n], xcs.b)
        self.cp(B_tm[0:n, :], B_tm.b, pb[0:n, 0:512], pb.b, eng="act")
        zs = self.f1
        self.linear_tm(self.hT, n, self.W_in0, 1024, 3072, 1024,
                       lambda o, b, ps, pb_: self.act(zs[0:n, o:o + b], zs.b, ps, pb_, AF.Silu))
        sm = self.sm
        DT, DA, CUM, ECUM, DTE, CL = (sm[0:n, 64:80], sm[0:n, 80:96], sm[0:n, 96:112], sm[0:n, 112:128], sm[0:n, 128:144],
                                     sm[0:n, 144:160])

        def cdt(o, b, ps, pb_):
            self.tt(DT, sm.b, ps, pb_, self.dt_bias[0:n, :], self.dt_bias.b, ALU.add)
            self.act(DT, sm.b, DT, sm.b, AF.Exp)
            self.act(DT, sm.b, DT, sm.b, AF.Ln, bias=1.0)
        self.linear_tm(self.hT, n, self.W_in0, 1024, 6144, 16, cdt)
        self.tt(DA, sm.b, DT, sm.b, self.Aneg[0:n, :], self.Aneg.b, ALU.mult)
        ps = self.psum()
        psx = self.psum()
        self.mm(ps[0:n, 0:16], ps.b, self.U[0:n, 0:n], self.U.b, DA, sm.b)
        self.mm(psx[:, 16:32], psx.b, self.onesf[0:n, :], self.onesf.b, DA, sm.b)
        self.cp(CUM, sm.b, ps[0:n, 0:16], ps.b)
        self.cp(sm[:, 144:160], sm.b, psx[:, 16:32], psx.b)
        self.act(ECUM, sm.b, CUM, sm.b, AF.Exp)
        self.tt(DTE, sm.b, CL, sm.b, CUM, sm.b, ALU.subtract)
        self.act(DTE, sm.b, DTE, sm.b, AF.Exp)
        self.act(sm[:, 144:160], sm.b, sm[:, 144:160], sm.b, AF.Exp)
        xdt, xdte = self.xdt, self.xdte
        self.tt(xdt[0:n, :].rearrange("p (h d) -> p h d", h=16), xdt.b, x_tm[0:n, :].rearrange("p (h d) -> p h d", h=16), x_tm.b,
                DT.unsqueeze(2).to_broadcast([n, 16, 64]), sm.b, ALU.mult)
        self.tt(xdte[0:n, :].rearrange("p (h d) -> p h d", h=16), xdte.b, xdt[0:n, :].rearrange("p (h d) -> p h d", h=16), xdt.b,
                DTE.unsqueeze(2).to_broadcast([n, 16, 64]), sm.b, ALU.mult)
        rmask = self.rmask
        cbm = self.cbm
        psc = self.psum()
        for g in range(4):
            self.mm(psc[0:n, g * n:(g + 1) * n], psc.b, xcs[:, 8 + g, 0:n], xcs.b, xcs[:, 12 + g, 0:n], xcs.b)
        self.tt(cbm[0:n, :, 0:n], cbm.b, psc[0:n, 0:4 * n].rearrange("p (g t) -> p g t", g=4), psc.b,
                self.U[0:n, 0:n].unsqueeze(1).to_broadcast([n, 4, n]), self.U.b, ALU.mult)
        seg, Mh = self.seg, self.Mh
        for g in range(4):
            self.tt(rmask[0:n, :, 0:n], rmask.b, self.U[0:n, 0:n].unsqueeze(1).to_broadcast([n, 4, n]), self.U.b,
                    sm[0:n, 80 + 4 * g:84 + 4 * g].unsqueeze(2).to_broadcast([n, 4, n]), sm.b, ALU.mult)
            ps = self.psum()
            for hh in range(4):
                h = 4 * g + hh
                self.mm(ps[0:n, hh * n:(hh + 1) * n], ps.b, self.onesf[0:n, 0:n], self.onesf.b, rmask[0:n, hh, 0:n], rmask.b)
            for hh in range(4):
                h = 4 * g + hh
                self.ts(seg[0:n, hh, 0:n], seg.b, ps[0:n, hh * n:(hh + 1) * n], ps.b, sm[0:n, 96 + h:97 + h], ALU.min,
                        sm[0:n, 96 + h:97 + h], ALU.subtract, rd=[sm.b])
            self.act(seg[0:n, :, 0:n], seg.b, seg[0:n, :, 0:n], seg.b, AF.Exp)
            self.tt(Mh[0:n, 4 * g:4 * g + 4, 0:n], Mh.b, seg[0:n, :, 0:n], seg.b,
                    cbm[0:n, g:g + 1, 0:n].to_broadcast([n, 4, n]), cbm.b, ALU.mult)
        y = self.f3
        hs, hsb = self.hstate, self.hstate_bf
        for half in range(2):
            psd = self.psum()
            pso = self.psum()
            for hh in range(8):
                h = 8 * half + hh
                self.mm(psd[0:n, hh * 64:(hh + 1) * 64], psd.b, Mh[0:n, h, 0:n], Mh.b, xdt[0:n, h * 64:(h + 1) * 64], xdt.b)
            for gg in range(2):
                g = 2 * half + gg
                self.mm(pso[0:n, gg * 256:(gg + 1) * 256], pso.b, xcs[:, 12 + g, 0:n], xcs.b, hsb[:, g * 256:(g + 1) * 256], hsb.b)
            yv = y[0:n, half * 512:(half + 1) * 512].rearrange("p (h d) -> p h d", h=8)
            self.tt(yv, y.b, pso[0:n, :].rearrange("p (h d) -> p h d", h=8), pso.b,
                    ECUM[:, 8 * half:8 * half + 8].unsqueeze(2).to_broadcast([n, 8, 64]), sm.b, ALU.mult)
            self.tt(y[0:n, half * 512:(half + 1) * 512], y.b, y[0:n, half * 512:(half + 1) * 512], y.b, psd[0:n, :], psd.b, ALU.add)
        jv = self.junk[0:n, :].rearrange("p (h d) -> p h d", h=16)
        self.tt(jv, self.junk.b, x_tm[0:n, :].rearrange("p (h d) -> p h d", h=16), x_tm.b,
                self.Dsk[0:n, :].unsqueeze(2).to_broadcast([n, 16, 64]), self.Dsk.b, ALU.mult)
        self.tt(y[0:n, :], y.b, y[0:n, :], y.b, self.junk[0:n, :], self.junk.b, ALU.add)
        self.tt(y[0:n, :], y.b, y[0:n, :], y.b, zs[0:n, :], zs.b, ALU.mult)
        self.tt(self.junk[0:n, :], self.junk.b, y[0:n, :], y.b, y[0:n, :], y.b, ALU.mult)
        self.red(sm[0:n, 160:164], sm.b, self.junk[0:n, :].rearrange("p (g d) -> p g d", g=4), self.junk.b)
        self.rstd(sm[0:n, 164:168], sm.b, sm[0:n, 160:164], sm.b, 256, 1.0 / 256)
        yb = self.big
        self.tt(yb[0:n, 0:1024].rearrange("p (g d) -> p g d", g=4), yb.b, y[0:n, :].rearrange("p (g d) -> p g d", g=4), y.b,
                sm[0:n, 164:168].unsqueeze(2).to_broadcast([n, 4, 256]), sm.b, ALU.mult)
        for c0 in (0, 4):
            pb = self.psumb()
            for j in range(4):
                self.tr(pb[:, j * n:(j + 1) * n], pb.b, yb[0:n, (c0 + j) * 128:(c0 + j + 1) * 128], yb.b)
            self.tt(ycat[:, 8 + c0:12 + c0, 0:n], ycat.b, pb[:, 0:4 * n].rearrange("p (c t) -> p c t", c=4), pb.b,
                    self.g_ssdn[:, c0:c0 + 4].unsqueeze(2).to_broadcast([128, 4, n]), self.g_ssdn.b, ALU.mult)
        for half in range(2):
            ps = self.psum()
            for gg in range(2):
                g = 2 * half + gg
                self.mm(ps[:, gg * 256:(gg + 1) * 256], ps.b, B_tm[0:n, g * 128:(g + 1) * 128], B_tm.b,
                        xdte[0:n, g * 256:(g + 1) * 256], xdte.b)
            hv = hs[:, half * 512:(half + 1) * 512].rearrange("p (h d) -> p h d", h=8)
            self.tt(hv, hs.b, hv, hs.b, sm[:, 144 + 8 * half:152 + 8 * half].unsqueeze(2).to_broadcast([128, 8, 64]), sm.b, ALU.mult)
            self.tt(hs[:, half * 512:(half + 1) * 512], hs.b, hs[:, half * 512:(half + 1) * 512], hs.b, ps[:, :], ps.b, ALU.add)
        self.cp(hsb[:], hsb.b, hs[:], hs.b, eng="act")
        if last:
            for c in range(8):
                ps = self.psum()
                self.S.op("pe", lambda e: e.transpose(ps[:, 0:128], hs[:, c * 128:(c + 1) * 128], self.identf[:]),
                          reads=[hs.b, self.identf.b], writes=[ps.b])
                self.cp(self.f2[:, 0:128], self.f2.b, ps[:, 0:128], ps.b)
                dst, db = (self.p_ssd, self.b_p_ssd) if self.cur_prompt else (self.s_ssd, self.b_s_ssd)
                self.S.dma("pool", dst[s, c * 128:(c + 1) * 128, :], self.f2[:, 0:128], reads=[self.f2.b], writes=[db])
        self.linear_tm(ycat, n, self.W_out0, 2048, 0, 1024,
                       lambda o, b, ps, pb_: self.tt(x[0:n, o:o + b], x.b, x[0:n, o:o + b], x.b, ps, pb_, ALU.add))

    def headnorm(self, v, vb, n, H, gain, gb, so):
        sm = self.sm
        jv = self.junk[0:n, 0:H * 64].rearrange("p (h d) -> p h d", h=H)
        self.tt(jv, self.junk.b, v, vb, v, vb, ALU.mult)
        self.red(sm[0:n, so:so + H], sm.b, jv, self.junk.b)
        self.rstd(sm[0:n, so + H:so + 2 * H], sm.b, sm[0:n, so:so + H], sm.b, 64, 1.0 / 64)
        self.tt(v, vb, v, vb, sm[0:n, so + H:so + 2 * H].unsqueeze(2).to_broadcast([n, H, 64]), sm.b, ALU.mult)
        self.tt(v, vb, v, vb, gain.unsqueeze(1).to_broadcast([n, H, 64]), gb, ALU.mult)

    def rope(self, v, vb, n, H, cos, sin, cb):
        f2 = self.f2
        t = [f2[0:n, i * 128:i * 128 + H * 8].rearrange("p (h k) -> p h k", k=8) for i in range(4)]
        cb_ = cos.unsqueeze(1).to_broadcast([n, H, 8])
        sb_ = sin.unsqueeze(1).to_broadcast([n, H, 8])
        x1, x2 = v[:, :, 0:8], v[:, :, 8:16]
        self.tt(t[0], f2.b, x1, vb, cb_, cb, ALU.mult)
        self.tt(t[1], f2.b, x2, vb, sb_, cb, ALU.mult)
        self.tt(t[2], f2.b, x2, vb, cb_, cb, ALU.mult)
        self.tt(t[3], f2.b, x1, vb, sb_, cb, ALU.mult)
        self.tt(x1, vb, t[0], f2.b, t[1], f2.b, ALU.subtract)
        self.tt(x2, vb, t[2], f2.b, t[3], f2.b, ALU.add)

    def nsa_front(self, n, cos, sin, cb):
        x, sm = self.xres, self.sm
        self.rmsnorm_T(x, n, self.g_mix[1], self.hT)
        rows, winr, gates, q = self.rows, self.winr, self.gates, self.f1

        def cons(off, nb, ps, pb):
            if off < 1024:
                self.cp(q[0:n, off:off + nb], q.b, ps, pb, eng="act")
            elif off == 1024:
                self.cp(rows[0:n, 0:512], rows.b, ps, pb, eng="act")
            elif off == 1536:
                self.cp(rows[0:n, 512:1024], rows.b, ps, pb, eng="act")
            elif off == 2048:
                self.cp(winr[0:n, :], winr.b, ps, pb, eng="act")
            else:
                self.act(gates[0:n, :], gates.b, ps, pb, AF.Exp, scale=-1.0)
                self.ts(gates[0:n, :], gates.b, gates[0:n, :], gates.b, 1.0, ALU.add)
                self.recip(gates[0:n, :], gates.b, gates[0:n, :], gates.b)
        self.linear_tm(self.hT, n, self.W_in1, 1024, 0, IN1, cons)
        qv = q[0:n, :].rearrange("p (h d) -> p h d", h=16)
        self.headnorm(qv, q.b, n, 16, self.nsaq_g[0:n, :], self.nsaq_g.b, 168)
        self.rope(qv, q.b, n, 16, cos, sin, cb)
        kv_ = rows[0:n, 512:768].rearrange("p (h d) -> p h d", h=4)
        self.headnorm(kv_, rows.b, n, 4, self.nsak_g[0:n, 1, :], self.nsak_g.b, 200)
        self.rope(kv_, rows.b, n, 4, cos, sin, cb)
        kw_ = winr[0:n, 0:256].rearrange("p (h d) -> p h d", h=4)
        self.headnorm(kw_, winr.b, n, 4, self.nsak_g[0:n, 2, :], self.nsak_g.b, 200)
        self.rope(kw_, winr.b, n, 4, cos, sin, cb)
        return qv, kv_, kw_

    def nsa_q_stage(self, n, qv):
        big, q, qT2 = self.big, self.f1, self.qT2
        qd = big[0:n, 0:2048].rearrange("p (h c d) -> p h c d", h=16, c=2)
        self.cp(qd[:, :, 0, :], big.b, qv, q.b, eng="act")
        self.cp(qd[:, :, 1, :], big.b, qv, q.b, eng="pool")
        for c0 in range(0, 16, 8):
            pb = self.psumb()
            for j in range(8):
                a = (c0 + j) * 128
                self.tr(pb[:, j * n:(j + 1) * n], pb.b, big[0:n, a:a + 128], big.b)
            pv = pb[:, 0:8 * n].rearrange("p (c t) -> p c t", c=8)
            self.cp(qT2[0:64, 0, c0:c0 + 8, 0:n], qT2.b, pv[0:64], pb.b, eng="act")
            self.cp(qT2[64:128, 1, c0:c0 + 8, 0:n], qT2.b, pv[64:128], pb.b)

    def layer1(self, s, tau, n):
        S = self.S
        x, sm = self.xres, self.sm
        rows, winr, gates, q = self.rows, self.winr, self.gates, self.f1
        qv, kv_, kw_ = self.nsa_front(n, self.cosT[0:n, tau, :], self.sinT[0:n, tau, :], self.cosT.b)
        S.dma("pool", self.p_nsa_rows[s, tau * 128:tau * 128 + n, :], rows[0:n, :], reads=[rows.b], writes=[self.b_p_nsa_rows])
        w0 = self.n_ptiles - 4
        if tau >= w0:
            S.dma("pool", self.p_nsa_win[s, (tau - w0) * 128:(tau - w0) * 128 + n, :], winr[0:n, :], reads=[winr.b],
                  writes=[self.b_p_nsa_win])
        big = self.big
        self.nsa_q_stage(n, qv)
        self.cp(big[0:n, 2048:2560].rearrange("p (g c d) -> p g c d", g=4, c=2), big.b,
                rows[0:n, 0:512].rearrange("p (c g d) -> p g c d", c=2, g=4), rows.b, eng="pool")
        kst = self.hn[0:n, 0:512].rearrange("p (g c d) -> p g c d", g=4, c=2)
        self.cp(kst[:, :, 0, :], self.hn.b, kv_, rows.b)
        self.cp(kst[:, :, 1, :], self.hn.b, kw_, winr.b)
        self.cp(self.VS1[0:n, tau, :, 0:64], self.VS1.b, rows[0:n, 768:1024].rearrange("p (g d) -> p g d", g=4), rows.b, eng="pool")
        self.cp(self.VW1[0:n, tau % 5, :, 0:64], self.VW1.b, winr[0:n, 256:512].rearrange("p (g d) -> p g d", g=4), winr.b, eng="pool")
        qT2 = self.qT2
        self.transpose_fm(big, n, 4, self.cT, width=128, col0=2048, dview=self.cT[:, :, 0:n])
        self.transpose_fm(self.hn, n, 4, self.kcache, width=128, col0=0, dview=self.kcache[:, :, tau * 128:tau * 128 + n])
        Aall = self.Aall
        ps = self.psum()
        for ab in range(2):
            ov = ps[:, ab * 32:(ab + 1) * 32].rearrange("p (g m) -> p g m", g=4)
            for j in range(16):
                self.mm(ov, ps.b, self.w1[:, ab, j, :], self.w1.b, self.cT[:, :, j:128:16], self.cT.b, start=(j == 0), stop=(j == 15))
        for ab in range(2):
            ov = ps[:, ab * 32:(ab + 1) * 32].rearrange("p (g m) -> p g m", g=4)
            self.ts(Aall[:, ab, :, 8 * tau:8 * tau + 8], Aall.b, ov, ps.b, self.cAB[:, ab:ab + 1], ALU.add, rd=[self.cAB.b])
        self.compress_finish(1)
        self.memset(self.maskc[:], self.maskc.b, 0.0)
        S.op("pool", lambda e: e.affine_select(out=self.maskc[:], in_=self.maskc[:], pattern=[[1, 128]], compare_op=ALU.is_ge,
                                              fill=self.fillreg(NEGM), base=128 * tau - 31, channel_multiplier=-16),
             reads=[self.maskc.b], writes=[self.maskc.b])
        pT, oacc, imp = self.pT, self.oacc, self.imp
        gv = gates[0:n, :].rearrange("p (h k) -> p h k", k=3)
        for g in range(4):
            ps = self.psum()
            pv4 = ps[:, 0:4 * n].rearrange("p (h t) -> p h t", h=4)
            self.mm(pv4, ps.b, self.kcT[:, g, :], self.kcT.b, qT2[:, 0, 4 * g:4 * g + 4, 0:n], qT2.b, start=True, stop=False)
            self.mm(pv4, ps.b, self.ident[:], self.ident.b, self.maskc[:, 0:n].unsqueeze(1).to_broadcast([128, 4, n]), self.maskc.b,
                    start=False, stop=True)
            self.act(pT[:, 0, 0:4 * n], pT.b, ps[:, 0:4 * n], ps.b, AF.Exp)
            ps2 = self.psum()
            for hh in range(4):
                self.mm(ps2[0:n, hh * 97:(hh + 1) * 97], ps2.b, pT[:, 0, hh * n:(hh + 1) * n], pT.b, self.VC1[:, g, :], self.VC1.b)
            pv = ps2[0:n, 0:388].rearrange("p (h c) -> p h c", h=4)
            self.ts(sm[0:n, 208:212], sm.b, pv[:, :, 64], ps2.b, 1e-20, ALU.max)
            self.recip(sm[0:n, 212:216], sm.b, sm[0:n, 208:212], sm.b)
            self.tt(sm[0:n, 216:220], sm.b, sm[0:n, 212:216], sm.b, gv[:, 4 * g:4 * g + 4, 0], gates.b, ALU.mult)
            self.tt(oacc[0:n, g * 256:(g + 1) * 256].rearrange("p (h d) -> p h d", h=4), oacc.b, pv[:, :, 0:64], ps2.b,
                    sm[0:n, 216:220].unsqueeze(2).to_broadcast([n, 4, 64]), sm.b, ALU.mult)
            jv = self.junk[0:n, 0:128].rearrange("p (h s) -> p h s", h=4)
            self.tt(jv, self.junk.b, pv[:, :, 65:97], ps2.b, sm[0:n, 212:216].unsqueeze(2).to_broadcast([n, 4, 32]), sm.b, ALU.mult)
            self.red(imp[0:n, g, :], imp.b, jv.rearrange("p h s -> p s h"), self.junk.b)
        FH = self.selFH
        S.op("pool", lambda e: e.memset(FH[:, 0, :], 1e9), writes=[FH.b])
        S.op("pool", lambda e: e.memset(FH[:, 1, :], 3e38), writes=[FH.b])
        for half in range(2):
            vF = FH[64 * half:64 * half + 64, 0, :]
            vH = FH[64 * half:64 * half + 64, 1, :]
            cur = 2 * tau + half
            S.op("pool", lambda e: e.affine_select(out=vF, in_=vF, pattern=[[1, 32]], compare_op=ALU.is_equal, fill=self.fillreg(0.0),
                                                  base=-cur, channel_multiplier=0), reads=[FH.b], writes=[FH.b])
            S.op("pool", lambda e: e.affine_select(out=vH, in_=vH, pattern=[[-1, 32]], compare_op=ALU.is_ge, fill=self.fillreg(-1e30),
                                                  base=cur, channel_multiplier=0), reads=[FH.b], writes=[FH.b])
        S.op("pool", lambda e: e.memset(FH[:, 0, 0:1], 1e9), reads=[FH.b], writes=[FH.b])
        self.select_topk(n, FH[0:n, 0, :], FH[0:n, 1, :], 32)
        for g in range(4):
            for kap in range(tau + 1):
                ps = self.psum()
                pv4 = ps[:, 0:4 * n].rearrange("p (h t) -> p h t", h=4)
                self.mm(pv4, ps.b, self.kcache[:, g, kap * 128:(kap + 1) * 128], self.kcache.b, qT2[:, 0, 4 * g:4 * g + 4, 0:n], qT2.b,
                        start=True, stop=False)
                self.mm(pv4, ps.b, self.Eall[:, kap, :], self.Eall.b, self.selT[:, g, 0:n].unsqueeze(1).to_broadcast([128, 4, n]),
                        self.selT.b, start=False, stop=(kap != tau))
                if kap == tau:
                    self.mm(pv4, ps.b, self.ident[:], self.ident.b, self.causal_neg[:, 0:n].unsqueeze(1).to_broadcast([128, 4, n]),
                            self.causal_neg.b, start=False, stop=True)
                self.act(pT[:, kap, 0:4 * n], pT.b, ps[:, 0:4 * n], ps.b, AF.Exp)
            self.attn_pv(g, n, range(tau + 1), self.VS1, 1, first=False)
        for g in range(4):
            k0 = max(0, tau - 4)
            for kap in range(k0, tau + 1):
                ps = self.psum()
                pv4 = ps[:, 0:4 * n].rearrange("p (h t) -> p h t", h=4)
                extra = []
                if kap == tau:
                    extra.append(self.causal_neg)
                if kap == tau - 4:
                    extra.append(self.win_neg)
                self.mm(pv4, ps.b, self.kcache[:, g, kap * 128:(kap + 1) * 128], self.kcache.b, qT2[:, 1, 4 * g:4 * g + 4, 0:n], qT2.b,
                        start=True, stop=(len(extra) == 0))
                for i, m in enumerate(extra):
                    self.mm(pv4, ps.b, self.ident[:], self.ident.b, m[:, 0:n].unsqueeze(1).to_broadcast([128, 4, n]), m.b,
                            start=False, stop=(i == len(extra) - 1))
                self.act(pT[:, kap, 0:4 * n], pT.b, ps[:, 0:4 * n], ps.b, AF.Exp)
            self.attn_pv(g, n, range(k0, tau + 1), self.VW1, 2, first=False, slot=lambda k: k % 5)
        self.cp(big[0:n, 0:1024], big.b, oacc[0:n, :], oacc.b, eng="act")
        self.transpose_fm(big, n, 8, self.bigT)
        self.linear_tm(self.bigT, n, self.W_out1, 1024, 0, 1024,
                       lambda o, b, ps, pb_: self.tt(x[0:n, o:o + b], x.b, x[0:n, o:o + b], x.b, ps, pb_, ALU.add))

    def attn_pv(self, g, n, kaps, V, gi, first, slot=lambda k: k):
        sm, pT, oacc = self.sm, self.pT, self.oacc
        gv = self.gates[0:n, :].rearrange("p (h k) -> p h k", k=3)
        kaps = list(kaps)
        ps2 = self.psum()
        for hh in range(4):
            for i, kap in enumerate(kaps):
                self.mm(ps2[0:n, hh * 65:(hh + 1) * 65], ps2.b, pT[:, kap, hh * n:(hh + 1) * n], pT.b, V[:, slot(kap), g, :], V.b,
                        start=(i == 0), stop=(i == len(kaps) - 1))
        pv = ps2[0:n, 0:260].rearrange("p (h c) -> p h c", h=4)
        self.ts(sm[0:n, 208:212], sm.b, pv[:, :, 64], ps2.b, 1e-20, ALU.max)
        self.recip(sm[0:n, 212:216], sm.b, sm[0:n, 208:212], sm.b)
        self.tt(sm[0:n, 216:220], sm.b, sm[0:n, 212:216], sm.b, gv[:, 4 * g:4 * g + 4, gi], self.gates.b, ALU.mult)
        ov = oacc[0:n, g * 256:(g + 1) * 256].rearrange("p (h d) -> p h d", h=4)
        gb = sm[0:n, 216:220].unsqueeze(2).to_broadcast([n, 4, 64])
        if first:
            self.tt(ov, oacc.b, pv[:, :, 0:64], ps2.b, gb, sm.b, ALU.mult)
        else:
            jv = self.junk[0:n, 0:256].rearrange("p (h d) -> p h d", h=4)
            self.tt(jv, self.junk.b, pv[:, :, 0:64], ps2.b, gb, sm.b, ALU.mult)
            self.tt(ov, oacc.b, ov, oacc.b, jv, self.junk.b, ALU.add)

    def compress_finish(self, nchunk):
        Aall, pre = self.Aall, self.pre
        jv = self.junk[:, 0:512].rearrange("p (g t) -> p g t", g=4)
        self.tt(jv[:, :, 0:127], self.junk.b, Aall[:, 0, :, 0:127], Aall.b, Aall[:, 1, :, 1:128], Aall.b, ALU.add)
        self.act(pre[:, :, 0:127], pre.b, jv[:, :, 0:127], self.junk.b, AF.Silu)
        ps = self.psum()
        for g in range(4):
            self.mm(ps[:, g * 64:(g + 1) * 64], ps.b, pre[0:64, g, :], pre.b, self.w2[0:64, :], self.w2.b)
        kc = self.f2[:, 512:768]
        self.cp(kc, self.f2.b, ps[:, 0:256], ps.b, eng="act")
        kcv = kc.rearrange("p (g d) -> p g d", g=4)
        self.headnorm(kcv, self.f2.b, 128, 4, self.nsak_g[:, 0, :], self.nsak_g.b, 200)
        self.rope(kcv, self.f2.b, 128, 4, self.cosC[:, 0, :], self.sinC[:, 0, :], self.cosC.b)
        self.cp(self.hn[:, 0:256], self.hn.b, kc, self.f2.b)
        self.transpose_fm(self.hn, 128, 4, self.kcT, width=64, dview=self.kcT[0:64, :, 0:128])
        ps = self.psum()
        for g in range(4):
            self.mm(ps[:, g * 64:(g + 1) * 64], ps.b, pre[64:128, g, :], pre.b, self.w2[64:128, :], self.w2.b)
        self.cp(self.VC1[:, :, 0:64], self.VC1.b, ps[:, 0:256].rearrange("p (g d) -> p g d", g=4), ps.b)

    def select_topk(self, n, F, H, nslc):
        sm, imp = self.sm, self.imp
        iv = imp[0:n, :, 0:nslc]
        self.tt(iv, imp.b, iv, imp.b, F.unsqueeze(1).to_broadcast([n, 4, nslc]), self.selFH.b, ALU.max)
        self.tt(iv, imp.b, iv, imp.b, H.unsqueeze(1).to_broadcast([n, 4, nslc]), self.selFH.b, ALU.min)
        work = self.junk
        for g in range(4):
            S = self.S
            S.op("dve", lambda e: e.max(out=sm[0:n, 224:232], in_=imp[0:n, g, 0:nslc]), reads=[imp.b], writes=[sm.b])
            S.op("dve", lambda e: e.match_replace(out=work[0:n, 0:nslc], in_to_replace=sm[0:n, 224:232],
                                                  in_values=imp[0:n, g, 0:nslc], imm_value=-3e38),
                 reads=[imp.b, sm.b], writes=[work.b])
            S.op("dve", lambda e: e.max(out=sm[0:n, 232:240], in_=work[0:n, 0:nslc]), reads=[work.b], writes=[sm.b])
            self.ts(work[0:n, 64:64 + nslc], work.b, imp[0:n, g, 0:nslc], imp.b, sm[0:n, 239:240], ALU.is_ge, rd=[sm.b])
            self.stt(work[0:n, 128:128 + nslc], work.b, imp[0:n, g, 0:nslc], imp.b, -5e29, work[0:n, 64:64 + nslc], work.b,
                     ALU.is_gt, ALU.mult)
            self.ts(self.hn[0:n, 256 + g * nslc:256 + (g + 1) * nslc], self.hn.b, work[0:n, 128:128 + nslc], work.b, -1.0, ALU.add)
        pb = self.psumb()
        for g in range(4):
            self.tr(pb[0:nslc, g * n:(g + 1) * n], pb.b, self.hn[0:n, 256 + g * nslc:256 + (g + 1) * nslc], self.hn.b)
        self.cp(self.selT[0:nslc, :, 0:n], self.selT.b, pb[0:nslc, 0:4 * n].rearrange("p (g t) -> p g t", g=4), pb.b, eng="act")


    def sample_seq(self, b):
        S = self.S
        n = TS
        self.cur_prompt = False
        ms = self.memset
        ubuf, xbuf, hs, hsb = self.ubuf, self.xbuf, self.hstate, self.hstate_bf
        with self.nc.allow_non_contiguous_dma("tiny state loads"):
            for k in range(2):
                S.dma("pool", ubuf[:, :, k], self.state_sconv[b, k].rearrange("(c p) -> p c", p=128), writes=[ubuf.b])
            for k in range(3):
                S.dma("pool", xbuf[:, :, k], self.state_ssd_conv[b, k].rearrange("(c p) -> p c", p=128), writes=[xbuf.b])
        for c in range(8):
            t = self.f2
            S.dma("pool", t[:, 0:128], self.state_ssd[b, c * 128:(c + 1) * 128, :], writes=[t.b])
            ps = self.psum()
            S.op("pe", lambda e: e.transpose(ps[:, 0:128], t[:, 0:128], self.identf[:]), reads=[t.b, self.identf.b], writes=[ps.b])
            self.cp(hs[:, c * 128:(c + 1) * 128], hs.b, ps[:, 0:128], ps.b)
        self.cp(hsb[:], hsb.b, hs[:], hs.b, eng="act")
        for l in range(2):
            for mc in range(2):
                t = self.f1
                S.dma("pool", t[:], self.cache_mem_kv[l, b, mc * 128:(mc + 1) * 128, :], writes=[t.b])
                self.cp(self.big[:, 0:512], self.big.b, t[:, 0:512], t.b, eng="act")
                pb = self.psumb()
                for h in range(4):
                    self.tr(pb[:, h * 128:(h + 1) * 128], pb.b, self.big[:, h * 128:(h + 1) * 128], self.big.b)
                self.cp(self.KT[l][:, :, mc * 128:(mc + 1) * 128], self.KT[l].b, pb[:, 0:512].rearrange("p (h t) -> p h t", h=4), pb.b)
                self.cp(self.V1[l][:, mc, :, 0:128], self.V1[l].b, t[:, 512:1024].rearrange("p (h d) -> p h d", h=4), t.b, eng="pool")
        x = self.xres
        S.dma("pool", x[0:n, :], self.x_sample[b], writes=[x.b])
        self.layer0(b, 0, n, last=True)
        self.mem_attn(0, n)
        self.ffn(0, n)
        self.layer1_sample(b)
        self.mem_attn(1, n)
        self.ffn(1, n)
        S.dma("pool", self.y_sample[b], x[0:n, :], reads=[x.b], writes=[self.b_y_sample])

    def layer1_sample(self, b):
        S = self.S
        n = TS
        x, sm = self.xres, self.sm
        rows, winr, gates, q = self.rows, self.winr, self.gates, self.f1
        qv, kv_, kw_ = self.nsa_front(n, self.cosS[0:n, 0, :], self.sinS[0:n, 0, :], self.cosS.b)
        S.dma("pool", self.s_nsa_rows[b], rows[0:n, :], reads=[rows.b], writes=[self.b_s_nsa_rows])
        S.dma("pool", self.s_nsa_win[b, 0:504, :], self.cache_nsa_win[b, 8:512, :], writes=[self.b_s_nsa_win])
        S.dma("pool", self.s_nsa_win[b, 504:512, :], winr[0:n, :], reads=[winr.b], writes=[self.b_s_nsa_win])
        self.nsa_q_stage(n, qv)
        kst = self.hn[0:n, 0:512].rearrange("p (g c d) -> p g c d", g=4, c=2)
        self.cp(kst[:, :, 0, :], self.hn.b, kv_, rows.b)
        self.cp(kst[:, :, 1, :], self.hn.b, kw_, winr.b)
        S.barrier()
        pTf = self.pT.t[:].rearrange("p a b -> p (a b)")
        o = [0]

        def carve(name, cols, shape_str=None, **kw):
            v = pTf[:, o[0]:o[0] + cols]
            o[0] += cols
            if shape_str:
                v = v.rearrange(shape_str, **kw)
            return TT(v, name)
        eT = carve("eT", 512, "p (c t) -> p c t", c=16)
        pgb = [carve("pgb%d" % i, 512, "p (g c d) -> p g c d", g=4, c=2) for i in range(2)]
        kst2 = [carve("kst2_%d" % i, 512, "p (g c d) -> p g c d", g=4, c=2) for i in range(2)]
        kTp = [carve("kTp%d" % i, 512, "p (g t) -> p g t", g=4) for i in range(2)]
        pTs = [carve("pTs%d" % i, 128) for i in range(4)]
        VC1s = carve("VC1s", 8 * 4 * 65, "p (c g d) -> p c g d", c=8, g=4)
        selTs = carve("selTs", 256, "p (c g t) -> p c g t", c=8, g=4)
        kTn = carve("kTn", 32, "p (g t) -> p g t", g=4)
        eTg = carve("eTg", 64, "p (c t) -> p c t", c=8)
        cTs = [carve("cTs%d" % i, 576, "p (g t) -> p g t", g=4) for i in range(2)]
        pre_all = TT(self.kcache.t[:].rearrange("p g t -> p (g t)")[:, 0:4096].rearrange("p (g t) -> p g t", g=4), "pre_all")
        kcTs = TT(self.VS1.t[:].rearrange("p a g d -> p (a g d)")[:, 0:4096].rearrange("p (g t) -> p g t", g=4), "kcTs")
        Vpg = [TT(self.VW1.t[:, i], "Vpg%d" % i) for i in (0, 1, 4)]
        Vn_s = TT(self.VW1.t[:, 2], "Vn_s")
        Vn_w = TT(self.VW1.t[:, 3], "Vn_w")
        xbf = self.xbuf.t[:].rearrange("p c t -> p (c t)")
        pg = [TT(self.hstate.t[:, i * 512:(i + 1) * 512], "pg%d" % i) for i in range(2)] + \
             [TT(xbf[:, i * 512:(i + 1) * 512], "pgx%d" % i) for i in range(4)]
        NPG = len(pg)
        Af = self.Aall.t[:].rearrange("p a c d -> p (a c d)")
        otm = TT(Af[0:8, 0:1024], "otm")
        cbf = self.cbm.t[:].rearrange("p a b -> p (a b)")
        abc = TT(cbf[:, 0:72].rearrange("p (a g m) -> p a g m", a=2, g=4), "abc")
        Fs = TT(cbf[0:8, 128:384], "Fs")
        acc = TT(self.rmask.t[0:32].rearrange("p a b -> p (a b)")[:, 0:260], "acc")
        imp_s = TT(self.f1.t[0:8, 0:1024].rearrange("p (g s) -> p g s", g=4), "imp_s", self.f1.b)
        idxA = TT(self.seg.t[:].rearrange("p a b -> p (a b)")[:, 0:128].bitcast(I32), "idxA")
        idxB = TT(self.seg.t[:].rearrange("p a b -> p (a b)")[:, 128:256].bitcast(I32), "idxB")
        ptf = TT(self.seg.t[:].rearrange("p a b -> p (a b)")[:, 256:384], "ptf")
        cache2 = self.cache_nsa_kv.rearrange("r (h c) -> (r h) c", h=2)
        ms = self.memset
        it = self.itmp
        S.dma("pool", it[:, 0:128], self.page_table[b:b + 1, :].partition_broadcast(128), writes=[it.b])
        self.cp(ptf[:], ptf.b, it[:, 0:128], it.b)
        self.ts(ptf[:], ptf.b, ptf[:], ptf.b, 256.0, ALU.mult, self.piota2[:, 0:1], ALU.add, rd=[self.piota2.b])
        self.cp(idxA[:], idxA.b, ptf[:], ptf.b)
        self.ts(ptf[:], ptf.b, ptf[:], ptf.b, 1.0, ALU.add)
        self.cp(idxB[:], idxB.b, ptf[:], ptf.b)
        self.transpose_fm(self.hn, n, 4, kTn, width=128, dview=kTn[:, :, 0:n])
        self.cp(Vn_s[0:n, :, 0:64], Vn_s.b, rows[0:n, 768:1024].rearrange("p (g d) -> p g d", g=4), rows.b)
        self.cp(Vn_w[0:n, :, 0:64], Vn_w.b, winr[0:n, 256:512].rearrange("p (g d) -> p g d", g=4), winr.b)
        ms(pre_all[:], pre_all.b, 0.0)
        ms(kcTs[:], kcTs.b, 0.0)
        ms(VC1s[:], VC1s.b, 1.0)
        ms(selTs[:], selTs.b, 0.0)

        def gather(dst, idx, j):
            S.dma("pool", None, None, fn=lambda e: e.indirect_dma_start(
                out=dst[:], out_offset=None, in_=cache2, in_offset=bass.IndirectOffsetOnAxis(ap=idx[:, j:j + 1], axis=0)),
                reads=[idx.b], writes=[dst.b])
        for i in range(2):
            ms(cTs[i][:], cTs[i].b, 0.0)
        cb2 = self.junk[:, 1000:1001]
        self.tt(cb2, self.junk.b, self.cAB[:, 0:1], self.cAB.b, self.cAB[:, 1:2], self.cAB.b, ALU.add)
        csum = TT(cbf[:, 100:101], "csum")
        self.cp(csum[:], csum.b, cb2, self.junk.b)
        cT3 = cTs + [TT(self.big.t[:, 0:576].rearrange("p (g t) -> p g t", g=4), "cTs2", self.big.b)]
        ms(cT3[2][:], cT3[2].b, 0.0)

        def p1_a(j):
            p_ = pg[j % NPG]
            gather(p_, idxA, j)
            pb_ = pgb[j % 2]
            self.cp(pb_[:], pb_.b, p_[:].rearrange("p (c g d) -> p g c d", c=2, g=4), p_.b, eng=("act" if j % 2 else "dve"))
            ct = cT3[j % 3]
            pbk = self.psumb()
            for g in range(4):
                self.tr(pbk[:, g * 128:(g + 1) * 128], pbk.b, pb_[:, g].rearrange("p c d -> p (c d)"), pb_.b)
            self.cp(ct[:, :, 16:144], ct.b, pbk[:, 0:512].rearrange("p (g t) -> p g t", g=4), pbk.b, eng=("dve" if j % 2 else "act"))
            nxt = cT3[(j + 1) % 3]
            self.cp(nxt[:, :, 0:16], nxt.b, ct[:, :, 128:144], ct.b, eng=("dve" if j % 2 else "act"))

        def p1_b(j):
            ct = cT3[j % 3]
            ps = self.psum()
            ov = ps[:, 0:32].rearrange("p (g m) -> p g m", g=4)
            for jj in range(16):
                self.mm(ov, ps.b, self.w1[:, 0, jj, :], self.w1.b, ct[:, :, jj:128:16], ct.b, start=(jj == 0), stop=False)
            for jj in range(16):
                self.mm(ov, ps.b, self.w1[:, 1, jj, :], self.w1.b, ct[:, :, 16 + jj:144:16], ct.b, start=False, stop=(jj == 15))
            m0 = 1 if j == 0 else 0
            self.act(pre_all[:, :, 8 * j - 1 + m0:8 * j + 7], pre_all.b, ov[:, :, m0:8], ps.b, AF.Silu, bias=csum[:, 0:1], rd=[csum.b])
        for it in range(NPAGES + 1):
            if it < NPAGES:
                p1_a(it)
            if it >= 1:
                p1_b(it - 1)
        for c in range(8):
            ps = self.psum()
            for g in range(4):
                self.mm(ps[:, g * 64:(g + 1) * 64], ps.b, pre_all[0:64, g, c * 128:(c + 1) * 128], pre_all.b, self.w2[0:64, :], self.w2.b)
            kc = self.f2[:, 512:768]
            self.cp(kc, self.f2.b, ps[:, 0:256], ps.b, eng="act")
            kcv = kc.rearrange("p (g d) -> p g d", g=4)
            self.headnorm(kcv, self.f2.b, 128, 4, self.nsak_g[:, 0, :], self.nsak_g.b, 200)
            self.rope(kcv, self.f2.b, 128, 4, self.cosC[:, c, :], self.sinC[:, c, :], self.cosC.b)
            self.cp(self.hn[:, 512:768], self.hn.b, kc, self.f2.b)
            self.transpose_fm(self.hn, 128, 4, kcTs, width=64, col0=512, dview=kcTs[0:64, :, c * 128:(c + 1) * 128])
            ps = self.psum()
            for g in range(4):
                self.mm(ps[:, g * 64:(g + 1) * 64], ps.b, pre_all[64:128, g, c * 128:(c + 1) * 128], pre_all.b, self.w2[64:128, :], self.w2.b)
            self.cp(VC1s[:, c, :, 0:64], VC1s.b, ps[:, 0:256].rearrange("p (g d) -> p g d", g=4), ps.b)
        ms(self.maskc[:], self.maskc.b, 0.0)
        S.op("pool", lambda e: e.affine_select(out=self.maskc[:, 0:8], in_=self.maskc[:, 0:8], pattern=[[1, 8]], compare_op=ALU.is_ge,
                                              fill=self.fillreg(NEGM), base=2017, channel_multiplier=-16),
             reads=[self.maskc.b], writes=[self.maskc.b])
        qT2 = self.qT2
        savedpT = self.pT
        self.pT = eT
        for g in range(4):
            for c in range(8):
                ps = self.psum()
                pv4 = ps[:, 0:4 * n].rearrange("p (h t) -> p h t", h=4)
                self.mm(pv4, ps.b, kcTs[:, g, c * 128:(c + 1) * 128], kcTs.b, qT2[:, 0, 4 * g:4 * g + 4, 0:n], qT2.b, start=True, stop=(c != 7))
                if c == 7:
                    self.mm(pv4, ps.b, self.ident[:], self.ident.b, self.maskc[:, 0:n].unsqueeze(1).to_broadcast([128, 4, n]),
                            self.maskc.b, start=False, stop=True)
                self.act(eT[:, c, 0:4 * n], eT.b, ps[:, 0:4 * n], ps.b, AF.Exp)
            self.attn_pv(g, n, range(8), VC1s, 0, first=True)
            ps3 = self.psum()
            for c in range(8):
                self.mm(ps3[:, 0:32], ps3.b, self.ones_bf[:], self.ones_bf.b, eT[:, c, 0:32], eT.b, start=(c == 0), stop=(c == 7))
            rdb = self.junk[:, 0:32]
            self.ts(rdb, self.junk.b, ps3[:, 0:32], ps3.b, 1e-20, ALU.max)
            self.recip(rdb, self.junk.b, rdb, self.junk.b)
            self.tt(eT[:, 8:16, 0:32], eT.b, eT[:, 0:8, 0:32], eT.b, rdb.unsqueeze(1).to_broadcast([128, 8, 32]), self.junk.b, ALU.mult)
            etf = self.junk[:, 64:128].rearrange("p (c t) -> p c t", c=8)
            self.red(etf, self.junk.b, eT[:, 8:16, 0:32].rearrange("p c (h t) -> p c t h", h=4), eT.b)
            self.cp(eTg[:], eTg.b, etf, self.junk.b)
            ps4 = self.psum()
            for c in range(8):
                self.mm(ps4[0:n, c * 33:(c + 1) * 33], ps4.b, eTg[:, c, :], eTg.b, self.mov33[:], self.mov33.b)
            u = ps4[0:n, 0:264].rearrange("p (c s) -> p c s", c=8)
            self.cp(imp_s[:, g, 0:256].rearrange("p (c s) -> p c s", c=8), imp_s.b, u[:, :, 0:32], ps4.b)
            self.tt(imp_s[:, g, 32:256:32], imp_s.b, imp_s[:, g, 32:256:32], imp_s.b, u[:, 0:7, 32], ps4.b, ALU.add)
        self.pT = savedpT
        ms(Fs[:], Fs.b, 0.0)
        ms(Fs[:, 0:1], Fs.b, 1e9)
        work = self.junk
        stage = self.big
        for g in range(4):
            iv = imp_s[:, g, 0:256]
            self.tt(iv, imp_s.b, iv, imp_s.b, Fs[:], Fs.b, ALU.max)
            S.op("dve", lambda e: e.max(out=sm[0:n, 224:232], in_=iv), reads=[imp_s.b], writes=[sm.b])
            S.op("dve", lambda e: e.match_replace(out=work[0:n, 0:256], in_to_replace=sm[0:n, 224:232], in_values=iv, imm_value=-3e38),
                 reads=[imp_s.b, sm.b], writes=[work.b])
            S.op("dve", lambda e: e.max(out=sm[0:n, 232:240], in_=work[0:n, 0:256]), reads=[work.b], writes=[sm.b])
            self.ts(work[0:n, 512:768], work.b, iv, imp_s.b, sm[0:n, 238:239], ALU.is_ge, rd=[sm.b])
            self.ts(stage[0:n, g * 256:(g + 1) * 256], stage.b, work[0:n, 512:768], work.b, -1.0, ALU.add)
        for g in range(4):
            pb = self.psumb()
            for c8 in range(8):
                self.tr(pb[0:32, c8 * n:(c8 + 1) * n], pb.b, stage[0:n, g * 256 + c8 * 32:g * 256 + (c8 + 1) * 32], stage.b)
            self.cp(selTs[0:32, :, g, :], selTs.b, pb[0:32, 0:8 * n].rearrange("p (c t) -> p c t", c=8), pb.b, eng="act")

        def key_tile(kT, nk, qsel, V, masks, i):
            key_tile_c(key_tile_b(kT, nk, qsel, masks, i), nk, V)

        def key_tile_b(kT, nk, qsel, masks, i):
            ps = self.psum()
            for g in range(4):
                pv4 = ps[0:nk, g * 32:(g + 1) * 32].rearrange("p (h t) -> p h t", h=4)
                self.mm(pv4, ps.b, kT[:, g, 0:nk], kT.b, qT2[:, qsel, 4 * g:4 * g + 4, 0:n], qT2.b, start=True, stop=(len(masks) == 0))
                for mi, (ml, mlb, mr, mrb, per_g) in enumerate(masks):
                    rhs = mr(g) if per_g else mr
                    self.mm(pv4, ps.b, ml, mlb, rhs, mrb, start=False, stop=(mi == len(masks) - 1))
            pt_ = pTs[i % 4]
            self.act(pt_[0:nk, :], pt_.b, ps[0:nk, 0:128], ps.b, AF.Exp)
            return pt_

        def key_tile_c(pt_, nk, V):
            ps2 = self.psum()
            for g in range(4):
                self.mm(ps2[0:32, g * 65:(g + 1) * 65], ps2.b, pt_[0:nk, g * 32:(g + 1) * 32], pt_.b, V[0:nk, g, :], V.b)
            self.tt(acc[:], acc.b, acc[:], acc.b, ps2[0:32, 0:260], ps2.b, ALU.add)

        def branch_finish(gi):
            av = acc[:].rearrange("p (g c) -> p g c", g=4)
            self.ts(sm[0:32, 208:212], sm.b, av[:, :, 64], acc.b, 1e-20, ALU.max)
            self.recip(sm[0:32, 212:216], sm.b, sm[0:32, 208:212], sm.b)
            onb = self.junk[0:32, 0:256].rearrange("p (g d) -> p g d", g=4)
            self.tt(onb, self.junk.b, av[:, :, 0:64], acc.b, sm[0:32, 212:216].unsqueeze(2).to_broadcast([32, 4, 64]), sm.b, ALU.mult)
            ov = otm[:].rearrange("p (g h d) -> p g h d", g=4, h=4)
            for hh in range(4):
                S.dma("pool", ov[:, :, hh, :], onb[hh * 8:(hh + 1) * 8], reads=[self.junk.b], writes=[otm.b])
            gv = gates[0:n, :].rearrange("p (h k) -> p h k", k=3)
            o3 = otm[:].rearrange("p (h d) -> p h d", h=16)
            self.tt(o3, otm.b, o3, otm.b, gv[:, :, gi:gi + 1].to_broadcast([n, 16, 64]), gates.b, ALU.mult)
            self.tt(self.oacc[0:n, :], self.oacc.b, self.oacc[0:n, :], self.oacc.b, otm[:], otm.b, ALU.add)

        idb, cnb = self.ident, self.causal_neg
        new_masks = [(idb[:, 0:n], idb.b, cnb[:, 0:n].unsqueeze(1).to_broadcast([128, 4, n]), cnb.b, False)]
        kst4 = kst2 + pgb
        kT4 = kTp + [TT(c.t[:, :, 0:128], c.b.name + "v", c.b) for c in cTs]
        ms(acc[:], acc.b, 0.0)
        pts = {}

        def stage_a(kap):
            p_ = pg[kap % NPG]
            gather(p_, idxB, kap)
            ks = kst4[kap % 4]
            self.cp(ks[:], ks.b, p_[:].rearrange("p (c g d) -> p g c d", c=2, g=4), p_.b, eng=("act" if kap % 2 else "dve"))
            V = Vpg[kap % 3]
            self.cp(V[:, :, 0:64], V.b, p_[:, 256:512].rearrange("p (g d) -> p g d", g=4), p_.b, eng=("dve" if kap % 2 else "act"))
            kT = kT4[kap % 4]
            pb = self.psumb()
            for g in range(4):
                self.tr(pb[:, g * 128:(g + 1) * 128], pb.b, ks[:, g].rearrange("p c d -> p (c d)"), ks.b)
            self.cp(kT[:], kT.b, pb[:, 0:512].rearrange("p (g t) -> p g t", g=4), pb.b, eng=("dve" if kap % 2 else "act"))

        def stage_b(kap):
            msk = [(self.Eall[:, kap % 16, :], self.Eall.b,
                    (lambda g, kap=kap: selTs[:, kap // 16, g, :].unsqueeze(1).to_broadcast([128, 4, n])), selTs.b, True)]
            pts[kap] = key_tile_b(kT4[kap % 4], 128, 0, msk, kap)

        def stage_c(kap):
            key_tile_c(pts.pop(kap), 128, Vpg[kap % 3])
        for it in range(NPAGES + 2):
            if it < NPAGES:
                stage_a(it)
            if 0 <= it - 1 < NPAGES:
                stage_b(it - 1)
            if 0 <= it - 2 < NPAGES:
                stage_c(it - 2)
        key_tile(kTn, n, 0, Vn_s, new_masks, 0)
        branch_finish(1)
        ms(acc[:], acc.b, 0.0)
        wnb = self.win_neg
        for w in range(4):
            p_ = pg[w % 2]
            S.dma("pool", p_[:], self.cache_nsa_win[b, w * 128:(w + 1) * 128, :], writes=[p_.b])
            ks = kst2[w % 2]
            self.cp(ks[:], ks.b, p_[:].rearrange("p (c g d) -> p g c d", c=2, g=4), p_.b)
            V = Vpg[w % 2]
            self.cp(V[:, :, 0:64], V.b, p_[:, 256:512].rearrange("p (g d) -> p g d", g=4), p_.b, eng="pool")
            kT = kTp[w % 2]
            pb = self.psumb()
            for g in range(4):
                self.tr(pb[:, g * 128:(g + 1) * 128], pb.b, ks[:, g].rearrange("p c d -> p (c d)"), ks.b)
            self.cp(kT[:], kT.b, pb[:, 0:512].rearrange("p (g t) -> p g t", g=4), pb.b, eng="act")
            msk = []
            if w == 0:
                msk = [(idb[:], idb.b, wnb[:, 0:n].unsqueeze(1).to_broadcast([128, 4, n]), wnb.b, False)]
            key_tile(kT, 128, 0, V, msk, w)
        key_tile(kTn, n, 1, Vn_w, new_masks, 0)
        branch_finish(2)
        S.barrier()
        big = self.big
        self.cp(big[0:n, 0:1024], big.b, self.oacc[0:n, :], self.oacc.b, eng="act")
        self.transpose_fm(big, n, 8, self.bigT)
        self.linear_tm(self.bigT, n, self.W_out1, 1024, 0, 1024,
                       lambda o_, b_, ps, pb_: self.tt(x[0:n, o_:o_ + b_], x.b, x[0:n, o_:o_ + b_], x.b, ps, pb_, ALU.add))


def build_program(**kw):
    p = Prog(**kw)
    p.cur_prompt = True
    p.cl_b = None
    return p


_NAMES = ["y_prompt", "y_sample", "p_sconv", "p_ssd_conv", "p_ssd", "p_nsa_rows", "p_nsa_win", "p_mem_kv",
          "s_sconv", "s_ssd_conv", "s_ssd", "s_nsa_rows", "s_nsa_win"]


def shard_inputs(inp, c):
    f = np.ascontiguousarray
    m = {}
    m["x_prompt"] = f(inp["x_prompt"][NPB * c:NPB * (c + 1)])
    m["x_sample"] = f(inp["x_sample"][NSB * c:NSB * (c + 1)])
    m["mem_prompt"] = f(inp["mem_prompt"][NPB * c:NPB * (c + 1)])
    m["state_sconv"] = f(inp["state_sconv"][0, NSB * c:NSB * (c + 1)])
    m["state_ssd_conv"] = f(inp["state_ssd_conv"][0, NSB * c:NSB * (c + 1)])
    m["state_ssd"] = f(inp["state_ssd"][0, NSB * c:NSB * (c + 1)]).reshape(NSB, 1024, 128)
    m["cache_nsa_kv"] = inp["cache_nsa_kv"].reshape(-1, 1024)
    m["cache_nsa_win"] = f(inp["cache_nsa_win"][0, NSB * c:NSB * (c + 1)]).reshape(NSB, 512, 512)
    m["cache_mem_kv"] = f(inp["cache_mem_kv"][:, NSB * c:NSB * (c + 1)]).reshape(2, NSB, 256, 1024)
    m["page_table"] = f(inp["page_table"][NSB * c:NSB * (c + 1)])
    for k in ["norm_mix", "norm_mem", "norm_memsrc", "norm_ffn", "ab_ssd_conv_b", "ab_dt_bias", "ab_a_log", "ab_d",
              "ab_ssd_norm", "nsa_q_norm", "mem_w_q", "mem_q_norm", "mem_w_kv", "mem_k_norm", "mem_w_o",
              "ffn_w_gate", "ffn_w_up", "ffn_w_down"]:
        m[k] = f(inp[k])
    for k in ["ab_w_in", "ab_sconv_w", "ab_ssd_conv_w", "ab_w_out", "nsa_w_in", "nsa_k_norm", "nsa_cmp_pe", "nsa_cmp_w1",
              "nsa_cmp_w2", "nsa_w_out"]:
        m[k] = f(inp[k][0])
    return m


def gather_outputs(results):
    cat = lambda name, ax: np.concatenate([r[name] for r in results], axis=ax)
    y_prompt = cat("y_prompt", 0)
    y_sample = cat("y_sample", 0)
    p_sconv = cat("p_sconv", 0)[None]
    p_ssd_conv = cat("p_ssd_conv", 0)[None]
    p_ssd = cat("p_ssd", 0).reshape(1, 16, 16, 64, 128)
    p_rows = cat("p_nsa_rows", 0).reshape(16, SEQ, 1, 4, 4, 64)
    p_win = cat("p_nsa_win", 0).reshape(1, 16, 512, 2, 4, 64)
    p_mem = cat("p_mem_kv", 1).reshape(2, 16, 256, 2, 4, 128)
    s_sconv = cat("s_sconv", 0)[None]
    s_ssd_conv = cat("s_ssd_conv", 0)[None]
    s_ssd = cat("s_ssd", 0).reshape(1, 32, 16, 64, 128)
    s_rows = cat("s_nsa_rows", 0).reshape(32, TS, 1, 4, 4, 64)
    s_win = cat("s_nsa_win", 0).reshape(1, 32, 512, 2, 4, 64)
    return (y_prompt, y_sample, p_sconv, p_ssd_conv, p_ssd, p_rows, p_win, p_mem,
            s_sconv, s_ssd_conv, s_ssd, s_rows, s_win)


def kernel(**inputs):
    inp = {k: np.asarray(v) for k, v in inputs.items()}
    p = build_program()
    nc = p.build()
    in_maps = [shard_inputs(inp, c) for c in range(NCORES)]
    res = run_bass_kernel_spmd(nc, in_maps, core_ids=list(range(NCORES)))
    return gather_outputs(res.results)
```

```python
import math
from contextlib import ExitStack

import numpy as np
import concourse.bass as bass
import concourse.mybir as mybir
from concourse.bass_utils import run_bass_kernel_spmd

F32 = mybir.dt.float32
BF16 = mybir.dt.bfloat16
I32 = mybir.dt.int32
ALU = mybir.AluOpType
AF = mybir.ActivationFunctionType
AX = mybir.AxisListType

NCORES = 8
D = 1024
SEQ = 2048
NPB = 2
NSB = 4
TS = 8
PAST = 16384
NPAGES = 128
EPS = 1e-6
IN0 = 6160
IN1 = 2608
DFF = 2816
NEGM = -30000.0
THETA = 500000.0


class Buf:
    __slots__ = ("name", "w", "r", "excl")

    def __init__(self, name, excl=False):
        self.name = name
        self.w = None
        self.r = {}
        self.excl = excl


class TT:
    def __init__(self, t, name, b=None):
        self.t = t
        self.b = b if b is not None else Buf(name)

    def __getitem__(self, k):
        return self.t[k]


class Sched:
    def __init__(self, nc, es, same_engine_sync=True, n_dma_sems=10):
        self.nc = nc
        self.engs = {"pe": nc.tensor, "act": nc.scalar, "dve": nc.vector, "pool": nc.gpsimd, "sp": nc.sync}
        self.sem = {}
        self.cnt = {}
        for k in self.engs:
            self.sem[k] = es.enter_context(nc.semaphore("s_" + k))
            self.cnt[k] = 0
        self.dma_sems = {}
        self.dma_rr = {}
        for q in ("sp", "pool", "act"):
            self.dma_sems[q] = []
            for i in range(n_dma_sems):
                key = "d_%s%d" % (q, i)
                self.sem[key] = es.enter_context(nc.semaphore(key))
                self.cnt[key] = 0
                self.dma_sems[q].append(key)
            self.dma_rr[q] = 0
        self.waited = {k: {} for k in self.engs}
        self.same = same_engine_sync
        self.ninstr = 0

    def _wait(self, e, deps):
        eng = self.engs[e]
        for k, v in deps.items():
            if v <= 0:
                continue
            if k == e and (not self.same or e == "pe"):
                continue
            if self.waited[e].get(k, 0) >= v:
                continue
            eng.wait_ge(self.sem[k], v)
            self.waited[e][k] = v

    @staticmethod
    def _deps(reads, writes):
        deps = {}
        for b in reads:
            if b.w is not None and deps.get(b.w[0], 0) < b.w[1]:
                deps[b.w[0]] = b.w[1]
        for b in writes:
            if b.w is not None and deps.get(b.w[0], 0) < b.w[1]:
                deps[b.w[0]] = b.w[1]
            for k, v in b.r.items():
                if deps.get(k, 0) < v:
                    deps[k] = v
        return deps

    def op(self, e, fn, reads=(), writes=()):
        if any(b.excl for b in reads):
            writes = list(writes) + [b for b in reads if b.excl and b not in writes]
            reads = [b for b in reads if not b.excl]
        deps = self._deps(reads, writes)
        self._wait(e, deps)
        ins = fn(self.engs[e])
        self.cnt[e] += 1
        ins.then_inc(self.sem[e], 1)
        v = self.cnt[e]
        for b in reads:
            if b.r.get(e, 0) < v:
                b.r[e] = v
        for b in writes:
            b.w = (e, v)
            b.r = {}
        self.ninstr += 1
        return ins

    def dma(self, q, out, in_, reads=(), writes=(), fn=None, **kw):
        deps = self._deps(reads, writes)
        key = self.dma_sems[q][self.dma_rr[q] % len(self.dma_sems[q])]
        self.dma_rr[q] += 1
        if self.cnt[key] > 0:
            deps[key] = max(deps.get(key, 0), self.cnt[key])
        self._wait(q, deps)
        if fn is not None:
            ins = fn(self.engs[q])
        else:
            ins = self.engs[q].dma_start(out=out, in_=in_, **kw)
        self.cnt[key] += 16
        ins.then_inc(self.sem[key], 16)
        v = self.cnt[key]
        for b in reads:
            b.r[key] = v
        for b in writes:
            b.w = (key, v)
            b.r = {}
        self.ninstr += 1
        return ins

    def barrier(self):
        snap = {k: v for k, v in self.cnt.items() if v > 0}
        for e in self.engs:
            self._wait(e, dict(snap))

    def finish(self, bufs, e="sp"):
        self._wait(e, self._deps(bufs, ()))


class Prog:
    def __init__(self, do_sample=True, n_ptiles=16, debug=False, pool_rows=5120 * 128, stage=9, do_prompt=True, n_sample=NSB):
        self.pool_rows = pool_rows
        self.stage = stage
        self.do_prompt = do_prompt
        self.n_sample = n_sample
        self.same_sync = True
        self.do_sample = do_sample
        self.n_ptiles = n_ptiles
        self.debug = debug
        self.nc = bass.Bass("TRN2", target_bir_lowering=False)
        self.es = ExitStack()
        self.out_bufs = []

    def sb(self, name, shape, dt=F32):
        return TT(self.es.enter_context(self.nc.sbuf_tensor(name, list(shape), dt)), name)

    def din(self, name, shape, dt=F32):
        return self.nc.dram_tensor(name, list(shape), dt, kind="ExternalInput").ap()

    def dout(self, name, shape, dt=F32):
        ap = self.nc.dram_tensor(name, list(shape), dt, kind="ExternalOutput").ap()
        b = Buf(name)
        self.out_bufs.append(b)
        return ap, b

    def dscratch(self, name, shape, dt):
        return TT(self.nc.dram_tensor(name, list(shape), dt, kind="Internal").ap(), name)

    def fillreg(self, v):
        if not hasattr(self, "_fillregs"):
            self._fillregs = {}
        if v not in self._fillregs:
            self._fillregs[v] = self.nc.gpsimd.to_reg(float(v))
        return self._fillregs[v]

    def psum(self):
        i = self.ps_rr % len(self.psF)
        self.ps_rr += 1
        return self.psF[i]

    def psumb(self):
        i = self.psb_rr % len(self.psB)
        self.psb_rr += 1
        return self.psB[i]

    def mm(self, out, ob, lhsT, lb, rhs, rb, start=True, stop=True):
        self.S.op("pe", lambda e: e.matmul(out, lhsT, rhs, start=start, stop=stop), reads=[lb, rb], writes=[ob])

    def tr(self, out, ob, in_, ib, ident=None):
        idt = ident if ident is not None else self.ident
        n = in_.shape[0]
        self.S.op("pe", lambda e: e.transpose(out, in_, idt[0:n, 0:n]), reads=[ib, idt.b], writes=[ob])

    def act(self, out, ob, in_, ib, func, bias=None, scale=None, accum=None, rd=(), wr=()):
        kw = {}
        if bias is not None:
            kw["bias"] = bias
        if scale is not None:
            kw["scale"] = scale
        if accum is not None:
            kw["accum_out"] = accum
        self.S.op("act", lambda e: e.activation(out=out, in_=in_, func=func, **kw), reads=[ib] + list(rd),
                  writes=[ob] + list(wr))

    def tt(self, out, ob, in0, b0, in1, b1, op, eng="dve"):
        self.S.op(eng, lambda e: e.tensor_tensor(out=out, in0=in0, in1=in1, op=op), reads=[b0, b1], writes=[ob])

    def ts(self, out, ob, in0, b0, s1, op0, s2=None, op1=None, rd=(), eng="dve", accum=None):
        kw = {}
        if op1 is not None:
            kw["op1"] = op1
        if accum is not None:
            kw["accum_out"] = accum
        self.S.op(eng, lambda e: e.tensor_scalar(out=out, in0=in0, scalar1=s1, scalar2=s2, op0=op0, **kw),
                  reads=[b0] + list(rd), writes=[ob])

    def stt(self, out, ob, in0, b0, sc, in1, b1, op0, op1, rd=()):
        self.S.op("dve", lambda e: e.scalar_tensor_tensor(out=out, in0=in0, scalar=sc, in1=in1, op0=op0, op1=op1),
                  reads=[b0, b1] + list(rd), writes=[ob])

    def cp(self, out, ob, in_, ib, eng="dve"):
        if eng == "act":
            self.S.op("act", lambda e: e.copy(out, in_), reads=[ib], writes=[ob])
        else:
            self.S.op(eng, lambda e: e.tensor_copy(out, in_), reads=[ib], writes=[ob])

    def memset(self, ap, b, val, eng="pool"):
        self.S.op(eng, lambda e: e.memset(ap, val), writes=[b])

    def red(self, out, ob, in_, ib, op=ALU.add, eng="dve"):
        self.S.op(eng, lambda e: e.tensor_reduce(out=out, in_=in_, axis=AX.X, op=op), reads=[ib], writes=[ob])

    def recip(self, out, ob, in_, ib):
        self.S.op("dve", lambda e: e.reciprocal(out, in_), reads=[ib], writes=[ob])

    def rstd(self, out, ob, ss, sb_, n, inv_n):
        self.act(out, ob, ss, sb_, AF.Ln, scale=inv_n, bias=self.epsc[0:out.shape[0], 0:1], rd=[self.epsc.b])
        self.act(out, ob, out, ob, AF.Exp, scale=-0.5)

    def wblock(self, Wd, K, c0, ncols):
        i = self.w_rr % len(self.wbufs)
        self.w_rr += 1
        wb = self.wbufs[i]
        kc = K // 128
        src = Wd.t[:, c0:c0 + ncols].rearrange("(c p) n -> p c n", p=128)
        view = wb[:, 0:kc * ncols].rearrange("p (c n) -> p c n", c=kc)
        self.S.dma("sp", view, src, reads=[Wd.b], writes=[wb.b])
        return view, wb.b

    def linear_tm(self, xT, n, Wd, K, c0, ncols_total, consume):
        kc = K // 128
        off = 0
        while off < ncols_total:
            maxc = min(512, (5632 // kc) // 128 * 128)
            nb = min(maxc, ncols_total - off)
            wv, wbb = self.wblock(Wd, K, c0 + off, nb)
            ps = self.psum()
            for k in range(kc):
                self.mm(ps[0:n, 0:nb], ps.b, xT[:, k, 0:n], xT.b, wv[:, k, 0:nb], wbb, start=(k == 0), stop=(k == kc - 1))
            consume(off, nb, ps[0:n, 0:nb], ps.b)
            off += nb

    def linear_fm(self, xT, n, Wd, K, c0, nchunks, consume):
        kc = K // 128
        ch = 0
        while ch < nchunks:
            nch = min(4, nchunks - ch)
            wv, wbb = self.wblock(Wd, K, c0 + ch * 128, nch * 128)
            ps = self.psum()
            for j in range(nch):
                for k in range(kc):
                    self.mm(ps[:, j * n:(j + 1) * n], ps.b, wv[:, k, j * 128:(j + 1) * 128], wbb, xT[:, k, 0:n], xT.b,
                            start=(k == 0), stop=(k == kc - 1))
            consume(ch, nch, ps[:, 0:nch * n].rearrange("p (c t) -> p c t", c=nch), ps.b)
            ch += nch

    def rmsnorm_T(self, x, n, gcol, hT):
        junk, ss, hn = self.junk, self.ss, self.hn
        self.act(junk[0:n, :], junk.b, x[0:n, :], x.b, AF.Square, accum=ss[0:n, 0:1], wr=[ss.b])
        self.rstd(ss[0:n, 1:2], ss.b, ss[0:n, 0:1], ss.b, n, 1.0 / D)
        self.ts(hn[0:n, :], hn.b, x[0:n, :], x.b, ss[0:n, 1:2], ALU.mult, rd=[ss.b])
        self.transpose_fm(hn, n, 8, hT, gcol)

    def transpose_fm(self, src, n, nchunks, dst, gcol=None, width=128, col0=0, dview=None):
        dv = dview if dview is not None else dst[0:width, 0:nchunks, 0:n]
        per = min(8, 1024 // n)
        c = 0
        while c < nchunks:
            m = min(per, nchunks - c)
            pb = self.psumb()
            for j in range(m):
                a = col0 + (c + j) * width
                self.tr(pb[0:width, j * n:(j + 1) * n], pb.b, src[0:n, a:a + width], src.b)
            pv = pb[0:width, 0:m * n].rearrange("p (c t) -> p c t", c=m)
            if gcol is None:
                self.cp(dv[:, c:c + m, :], dst.b, pv, pb.b, eng="act")
            else:
                g1 = gcol[0:width, c:c + m].unsqueeze(2).to_broadcast([width, m, n])
                self.tt(dv[:, c:c + m, :], dst.b, pv, pb.b, g1, gcol.b, ALU.mult)
            c += m

    def build(self):
        nc = self.nc
        es = self.es
        with es:
            self.S = Sched(nc, es, same_engine_sync=self.same_sync)
            self.declare_io()
            self.alloc()
            st = self.stage
            self.setup_consts()
            if st >= 1:
                self.convert_weights()
            if st >= 2 and self.do_prompt:
                self.cur_prompt = True
                for s in range(NPB if st >= 9 else 1):
                    self.prompt_seq(s)
            if self.do_sample:
                for b in range(self.n_sample):
                    self.sample_seq(b)
            self.S.finish([b for b in self.out_bufs if b.w is not None])
        return nc

    def declare_io(self):
        d = self.din
        self.x_prompt = d("x_prompt", [NPB, SEQ, D])
        self.x_sample = d("x_sample", [NSB, TS, D])
        self.mem_prompt = d("mem_prompt", [NPB, 256, D])
        self.state_sconv = d("state_sconv", [NSB, 2, 1024])
        self.state_ssd_conv = d("state_ssd_conv", [NSB, 3, 2048])
        self.state_ssd = d("state_ssd", [NSB, 1024, 128])
        self.cache_nsa_kv = d("cache_nsa_kv", [self.pool_rows, 1024])
        self.cache_nsa_win = d("cache_nsa_win", [NSB, 512, 512])
        self.cache_mem_kv = d("cache_mem_kv", [2, NSB, 256, 1024])
        self.page_table = d("page_table", [NSB, 128], I32)
        self.norm_mix = d("norm_mix", [2, D])
        self.norm_mem = d("norm_mem", [2, D])
        self.norm_memsrc = d("norm_memsrc", [2, D])
        self.norm_ffn = d("norm_ffn", [2, D])
        self.ab_w_in = d("ab_w_in", [D, IN0])
        self.ab_sconv_w = d("ab_sconv_w", [3, 1024])
        self.ab_ssd_conv_w = d("ab_ssd_conv_w", [4, 2048])
        self.ab_ssd_conv_b = d("ab_ssd_conv_b", [1, 2048])
        self.ab_dt_bias = d("ab_dt_bias", [1, 16])
        self.ab_a_log = d("ab_a_log", [1, 16])
        self.ab_d = d("ab_d", [1, 16])
        self.ab_ssd_norm = d("ab_ssd_norm", [1, 1024])
        self.ab_w_out = d("ab_w_out", [2048, D])
        self.nsa_w_in = d("nsa_w_in", [D, IN1])
        self.nsa_q_norm = d("nsa_q_norm", [1, 64])
        self.nsa_k_norm = d("nsa_k_norm", [3, 64])
        self.nsa_cmp_pe = d("nsa_cmp_pe", [2, 32, 64])
        self.nsa_cmp_w1 = d("nsa_cmp_w1", [2, 32, 64, 64])
        self.nsa_cmp_w2 = d("nsa_cmp_w2", [2, 64, 64])
        self.nsa_w_out = d("nsa_w_out", [D, D])
        self.mem_w_q = d("mem_w_q", [2, D, 512])
        self.mem_q_norm = d("mem_q_norm", [2, 128])
        self.mem_w_kv = d("mem_w_kv", [2, D, 1024])
        self.mem_k_norm = d("mem_k_norm", [2, 128])
        self.mem_w_o = d("mem_w_o", [2, 512, D])
        self.ffn_w_gate = d("ffn_w_gate", [2, D, DFF])
        self.ffn_w_up = d("ffn_w_up", [2, D, DFF])
        self.ffn_w_down = d("ffn_w_down", [2, DFF, D])
        o = self.dout
        self.y_prompt, self.b_y_prompt = o("y_prompt", [NPB, SEQ, D])
        self.y_sample, self.b_y_sample = o("y_sample", [NSB, TS, D])
        self.p_sconv, self.b_p_sconv = o("p_sconv", [NPB, 2, 1024])
        self.p_ssd_conv, self.b_p_ssd_conv = o("p_ssd_conv", [NPB, 3, 2048])
        self.p_ssd, self.b_p_ssd = o("p_ssd", [NPB, 1024, 128])
        self.p_nsa_rows, self.b_p_nsa_rows = o("p_nsa_rows", [NPB, SEQ, 1024])
        self.p_nsa_win, self.b_p_nsa_win = o("p_nsa_win", [NPB, 512, 512])
        self.p_mem_kv, self.b_p_mem_kv = o("p_mem_kv", [2, NPB, 256, 1024])
        self.s_sconv, self.b_s_sconv = o("s_sconv", [NSB, 2, 1024])
        self.s_ssd_conv, self.b_s_ssd_conv = o("s_ssd_conv", [NSB, 3, 2048])
        self.s_ssd, self.b_s_ssd = o("s_ssd", [NSB, 1024, 128])
        self.s_nsa_rows, self.b_s_nsa_rows = o("s_nsa_rows", [NSB, TS, 1024])
        self.s_nsa_win, self.b_s_nsa_win = o("s_nsa_win", [NSB, 512, 512])
        ds = self.dscratch
        self.W_in0 = ds("W_in0", [D, IN0], BF16)
        self.W_out0 = ds("W_out0", [2048, D], BF16)
        self.W_in1 = ds("W_in1", [D, IN1], BF16)
        self.W_out1 = ds("W_out1", [D, D], BF16)
        self.W_q = [ds("W_q%d" % l, [D, 512], BF16) for l in range(2)]
        self.W_kv = [ds("W_kv%d" % l, [D, 1024], BF16) for l in range(2)]
        self.W_o = [ds("W_o%d" % l, [512, D], BF16) for l in range(2)]
        self.W_g = [ds("W_g%d" % l, [D, DFF], BF16) for l in range(2)]
        self.W_u = [ds("W_u%d" % l, [D, DFF], BF16) for l in range(2)]
        self.W_d = [ds("W_d%d" % l, [DFF, D], BF16) for l in range(2)]

    def convert_weights(self):
        S = self.S
        pairs = [(self.W_in0, self.ab_w_in), (self.W_out0, self.ab_w_out), (self.W_in1, self.nsa_w_in),
                 (self.W_out1, self.nsa_w_out)]
        for l in range(2):
            pairs += [(self.W_q[l], self.mem_w_q[l]), (self.W_kv[l], self.mem_w_kv[l]), (self.W_o[l], self.mem_w_o[l]),
                      (self.W_g[l], self.ffn_w_gate[l]), (self.W_u[l], self.ffn_w_up[l]), (self.W_d[l], self.ffn_w_down[l])]
        for dst, src in pairs:
            K, N = src.shape[0], src.shape[1]
            for r in range(0, K, 256):
                r1 = min(K, r + 256)
                for c in range(0, N, 2048):
                    c1 = min(N, c + 2048)
                    S.dma("pool", dst.t[r:r1, c:c1], src[r:r1, c:c1], writes=[dst.b])

    def alloc(self):
        nc, es, sb = self.nc, self.es, self.sb
        self.psF = [TT(es.enter_context(nc.psum_tensor("psF%d" % i, [128, 512], F32)), "psF%d" % i, Buf("psF%d" % i, True))
                    for i in range(6)]
        self.psB = [TT(es.enter_context(nc.psum_tensor("psB%d" % i, [128, 1024], BF16)), "psB%d" % i, Buf("psB%d" % i, True))
                    for i in range(2)]
        self.ps_rr = 0
        self.psb_rr = 0
        self.wbufs = [sb("wbuf%d" % i, [128, 5632], BF16) for i in range(3)]
        self.w_rr = 0
        self.ident = sb("ident", [128, 128], BF16)
        self.identf = sb("identf", [128, 128], F32)
        self.U = sb("U", [128, 128], F32)
        self.onesf = sb("onesf", [128, 128], F32)
        self.causal_neg = sb("causal_neg", [128, 128], BF16)
        self.win_neg = sb("win_neg", [128, 128], BF16)
        self.maskc = sb("maskc", [128, 128], BF16)
        self.Eall = sb("Eall", [128, 16, 128], BF16)
        self.mov = sb("mov", [128, 33], F32)
        self.itmp = sb("itmp", [128, 256], I32)
        self.piota2 = sb("piota2", [128, 1], F32)
        self.ones_bf = sb("ones_bf", [128, 128], BF16)
        self.mov33 = sb("mov33", [128, 33], BF16)
        self.epsc = sb("epsc", [128, 1], F32)
        self.g_mix = [sb("g_mix%d" % l, [128, 8]) for l in range(2)]
        self.g_mem = [sb("g_mem%d" % l, [128, 8]) for l in range(2)]
        self.g_src = [sb("g_src%d" % l, [128, 8]) for l in range(2)]
        self.g_ffn = [sb("g_ffn%d" % l, [128, 8]) for l in range(2)]
        self.g_ssdn = sb("g_ssdn", [128, 8])
        self.sconv_w = sb("sconv_w", [128, 8, 3])
        self.ssdc_w = sb("ssdc_w", [128, 16, 4])
        self.ssdc_b = sb("ssdc_b", [128, 16])
        self.dt_bias = sb("dt_bias", [128, 16])
        self.Aneg = sb("Aneg", [128, 16])
        self.Dsk = sb("Dsk", [128, 16])
        self.memq_g = [sb("memq_g%d" % l, [128, 1]) for l in range(2)]
        self.memk_g = [sb("memk_g%d" % l, [128, 128]) for l in range(2)]
        self.memk_gc = [sb("memk_gc%d" % l, [128, 1]) for l in range(2)]
        self.nsaq_g = sb("nsaq_g", [128, 64])
        self.nsak_g = sb("nsak_g", [128, 3, 64])
        self.w1 = sb("w1", [128, 2, 16, 128], BF16)
        self.w2 = sb("w2", [128, 64], BF16)
        self.peT = sb("peT", [128, 2, 16], BF16)
        self.cAB = sb("cAB", [128, 2])
        self.cosT = sb("cosT", [128, 16, 8])
        self.sinT = sb("sinT", [128, 16, 8])
        self.cosC = sb("cosC", [128, 8, 8])
        self.sinC = sb("sinC", [128, 8, 8])
        self.cosS = sb("cosS", [8, 1, 8])
        self.sinS = sb("sinS", [8, 1, 8])
        self.selFH = sb("selFH", [128, 2, 32])
        self.xres2 = sb("xres2", [128, 1024])
        self.xres = sb("xres", [128, 1024])
        self.junk = sb("junk", [128, 1024])
        self.ss = sb("ss", [128, 2])
        self.hn = sb("hn", [128, 1024], BF16)
        self.hT = sb("hT", [128, 8, 128], BF16)
        self.big = sb("big", [128, 2816], BF16)
        self.bigT = sb("bigT", [128, 22, 128], BF16)
        self.f1 = sb("f1", [128, 1024])
        self.f2 = sb("f2", [128, 1024])
        self.f3 = sb("f3", [128, 1024])
        self.sm = sb("sm", [128, 256])
        self.ubuf = sb("ubuf", [128, 8, 130])
        self.xbuf = sb("xbuf", [128, 16, 131])
        self.xcs = sb("xcs", [128, 16, 128], BF16)
        self.rmask = sb("rmask", [128, 4, 128])
        self.seg = sb("seg", [128, 4, 128])
        self.cbm = sb("cbm", [128, 4, 128])
        self.xdt = sb("xdt", [128, 1024], BF16)
        self.xdte = sb("xdte", [128, 1024], BF16)
        self.hstate = sb("hstate", [128, 1024])
        self.hstate_bf = sb("hstate_bf", [128, 1024], BF16)
        self.KT = [sb("KT%d" % l, [128, 4, 256], BF16) for l in range(2)]
        self.V1 = [sb("V1%d" % l, [128, 2, 4, 129], BF16) for l in range(2)]
        self.qT = sb("qT", [128, 4, 128], BF16)
        self.qT2 = sb("qT2", [128, 2, 16, 128], BF16)
        self.pT = sb("pT", [128, 16, 512], BF16)
        self.kcache = sb("kcache", [128, 4, 2048], BF16)
        self.VS1 = sb("VS1", [128, 16, 4, 65], BF16)
        self.VW1 = sb("VW1", [128, 5, 4, 65], BF16)
        self.Aall = sb("Aall", [128, 2, 4, 129])
        self.pre = sb("pre", [128, 4, 128], BF16)
        self.kcT = sb("kcT", [128, 4, 128], BF16)
        self.VC1 = sb("VC1", [128, 4, 97], BF16)
        self.cT = sb("cT", [128, 4, 128], BF16)
        self.selT = sb("selT", [128, 4, 128], BF16)
        self.rows = sb("rows", [128, 1024])
        self.winr = sb("winr", [128, 512])
        self.gates = sb("gates", [128, 48])
        self.imp = sb("imp", [128, 4, 32])
        self.xr = [self.xres, self.xres2]
        self.s_h = TT(self.f1[:, 0:1024].rearrange("p (c t) -> p c t", c=8), "s_h", self.f1.b)
        self.cacc = TT(self.f3[:, 0:1024].rearrange("p (c t) -> p c t", c=8), "cacc", self.f3.b)
        self.x_tm = TT(self.big[:, 1024:2048], "x_tm", self.big.b)
        self.B_tm = TT(self.big[:, 2048:2560], "B_tm", self.big.b)
        self.Mh = TT(self.pT[:, 0:4, :].rearrange("p a (b t) -> p (a b) t", b=4), "Mh", self.pT.b)
        self.oacc = TT(self.f3[:, :], "oacc", self.f3.b)

    def setup_consts(self):
        S = self.S
        P = self
        ms = self.memset

        def asel(t, pattern, op, fill, base, cm, rows=128, view=None):
            v = view if view is not None else t[0:rows]
            S.op("pool", lambda e: e.affine_select(out=v, in_=v, pattern=pattern, compare_op=op, fill=self.fillreg(fill),
                                                  base=base, channel_multiplier=cm), reads=[t.b], writes=[t.b])
        ms(self.epsc[:], self.epsc.b, EPS)
        ms(self.ident[:], self.ident.b, 1.0)
        asel(self.ident, [[-1, 128]], ALU.is_equal, 0.0, 0, 1)
        ms(self.identf[:], self.identf.b, 1.0)
        asel(self.identf, [[-1, 128]], ALU.is_equal, 0.0, 0, 1)
        ms(self.onesf[:], self.onesf.b, 1.0)
        ms(self.U[:], self.U.b, 1.0)
        asel(self.U, [[1, 128]], ALU.is_ge, 0.0, 0, -1)
        ms(self.causal_neg[:], self.causal_neg.b, 0.0)
        asel(self.causal_neg, [[1, 128]], ALU.is_ge, NEGM, 0, -1)
        ms(self.win_neg[:], self.win_neg.b, 0.0)
        asel(self.win_neg, [[-1, 128]], ALU.is_gt, NEGM, 0, 1)
        ms(self.Eall[:], self.Eall.b, -NEGM)
        asel(self.Eall, [[128, 16], [1, 128]], ALU.is_ge, 0.0, 0, -64)
        asel(self.Eall, [[-128, 16], [-1, 128]], ALU.is_ge, 0.0, 63, 64)
        ms(self.qT2[:], self.qT2.b, 0.0)
        ms(self.kcT[:], self.kcT.b, 0.0)
        ms(self.selT[:], self.selT.b, 0.0)
        def col(t, src_row):
            S.dma("pool", t[:], src_row.rearrange("(c p) -> p c", p=128), writes=[t.b])
        nc = self.nc
        with nc.allow_non_contiguous_dma("tiny constant loads"):
            for l in range(2):
                col(self.g_mix[l], self.norm_mix[l])
                col(self.g_mem[l], self.norm_mem[l])
                col(self.g_src[l], self.norm_memsrc[l])
                col(self.g_ffn[l], self.norm_ffn[l])
            col(self.g_ssdn, self.ab_ssd_norm[0])
            for k in range(3):
                S.dma("pool", self.sconv_w[:, :, k], self.ab_sconv_w[k].rearrange("(c p) -> p c", p=128), writes=[self.sconv_w.b])
            for k in range(4):
                S.dma("pool", self.ssdc_w[:, :, k], self.ab_ssd_conv_w[k].rearrange("(c p) -> p c", p=128), writes=[self.ssdc_w.b])
            S.dma("pool", self.ssdc_b[:], self.ab_ssd_conv_b[0].rearrange("(c p) -> p c", p=128), writes=[self.ssdc_b.b])
            for l in range(2):
                S.dma("pool", self.memq_g[l][:], self.mem_q_norm[l].rearrange("(p o) -> p o", o=1), writes=[self.memq_g[l].b])
                S.dma("pool", self.memk_gc[l][:], self.mem_k_norm[l].rearrange("(p o) -> p o", o=1), writes=[self.memk_gc[l].b])
            ms(self.w1[:], self.w1.b, 0.0)
            for kv in range(2):
                r0 = 64 * kv
                for ab in range(2):
                    S.dma("pool", self.peT[r0:r0 + 64, ab, :], self.nsa_cmp_pe[kv, 16 * ab:16 * ab + 16].rearrange("j d -> d j"),
                          writes=[self.peT.b])
                    S.dma("pool", self.w1[r0:r0 + 64, ab, :, r0:r0 + 64],
                          self.nsa_cmp_w1[kv, 16 * ab:16 * ab + 16].rearrange("j d e -> d j e"), writes=[self.w1.b])
                S.dma("pool", self.w2[r0:r0 + 64, :], self.nsa_cmp_w2[kv], writes=[self.w2.b])
        S.dma("pool", self.dt_bias[:], self.ab_dt_bias.partition_broadcast(128), writes=[self.dt_bias.b])
        S.dma("pool", self.Aneg[:], self.ab_a_log.partition_broadcast(128), writes=[self.Aneg.b])
        S.dma("pool", self.Dsk[:], self.ab_d.partition_broadcast(128), writes=[self.Dsk.b])
        for l in range(2):
            S.dma("pool", self.memk_g[l][:], self.mem_k_norm[l:l + 1, :].partition_broadcast(128), writes=[self.memk_g[l].b])
            self.ts(self.memq_g[l][:], self.memq_g[l].b, self.memq_g[l][:], self.memq_g[l].b, 128 ** -0.5, ALU.mult)
        S.dma("pool", self.nsaq_g[:], self.nsa_q_norm.partition_broadcast(128), writes=[self.nsaq_g.b])
        self.ts(self.nsaq_g[:], self.nsaq_g.b, self.nsaq_g[:], self.nsaq_g.b, 64 ** -0.5, ALU.mult)
        for i in range(3):
            S.dma("pool", self.nsak_g[:, i, :], self.nsa_k_norm[i:i + 1, :].partition_broadcast(128), writes=[self.nsak_g.b])
        self.act(self.Aneg[:], self.Aneg.b, self.Aneg[:], self.Aneg.b, AF.Exp)
        self.ts(self.Aneg[:], self.Aneg.b, self.Aneg[:], self.Aneg.b, -1.0, ALU.mult)
        ps = self.psum()
        for ab in range(2):
            for j in range(16):
                self.mm(ps[:, ab:ab + 1], ps.b, self.w1[:, ab, j, :], self.w1.b, self.peT[:, ab, j:j + 1], self.peT.b,
                        start=(j == 0), stop=(j == 15))
        self.cp(self.cAB[:], self.cAB.b, ps[:, 0:2], ps.b)
        t2 = self.junk
        it = self.itmp
        S.op("pool", lambda e: e.iota(it[:, 0:33], pattern=[[64, 33]], base=0, channel_multiplier=0), writes=[it.b])
        S.op("pool", lambda e: e.iota(it[:, 64:97], pattern=[[0, 33]], base=0, channel_multiplier=16), writes=[it.b])
        S.op("pool", lambda e: e.iota(it[:, 128:129], pattern=[[0, 1]], base=0, channel_multiplier=2), writes=[it.b])
        self.cp(t2[:, 0:129], t2.b, it[:, 0:129], it.b)
        self.cp(self.piota2[:], self.piota2.b, t2[:, 128:129], t2.b)
        self.ts(t2[:, 256:289], t2.b, t2[:, 0:33], t2.b, 64.0, ALU.add)
        self.stt(t2[:, 320:353], t2.b, t2[:, 64:97], t2.b, 32.0, t2[:, 256:289], t2.b, ALU.add, ALU.min)
        self.tt(t2[:, 384:417], t2.b, t2[:, 64:97], t2.b, t2[:, 0:33], t2.b, ALU.max)
        self.tt(t2[:, 448:481], t2.b, t2[:, 320:353], t2.b, t2[:, 384:417], t2.b, ALU.subtract)
        self.ts(self.mov[:], self.mov.b, t2[:, 448:481], t2.b, 0.0, ALU.max, 1.0 / 32, ALU.mult)
        self.cp(self.mov33[:], self.mov33.b, self.mov[:], self.mov.b)
        ms(self.ones_bf[:], self.ones_bf.b, 1.0)
        self.rope_table(self.cosT, self.sinT, 16, lambda e, v: e.iota(v, pattern=[[128, 16], [0, 8]], base=0, channel_multiplier=1), 128)
        self.rope_table(self.cosC, self.sinC, 8, lambda e, v: e.iota(v, pattern=[[2048, 8], [0, 8]], base=31, channel_multiplier=16), 128)
        self.rope_table(self.cosS, self.sinS, 1, lambda e, v: e.iota(v, pattern=[[0, 1], [0, 8]], base=PAST, channel_multiplier=1), 8)
        ms(self.VS1[:], self.VS1.b, 1.0)
        ms(self.VW1[:], self.VW1.b, 1.0)
        ms(self.VC1[:], self.VC1.b, 1.0)
        for l in range(2):
            ms(self.V1[l][:], self.V1[l].b, 1.0)
        for g in range(4):
            self.cp(self.VC1[:, g, 65:97], self.VC1.b, self.mov[:, 0:32], self.mov.b)

    def rope_table(self, cosT, sinT, ncol, iota_fn, rows):
        S = self.S
        t = self.f1
        v = t[0:rows, 0:ncol * 8].rearrange("p (c k) -> p c k", k=8)
        iv = self.itmp[0:rows, 0:ncol * 8].rearrange("p (c k) -> p c k", k=8)
        S.op("pool", lambda e: iota_fn(e, iv), writes=[self.itmp.b])
        self.cp(v, t.b, iv, self.itmp.b)
        inv = self.f2
        for k in range(8):
            self.memset(inv[0:rows, k:k + 1], inv.b, THETA ** (-k / 8.0))
        invb = inv[0:rows, 0:8].unsqueeze(1).to_broadcast([rows, ncol, 8])
        ang = self.f3
        av = ang[0:rows, 0:ncol * 8].rearrange("p (c k) -> p c k", k=8)
        self.tt(av, ang.b, v, t.b, invb, inv.b, ALU.mult)
        a2 = ang[0:rows, 0:ncol * 8]
        w = self.junk[0:rows, 0:ncol * 8]
        self.range_reduce(w, a2, ang.b, rows, ncol * 8, 0.0)
        self.act(sinT[0:rows].rearrange("p c k -> p (c k)"), sinT.b, w, self.junk.b, AF.Sin)
        self.range_reduce(w, a2, ang.b, rows, ncol * 8, math.pi / 2)
        self.act(cosT[0:rows].rearrange("p c k -> p (c k)"), cosT.b, w, self.junk.b, AF.Sin)

    def range_reduce(self, w, a, ab, rows, m, shift):
        jb = self.junk.b
        two_pi = 2 * math.pi
        hi = 6.28125
        lo = two_pi - hi
        t = self.junk[0:rows, 512:512 + m]
        it = self.itmp[0:rows, 0:m]
        kf = self.junk[0:rows, 768:768 + m]
        self.ts(w, jb, a, ab, shift, ALU.add)
        self.ts(t, jb, w, jb, 1.0 / two_pi, ALU.mult)
        self.cp(it, self.itmp.b, t, jb)
        self.cp(kf, jb, it, self.itmp.b)
        self.stt(w, jb, kf, jb, -hi, w, jb, ALU.mult, ALU.add)
        self.stt(w, jb, kf, jb, -lo, w, jb, ALU.mult, ALU.add)
        self.ts(t, jb, w, jb, math.pi, ALU.is_gt)
        self.stt(w, jb, t, jb, -two_pi, w, jb, ALU.mult, ALU.add)
        self.ts(t, jb, w, jb, -math.pi, ALU.is_lt)
        self.stt(w, jb, t, jb, two_pi, w, jb, ALU.mult, ALU.add)
        self.ts(w, jb, w, jb, math.pi - 1e-5, ALU.min, -(math.pi - 1e-5), ALU.max)

    def prompt_seq(self, s):
        self.mem_kv_prompt(s)
        if self.stage < 3:
            return
        self.seq_reset()
        for tau in range(0, self.n_ptiles, 2):
            self.tile_pair(s, tau)
        self.xres = self.xr[0]

    def seq_reset(self):
        ms = self.memset
        ms(self.ubuf[:], self.ubuf.b, 0.0)
        ms(self.xbuf[:], self.xbuf.b, 0.0)
        ms(self.hstate[:], self.hstate.b, 0.0)
        ms(self.hstate_bf[:], self.hstate_bf.b, 0.0)
        ms(self.Aall[:], self.Aall.b, 0.0)
        ms(self.pre[:], self.pre.b, 0.0)

    def mem_kv_prompt(self, s):
        S = self.S
        for l in range(2):
            for mc in range(2):
                x = self.xres
                S.dma("pool", x[:], self.mem_prompt[s, mc * 128:(mc + 1) * 128, :], writes=[x.b])
                sub = 9
                if sub >= 1:
                    self.rmsnorm_T(x, 128, self.g_src[l], self.hT)
                if sub < 2:
                    S.dma("pool", self.p_mem_kv[l, s, mc * 128:(mc + 1) * 128, :], x[:], reads=[x.b, self.hT.b], writes=[self.b_p_mem_kv])
                    continue
                out = self.f1

                def consume(off, nb, ps, pb, out=out, l=l, mc=mc):
                    if sub < 3:
                        self.cp(out[:, off:off + nb], out.b, ps, pb, eng="act")
                        return
                    if off == 0:
                        sq = self.junk
                        self.act(sq[:, 0:512], sq.b, ps, pb, AF.Square)
                        self.red(self.sm[:, 0:4], self.sm.b, sq[:, 0:512].rearrange("p (h d) -> p h d", h=4), sq.b)
                        self.rstd(self.sm[:, 4:8], self.sm.b, self.sm[:, 0:4], self.sm.b, 128, 1.0 / 128)
                        rb = self.sm[:, 4:8].unsqueeze(2).to_broadcast([128, 4, 128])
                        ov = out[:, 0:512].rearrange("p (h d) -> p h d", h=4)
                        self.tt(ov, out.b, ps.rearrange("p (h d) -> p h d", h=4), pb, rb, self.sm.b, ALU.mult)
                        self.cp(self.big[:, 0:512], self.big.b, out[:, 0:512], out.b, eng="act")
                        gb = self.memk_g[l][:].unsqueeze(1).to_broadcast([128, 4, 128])
                        self.tt(ov, out.b, ov, out.b, gb, self.memk_g[l].b, ALU.mult)
                    else:
                        self.cp(out[:, 512:1024], out.b, ps, pb, eng="act")
                        self.cp(self.V1[l][:, mc, :, 0:128], self.V1[l].b, ps.rearrange("p (h d) -> p h d", h=4), pb)
                self.linear_tm(self.hT, 128, self.W_kv[l], 1024, 0, 1024, consume)
                S.dma("pool", self.p_mem_kv[l, s, mc * 128:(mc + 1) * 128, :], out[:], reads=[out.b], writes=[self.b_p_mem_kv])
                if sub < 4:
                    continue
                pb = self.psumb()
                for h in range(4):
                    self.tr(pb[:, h * 128:(h + 1) * 128], pb.b, self.big[:, h * 128:(h + 1) * 128], self.big.b)
                self.ts(self.KT[l][:, :, mc * 128:(mc + 1) * 128], self.KT[l].b, pb[:, 0:512].rearrange("p (h t) -> p h t", h=4),
                        pb.b, self.memk_gc[l][:, 0:1], ALU.mult, rd=[self.memk_gc[l].b])

    def tile(self, s, tau):
        S = self.S
        n = 128
        x = self.xres
        S.dma("pool", x[:], self.x_prompt[s, tau * 128:(tau + 1) * 128, :], writes=[x.b])
        st = self.stage
        if st >= 4:
            self.layer0(s, tau, n)
        if st >= 5:
            self.mem_attn(0, n)
        if st >= 6:
            self.ffn(0, n)
        if st >= 7:
            self.layer1(s, tau, n)
        if st >= 8:
            self.mem_attn(1, n)
            self.ffn(1, n)
        S.dma("pool", self.y_prompt[s, tau * 128:(tau + 1) * 128, :], x[:], reads=[x.b], writes=[self.b_y_prompt])

    def tile_pair(self, s, tau):
        S = self.S
        n = 128
        xr = self.xr
        for i in range(2):
            S.dma("sp", xr[i][:], self.x_prompt[s, (tau + i) * 128:(tau + i + 1) * 128, :], writes=[xr[i].b])
        for i in range(2):
            self.xres = xr[i]
            self.layer0(s, tau + i, n)
            self.mem_attn(0, n)
        self.ffn_pair(0, n)
        for i in range(2):
            self.xres = xr[i]
            self.layer1(s, tau + i, n)
            self.mem_attn(1, n)
        self.ffn_pair(1, n)
        for i in range(2):
            S.dma("pool", self.y_prompt[s, (tau + i) * 128:(tau + i + 1) * 128, :], xr[i][:], reads=[xr[i].b],
                  writes=[self.b_y_prompt])

    def wrows(self, Wd, r0, m, ncols):
        i = self.w_rr % len(self.wbufs)
        self.w_rr += 1
        wb = self.wbufs[i]
        src = Wd.t[r0 * 128:(r0 + m) * 128, 0:ncols].rearrange("(c p) n -> p c n", p=128)
        view = wb[:, 0:m * ncols].rearrange("p (c n) -> p c n", c=m)
        self.S.dma("sp", view, src, reads=[Wd.b], writes=[wb.b])
        return view, wb.b

    def ffn_pair(self, l, n):
        xr = self.xr
        pTf = self.pT.t[:].rearrange("p a b -> p (a b)")
        hTs = [self.hT, TT(pTf[:, 0:1024].rearrange("p (c t) -> p c t", c=8), "hTB", self.pT.b)]
        acts = [self.big, TT(pTf[:, 1024:1024 + DFF], "actB", self.pT.b)]
        sgs = [TT(self.f1.t[:, 0:512], "sgA", self.f1.b), TT(self.f2.t[:, 0:512], "sgB", self.f2.b)]
        for i in range(2):
            self.rmsnorm_T(xr[i], n, self.g_ffn[l], hTs[i])
        off = 0
        while off < DFF:
            nb = min(512, DFF - off)
            wv, wbb = self.wblock(self.W_g[l], 1024, off, nb)
            for i in range(2):
                ps = self.psum()
                for k in range(8):
                    self.mm(ps[0:n, 0:nb], ps.b, hTs[i][:, k, 0:n], hTs[i].b, wv[:, k, 0:nb], wbb, start=(k == 0), stop=(k == 7))
                self.act(sgs[i][0:n, 0:nb], sgs[i].b, ps[0:n, 0:nb], ps.b, AF.Silu)
            wv, wbb = self.wblock(self.W_u[l], 1024, off, nb)
            for i in range(2):
                ps = self.psum()
                for k in range(8):
                    self.mm(ps[0:n, 0:nb], ps.b, hTs[i][:, k, 0:n], hTs[i].b, wv[:, k, 0:nb], wbb, start=(k == 0), stop=(k == 7))
                self.tt(acts[i][0:n, off:off + nb], acts[i].b, ps[0:n, 0:nb], ps.b, sgs[i][0:n, 0:nb], sgs[i].b, ALU.mult)
            off += nb
        po = [[self.psum(), self.psum()] for _ in range(2)]
        bT = self.bigT
        aTs = [[TT(bT.t[:, 4 * (2 * par + i):4 * (2 * par + i) + 4, :], "aT%d%d" % (par, i)) for i in range(2)] for par in range(2)]
        nk = DFF // 128
        groups = [(kg, min(4, nk - kg)) for kg in range(0, nk, 4)]

        def dn_t(gi):
            kg, m = groups[gi]
            for i in range(2):
                aT = aTs[gi % 2][i]
                pb = self.psumb()
                for j in range(m):
                    self.tr(pb[:, j * n:(j + 1) * n], pb.b, acts[i][0:n, (kg + j) * 128:(kg + j + 1) * 128], acts[i].b)
                self.cp(aT[:, 0:m, 0:n], aT.b, pb[:, 0:m * n].rearrange("p (c t) -> p c t", c=m), pb.b, eng=("act" if i == 0 else "dve"))

        def dn_mm(gi):
            kg, m = groups[gi]
            wv, wbb = self.wrows(self.W_d[l], kg, m, 1024)
            for i in range(2):
                aT = aTs[gi % 2][i]
                for half in range(2):
                    p_ = po[i][half]
                    for j in range(m):
                        self.mm(p_[0:n, :], p_.b, aT[:, j, 0:n], aT.b, wv[:, j, half * 512:(half + 1) * 512], wbb,
                                start=(kg == 0 and j == 0), stop=(kg + j == nk - 1))
        dn_t(0)
        for gi in range(len(groups)):
            if gi + 1 < len(groups):
                dn_t(gi + 1)
            dn_mm(gi)
        for i in range(2):
            x = xr[i]
            for half in range(2):
                p_ = po[i][half]
                self.tt(x[0:n, half * 512:(half + 1) * 512], x.b, x[0:n, half * 512:(half + 1) * 512], x.b, p_[0:n, :], p_.b, ALU.add)

    def ffn(self, l, n):
        x = self.xres
        self.rmsnorm_T(x, n, self.g_ffn[l], self.hT)
        act = self.big
        sg = self.f1

        def c_gate(off, nb, ps, pb):
            self.act(sg[0:n, off:off + nb], sg.b, ps, pb, AF.Silu)
        off = 0
        while off < DFF:
            nb = min(512, DFF - off)
            self.linear_tm(self.hT, n, self.W_g[l], 1024, off, nb, lambda o, b, ps, pb: self.act(sg[0:n, 0:nb], sg.b, ps, pb, AF.Silu))
            self.linear_tm(self.hT, n, self.W_u[l], 1024, off, nb,
                           lambda o, b, ps, pb: self.tt(act[0:n, off:off + nb], act.b, ps, pb, sg[0:n, 0:nb], sg.b, ALU.mult))
            off += nb
        self.transpose_fm(act, n, 22, self.bigT)
        self.linear_tm(self.bigT, n, self.W_d[l], DFF, 0, 1024,
                       lambda o, b, ps, pb: self.tt(x[0:n, o:o + b], x.b, x[0:n, o:o + b], x.b, ps, pb, ALU.add))

    def mem_attn(self, l, n):
        x = self.xres
        self.rmsnorm_T(x, n, self.g_mem[l], self.hT)
        q = self.f1
        sm = self.sm

        def cq(off, nb, ps, pb):
            sq = self.junk
            self.act(sq[0:n, 0:512], sq.b, ps, pb, AF.Square)
            self.red(sm[0:n, 0:4], sm.b, sq[0:n, 0:512].rearrange("p (h d) -> p h d", h=4), sq.b)
            self.rstd(sm[0:n, 4:8], sm.b, sm[0:n, 0:4], sm.b, 128, 1.0 / 128)
            rb = sm[0:n, 4:8].unsqueeze(2).to_broadcast([n, 4, 128])
            self.tt(self.big[0:n, 0:512].rearrange("p (h d) -> p h d", h=4), self.big.b,
                    ps.rearrange("p (h d) -> p h d", h=4), pb, rb, sm.b, ALU.mult)
        self.linear_tm(self.hT, n, self.W_q[l], 1024, 0, 512, cq)
        pb = self.psumb()
        for h in range(4):
            self.tr(pb[:, h * n:(h + 1) * n], pb.b, self.big[0:n, h * 128:(h + 1) * 128], self.big.b)
        self.ts(self.qT[:, 0:4, 0:n], self.qT.b, pb[:, 0:4 * n].rearrange("p (h t) -> p h t", h=4), pb.b,
                self.memq_g[l][:, 0:1], ALU.mult, rd=[self.memq_g[l].b])
        pT = self.pT
        for mc in range(2):
            ps = self.psum()
            for h in range(4):
                self.mm(ps[:, h * n:(h + 1) * n], ps.b, self.KT[l][:, h, mc * 128:(mc + 1) * 128], self.KT[l].b,
                        self.qT[:, h, 0:n], self.qT.b)
            self.act(pT[:, mc, 0:4 * n], pT.b, ps[:, 0:4 * n], ps.b, AF.Exp)
        o = self.big
        for h in range(4):
            ps = self.psum()
            for mc in range(2):
                self.mm(ps[0:n, 0:129], ps.b, pT[:, mc, h * n:(h + 1) * n], pT.b, self.V1[l][:, mc, h, :], self.V1[l].b,
                        start=(mc == 0), stop=(mc == 1))
            self.recip(sm[0:n, 8:9], sm.b, ps[0:n, 128:129], ps.b)
            self.ts(o[0:n, h * 128:(h + 1) * 128], o.b, ps[0:n, 0:128], ps.b, sm[0:n, 8:9], ALU.mult, rd=[sm.b])
        self.transpose_fm(o, n, 4, self.bigT)
        self.linear_tm(self.bigT, n, self.W_o[l], 512, 0, 1024,
                       lambda o_, b, ps, pb: self.tt(x[0:n, o_:o_ + b], x.b, x[0:n, o_:o_ + b], x.b, ps, pb, ALU.add))

    def layer0(self, s, tau, n, last=None):
        S = self.S
        x = self.xres
        last = (tau == self.n_ptiles - 1) if last is None else last
        self.rmsnorm_T(x, n, self.g_mix[0], self.hT)
        ubuf, xbuf, s_h = self.ubuf, self.xbuf, self.s_h
        self.linear_fm(self.hT, n, self.W_in0, 1024, 0, 8,
                       lambda c, m, ps, pb: self.cp(s_h[:, c:c + m, 0:n], s_h.b, ps, pb, eng="act"))
        sbv = self.f2[:, 0:8 * n].rearrange("p (c t) -> p c t", c=8)
        self.linear_fm(self.hT, n, self.W_in0, 1024, 1024, 8,
                       lambda c, m, ps, pb: self.cp(sbv[:, c:c + m, :], self.f2.b, ps, pb, eng="act"))
        self.linear_fm(self.hT, n, self.W_in0, 1024, 2048, 8,
                       lambda c, m, ps, pb: self.tt(ubuf[:, c:c + m, 2:2 + n], ubuf.b, ps, pb, s_h[:, c:c + m, 0:n], s_h.b, ALU.mult))
        ca = self.cacc
        for k in range(3):
            wk = self.sconv_w[:, :, k:k + 1].to_broadcast([128, 8, n])
            if k == 0:
                self.tt(ca[:, 0:8, 0:n], ca.b, ubuf[:, :, 0:n], ubuf.b, wk, self.sconv_w.b, ALU.mult)
            else:
                self.tt(self.junk[:, 0:8 * n].rearrange("p (c t) -> p c t", c=8), self.junk.b, ubuf[:, :, k:k + n], ubuf.b,
                        wk, self.sconv_w.b, ALU.mult)
                self.tt(ca[:, 0:8, 0:n], ca.b, ca[:, 0:8, 0:n], ca.b, self.junk[:, 0:8 * n].rearrange("p (c t) -> p c t", c=8),
                        self.junk.b, ALU.add)
        ycat = self.bigT
        self.tt(ycat[:, 0:8, 0:n], ycat.b, ca[:, 0:8, 0:n], ca.b, sbv, self.f2.b, ALU.mult)
        if last:
            with self.nc.allow_non_contiguous_dma("tiny state out"):
                dst, db = (self.p_sconv, self.b_p_sconv) if self.cur_prompt else (self.s_sconv, self.b_s_sconv)
                for k in range(2):
                    S.dma("pool", dst[s, k].rearrange("(c p) -> p c", p=128), ubuf[:, :, n + k], reads=[ubuf.b], writes=[db])
        self.cp(self.sm[:, 0:16].rearrange("p (c k) -> p c k", k=2), self.sm.b, ubuf[:, :, n:n + 2], ubuf.b)
        self.cp(ubuf[:, :, 0:2], ubuf.b, self.sm[:, 0:16].rearrange("p (c k) -> p c k", k=2), self.sm.b)
        self.linear_fm(self.hT, n, self.W_in0, 1024, 4096, 16,
                       lambda c, m, ps, pb: self.cp(xbuf[:, c:c + m, 3:3 + n], xbuf.b, ps, pb, eng="act"))
        if last:
            with self.nc.allow_non_contiguous_dma("tiny state out"):
                dst, db = (self.p_ssd_conv, self.b_p_ssd_conv) if self.cur_prompt else (self.s_ssd_conv, self.b_s_ssd_conv)
                for k in range(3):
                    S.dma("pool", dst[s, k].rearrange("(c p) -> p c", p=128), xbuf[:, :, n + k], reads=[xbuf.b], writes=[db])
        xcs = self.xcs
        for hh in range(2):
            c0 = 8 * hh
            acc = ca[:, 0:8, 0:n]
            jv2 = self.junk[:, 0:8 * n].rearrange("p (c t) -> p c t", c=8)
            for k in range(4):
                wk2 = self.ssdc_w[:, c0:c0 + 8, k:k + 1].to_broadcast([128, 8, n])
                if k == 0:
                    self.tt(acc, ca.b, xbuf[:, c0:c0 + 8, 0:n], xbuf.b, wk2, self.ssdc_w.b, ALU.mult)
                else:
                    self.tt(jv2, self.junk.b, xbuf[:, c0:c0 + 8, k:k + n], xbuf.b, wk2, self.ssdc_w.b, ALU.mult)
                    self.tt(acc, ca.b, acc, ca.b, jv2, self.junk.b, ALU.add)
            bb = self.ssdc_b[:, c0:c0 + 8].unsqueeze(2).to_broadcast([128, 8, n])
            self.tt(acc, ca.b, acc, ca.b, bb, self.ssdc_b.b, ALU.add)
            self.act(xcs[:, c0:c0 + 8, 0:n], xcs.b, acc, ca.b, AF.Silu)
        self.cp(self.sm[:, 16:64].rearrange("p (c k) -> p c k", k=3), self.sm.b, xbuf[:, :, n:n + 3], xbuf.b)
        self.cp(xbuf[:, :, 0:3], xbuf.b, self.sm[:, 16:64].rearrange("p (c k) -> p c k", k=3), self.sm.b)
        x_tm, B_tm = self.x_tm, self.B_tm
        pb = self.psumb()
        for c in range(8):
            self.tr(pb[0:n, c * 128:(c + 1) * 128], pb.b, xcs[:, c, 0:n], xcs.b)
        self.cp(x_tm[0:n, :], x_tm.b, pb[0:n, :], pb.b, eng="act")
        pb = self.psumb()
        for c in range(4):
            self.tr(pb[0:n, c * 128:(c + 1) * 128], pb.b, xcs[:, 8 + c, 0:n], xcs.b)
        self.cp(B_tm[0:n, :], B_tm.b, pb[0:n, 0:512], pb.b, eng="act")
        zs = self.f1
        self.linear_tm(self.hT, n, self.W_in0, 1024, 3072, 1024,
                       lambda o, b, ps, pb_: self.act(zs[0:n, o:o + b], zs.b, ps, pb_, AF.Silu))
        sm = self.sm
        DT, DA, CUM, ECUM, DTE, CL = (sm[0:n, 64:80], sm[0:n, 80:96], sm[0:n, 96:112], sm[0:n, 112:128], sm[0:n, 128:144],
                                     sm[0:n, 144:160])

        def cdt(o, b, ps, pb_):
            self.tt(DT, sm.b, ps, pb_, self.dt_bias[0:n, :], self.dt_bias.b, ALU.add)
            self.act(DT, sm.b, DT, sm.b, AF.Exp)
            self.act(DT, sm.b, DT, sm.b, AF.Ln, bias=1.0)
        self.linear_tm(self.hT, n, self.W_in0, 1024, 6144, 16, cdt)
        self.tt(DA, sm.b, DT, sm.b, self.Aneg[0:n, :], self.Aneg.b, ALU.mult)
        ps = self.psum()
        psx = self.psum()
        self.mm(ps[0:n, 0:16], ps.b, self.U[0:n, 0:n], self.U.b, DA, sm.b)
        self.mm(psx[:, 16:32], psx.b, self.onesf[0:n, :], self.onesf.b, DA, sm.b)
        self.cp(CUM, sm.b, ps[0:n, 0:16], ps.b)
        self.cp(sm[:, 144:160], sm.b, psx[:, 16:32], psx.b)
        self.act(ECUM, sm.b, CUM, sm.b, AF.Exp)
        self.tt(DTE, sm.b, CL, sm.b, CUM, sm.b, ALU.subtract)
        self.act(DTE, sm.b, DTE, sm.b, AF.Exp)
        self.act(sm[:, 144:160], sm.b, sm[:, 144:160], sm.b, AF.Exp)
        xdt, xdte = self.xdt, self.xdte
        self.tt(xdt[0:n, :].rearrange("p (h d) -> p h d", h=16), xdt.b, x_tm[0:n, :].rearrange("p (h d) -> p h d", h=16), x_tm.b,
                DT.unsqueeze(2).to_broadcast([n, 16, 64]), sm.b, ALU.mult)
        self.tt(xdte[0:n, :].rearrange("p (h d) -> p h d", h=16), xdte.b, xdt[0:n, :].rearrange("p (h d) -> p h d", h=16), xdt.b,
                DTE.unsqueeze(2).to_broadcast([n, 16, 64]), sm.b, ALU.mult)
        rmask = self.rmask
        cbm = self.cbm
        psc = self.psum()
        for g in range(4):
            self.mm(psc[0:n, g * n:(g + 1) * n], psc.b, xcs[:, 8 + g, 0:n], xcs.b, xcs[:, 12 + g, 0:n], xcs.b)
        self.tt(cbm[0:n, :, 0:n], cbm.b, psc[0:n, 0:4 * n].rearrange("p (g t) -> p g t", g=4), psc.b,
                self.U[0:n, 0:n].unsqueeze(1).to_broadcast([n, 4, n]), self.U.b, ALU.mult)
        seg, Mh = self.seg, self.Mh
        for g in range(4):
            self.tt(rmask[0:n, :, 0:n], rmask.b, self.U[0:n, 0:n].unsqueeze(1).to_broadcast([n, 4, n]), self.U.b,
                    sm[0:n, 80 + 4 * g:84 + 4 * g].unsqueeze(2).to_broadcast([n, 4, n]), sm.b, ALU.mult)
            ps = self.psum()
            for hh in range(4):
                h = 4 * g + hh
                self.mm(ps[0:n, hh * n:(hh + 1) * n], ps.b, self.onesf[0:n, 0:n], self.onesf.b, rmask[0:n, hh, 0:n], rmask.b)
            for hh in range(4):
                h = 4 * g + hh
                self.ts(seg[0:n, hh, 0:n], seg.b, ps[0:n, hh * n:(hh + 1) * n], ps.b, sm[0:n, 96 + h:97 + h], ALU.min,
                        sm[0:n, 96 + h:97 + h], ALU.subtract, rd=[sm.b])
            self.act(seg[0:n, :, 0:n], seg.b, seg[0:n, :, 0:n], seg.b, AF.Exp)
            self.tt(Mh[0:n, 4 * g:4 * g + 4, 0:n], Mh.b, seg[0:n, :, 0:n], seg.b,
                    cbm[0:n, g:g + 1, 0:n].to_broadcast([n, 4, n]), cbm.b, ALU.mult)
        y = self.f3
        hs, hsb = self.hstate, self.hstate_bf
        for half in range(2):
            psd = self.psum()
            pso = self.psum()
            for hh in range(8):
                h = 8 * half + hh
                self.mm(psd[0:n, hh * 64:(hh + 1) * 64], psd.b, Mh[0:n, h, 0:n], Mh.b, xdt[0:n, h * 64:(h + 1) * 64], xdt.b)
            for gg in range(2):
                g = 2 * half + gg
                self.mm(pso[0:n, gg * 256:(gg + 1) * 256], pso.b, xcs[:, 12 + g, 0:n], xcs.b, hsb[:, g * 256:(g + 1) * 256], hsb.b)
            yv = y[0:n, half * 512:(half + 1) * 512].rearrange("p (h d) -> p h d", h=8)
            self.tt(yv, y.b, pso[0:n, :].rearrange("p (h d) -> p h d", h=8), pso.b,
                    ECUM[:, 8 * half:8 * half + 8].unsqueeze(2).to_broadcast([n, 8, 64]), sm.b, ALU.mult)
            self.tt(y[0:n, half * 512:(half + 1) * 512], y.b, y[0:n, half * 512:(half + 1) * 512], y.b, psd[0:n, :], psd.b, ALU.add)
        jv = self.junk[0:n, :].rearrange("p (h d) -> p h d", h=16)
        self.tt(jv, self.junk.b, x_tm[0:n, :].rearrange("p (h d) -> p h d", h=16), x_tm.b,
                self.Dsk[0:n, :].unsqueeze(2).to_broadcast([n, 16, 64]), self.Dsk.b, ALU.mult)
        self.tt(y[0:n, :], y.b, y[0:n, :], y.b, self.junk[0:n, :], self.junk.b, ALU.add)
        self.tt(y[0:n, :], y.b, y[0:n, :], y.b, zs[0:n, :], zs.b, ALU.mult)
        self.tt(self.junk[0:n, :], self.junk.b, y[0:n, :], y.b, y[0:n, :], y.b, ALU.mult)
        self.red(sm[0:n, 160:164], sm.b, self.junk[0:n, :].rearrange("p (g d) -> p g d", g=4), self.junk.b)
        self.rstd(sm[0:n, 164:168], sm.b, sm[0:n, 160:164], sm.b, 256, 1.0 / 256)
        yb = self.big
        self.tt(yb[0:n, 0:1024].rearrange("p (g d) -> p g d", g=4), yb.b, y[0:n, :].rearrange("p (g d) -> p g d", g=4), y.b,
                sm[0:n, 164:168].unsqueeze(2).to_broadcast([n, 4, 256]), sm.b, ALU.mult)
        for c0 in (0, 4):
            pb = self.psumb()
            for j in range(4):
                self.tr(pb[:, j * n:(j + 1) * n], pb.b, yb[0:n, (c0 + j) * 128:(c0 + j + 1) * 128], yb.b)
            self.tt(ycat[:, 8 + c0:12 + c0, 0:n], ycat.b, pb[:, 0:4 * n].rearrange("p (c t) -> p c t", c=4), pb.b,
                    self.g_ssdn[:, c0:c0 + 4].unsqueeze(2).to_broadcast([128, 4, n]), self.g_ssdn.b, ALU.mult)
        for half in range(2):
            ps = self.psum()
            for gg in range(2):
                g = 2 * half + gg
                self.mm(ps[:, gg * 256:(gg + 1) * 256], ps.b, B_tm[0:n, g * 128:(g + 1) * 128], B_tm.b,
                        xdte[0:n, g * 256:(g + 1) * 256], xdte.b)
            hv = hs[:, half * 512:(half + 1) * 512].rearrange("p (h d) -> p h d", h=8)
            self.tt(hv, hs.b, hv, hs.b, sm[:, 144 + 8 * half:152 + 8 * half].unsqueeze(2).to_broadcast([128, 8, 64]), sm.b, ALU.mult)
            self.tt(hs[:, half * 512:(half + 1) * 512], hs.b, hs[:, half * 512:(half + 1) * 512], hs.b, ps[:, :], ps.b, ALU.add)
        self.cp(hsb[:], hsb.b, hs[:], hs.b, eng="act")
        if last:
            for c in range(8):
                ps = self.psum()
                self.S.op("pe", lambda e: e.transpose(ps[:, 0:128], hs[:, c * 128:(c + 1) * 128], self.identf[:]),
                          reads=[hs.b, self.identf.b], writes=[ps.b])
                self.cp(self.f2[:, 0:128], self.f2.b, ps[:, 0:128], ps.b)
                dst, db = (self.p_ssd, self.b_p_ssd) if self.cur_prompt else (self.s_ssd, self.b_s_ssd)
                self.S.dma("pool", dst[s, c * 128:(c + 1) * 128, :], self.f2[:, 0:128], reads=[self.f2.b], writes=[db])
        self.linear_tm(ycat, n, self.W_out0, 2048, 0, 1024,
                       lambda o, b, ps, pb_: self.tt(x[0:n, o:o + b], x.b, x[0:n, o:o + b], x.b, ps, pb_, ALU.add))

    def headnorm(self, v, vb, n, H, gain, gb, so):
        sm = self.sm
        jv = self.junk[0:n, 0:H * 64].rearrange("p (h d) -> p h d", h=H)
        self.tt(jv, self.junk.b, v, vb, v, vb, ALU.mult)
        self.red(sm[0:n, so:so + H], sm.b, jv, self.junk.b)
        self.rstd(sm[0:n, so + H:so + 2 * H], sm.b, sm[0:n, so:so + H], sm.b, 64, 1.0 / 64)
        self.tt(v, vb, v, vb, sm[0:n, so + H:so + 2 * H].unsqueeze(2).to_broadcast([n, H, 64]), sm.b, ALU.mult)
        self.tt(v, vb, v, vb, gain.unsqueeze(1).to_broadcast([n, H, 64]), gb, ALU.mult)

    def rope(self, v, vb, n, H, cos, sin, cb):
        f2 = self.f2
        t = [f2[0:n, i * 128:i * 128 + H * 8].rearrange("p (h k) -> p h k", k=8) for i in range(4)]
        cb_ = cos.unsqueeze(1).to_broadcast([n, H, 8])
        sb_ = sin.unsqueeze(1).to_broadcast([n, H, 8])
        x1, x2 = v[:, :, 0:8], v[:, :, 8:16]
        self.tt(t[0], f2.b, x1, vb, cb_, cb, ALU.mult)
        self.tt(t[1], f2.b, x2, vb, sb_, cb, ALU.mult)
        self.tt(t[2], f2.b, x2, vb, cb_, cb, ALU.mult)
        self.tt(t[3], f2.b, x1, vb, sb_, cb, ALU.mult)
        self.tt(x1, vb, t[0], f2.b, t[1], f2.b, ALU.subtract)
        self.tt(x2, vb, t[2], f2.b, t[3], f2.b, ALU.add)

    def nsa_front(self, n, cos, sin, cb):
        x, sm = self.xres, self.sm
        self.rmsnorm_T(x, n, self.g_mix[1], self.hT)
        rows, winr, gates, q = self.rows, self.winr, self.gates, self.f1

        def cons(off, nb, ps, pb):
            if off < 1024:
                self.cp(q[0:n, off:off + nb], q.b, ps, pb, eng="act")
            elif off == 1024:
                self.cp(rows[0:n, 0:512], rows.b, ps, pb, eng="act")
            elif off == 1536:
                self.cp(rows[0:n, 512:1024], rows.b, ps, pb, eng="act")
            elif off == 2048:
                self.cp(winr[0:n, :], winr.b, ps, pb, eng="act")
            else:
                self.act(gates[0:n, :], gates.b, ps, pb, AF.Exp, scale=-1.0)
                self.ts(gates[0:n, :], gates.b, gates[0:n, :], gates.b, 1.0, ALU.add)
                self.recip(gates[0:n, :], gates.b, gates[0:n, :], gates.b)
        self.linear_tm(self.hT, n, self.W_in1, 1024, 0, IN1, cons)
        qv = q[0:n, :].rearrange("p (h d) -> p h d", h=16)
        self.headnorm(qv, q.b, n, 16, self.nsaq_g[0:n, :], self.nsaq_g.b, 168)
        self.rope(qv, q.b, n, 16, cos, sin, cb)
        kv_ = rows[0:n, 512:768].rearrange("p (h d) -> p h d", h=4)
        self.headnorm(kv_, rows.b, n, 4, self.nsak_g[0:n, 1, :], self.nsak_g.b, 200)
        self.rope(kv_, rows.b, n, 4, cos, sin, cb)
        kw_ = winr[0:n, 0:256].rearrange("p (h d) -> p h d", h=4)
        self.headnorm(kw_, winr.b, n, 4, self.nsak_g[0:n, 2, :], self.nsak_g.b, 200)
        self.rope(kw_, winr.b, n, 4, cos, sin, cb)
        return qv, kv_, kw_

    def nsa_q_stage(self, n, qv):
        big, q, qT2 = self.big, self.f1, self.qT2
        qd = big[0:n, 0:2048].rearrange("p (h c d) -> p h c d", h=16, c=2)
        self.cp(qd[:, :, 0, :], big.b, qv, q.b, eng="act")
        self.cp(qd[:, :, 1, :], big.b, qv, q.b, eng="pool")
        for c0 in range(0, 16, 8):
            pb = self.psumb()
            for j in range(8):
                a = (c0 + j) * 128
                self.tr(pb[:, j * n:(j + 1) * n], pb.b, big[0:n, a:a + 128], big.b)
            pv = pb[:, 0:8 * n].rearrange("p (c t) -> p c t", c=8)
            self.cp(qT2[0:64, 0, c0:c0 + 8, 0:n], qT2.b, pv[0:64], pb.b, eng="act")
            self.cp(qT2[64:128, 1, c0:c0 + 8, 0:n], qT2.b, pv[64:128], pb.b)

    def layer1(self, s, tau, n):
        S = self.S
        x, sm = self.xres, self.sm
        rows, winr, gates, q = self.rows, self.winr, self.gates, self.f1
        qv, kv_, kw_ = self.nsa_front(n, self.cosT[0:n, tau, :], self.sinT[0:n, tau, :], self.cosT.b)
        S.dma("pool", self.p_nsa_rows[s, tau * 128:tau * 128 + n, :], rows[0:n, :], reads=[rows.b], writes=[self.b_p_nsa_rows])
        w0 = self.n_ptiles - 4
        if tau >= w0:
            S.dma("pool", self.p_nsa_win[s, (tau - w0) * 128:(tau - w0) * 128 + n, :], winr[0:n, :], reads=[winr.b],
                  writes=[self.b_p_nsa_win])
        big = self.big
        self.nsa_q_stage(n, qv)
        self.cp(big[0:n, 2048:2560].rearrange("p (g c d) -> p g c d", g=4, c=2), big.b,
                rows[0:n, 0:512].rearrange("p (c g d) -> p g c d", c=2, g=4), rows.b, eng="pool")
        kst = self.hn[0:n, 0:512].rearrange("p (g c d) -> p g c d", g=4, c=2)
        self.cp(kst[:, :, 0, :], self.hn.b, kv_, rows.b)
        self.cp(kst[:, :, 1, :], self.hn.b, kw_, winr.b)
        self.cp(self.VS1[0:n, tau, :, 0:64], self.VS1.b, rows[0:n, 768:1024].rearrange("p (g d) -> p g d", g=4), rows.b, eng="pool")
        self.cp(self.VW1[0:n, tau % 5, :, 0:64], self.VW1.b, winr[0:n, 256:512].rearrange("p (g d) -> p g d", g=4), winr.b, eng="pool")
        qT2 = self.qT2
        self.transpose_fm(big, n, 4, self.cT, width=128, col0=2048, dview=self.cT[:, :, 0:n])
        self.transpose_fm(self.hn, n, 4, self.kcache, width=128, col0=0, dview=self.kcache[:, :, tau * 128:tau * 128 + n])
        Aall = self.Aall
        ps = self.psum()
        for ab in range(2):
            ov = ps[:, ab * 32:(ab + 1) * 32].rearrange("p (g m) -> p g m", g=4)
            for j in range(16):
                self.mm(ov, ps.b, self.w1[:, ab, j, :], self.w1.b, self.cT[:, :, j:128:16], self.cT.b, start=(j == 0), stop=(j == 15))
        for ab in range(2):
            ov = ps[:, ab * 32:(ab + 1) * 32].rearrange("p (g m) -> p g m", g=4)
            self.ts(Aall[:, ab, :, 8 * tau:8 * tau + 8], Aall.b, ov, ps.b, self.cAB[:, ab:ab + 1], ALU.add, rd=[self.cAB.b])
        self.compress_finish(1)
        self.memset(self.maskc[:], self.maskc.b, 0.0)
        S.op("pool", lambda e: e.affine_select(out=self.maskc[:], in_=self.maskc[:], pattern=[[1, 128]], compare_op=ALU.is_ge,
                                              fill=self.fillreg(NEGM), base=128 * tau - 31, channel_multiplier=-16),
             reads=[self.maskc.b], writes=[self.maskc.b])
        pT, oacc, imp = self.pT, self.oacc, self.imp
        gv = gates[0:n, :].rearrange("p (h k) -> p h k", k=3)
        for g in range(4):
            ps = self.psum()
            pv4 = ps[:, 0:4 * n].rearrange("p (h t) -> p h t", h=4)
            self.mm(pv4, ps.b, self.kcT[:, g, :], self.kcT.b, qT2[:, 0, 4 * g:4 * g + 4, 0:n], qT2.b, start=True, stop=False)
            self.mm(pv4, ps.b, self.ident[:], self.ident.b, self.maskc[:, 0:n].unsqueeze(1).to_broadcast([128, 4, n]), self.maskc.b,
                    start=False, stop=True)
            self.act(pT[:, 0, 0:4 * n], pT.b, ps[:, 0:4 * n], ps.b, AF.Exp)
            ps2 = self.psum()
            for hh in range(4):
                self.mm(ps2[0:n, hh * 97:(hh + 1) * 97], ps2.b, pT[:, 0, hh * n:(hh + 1) * n], pT.b, self.VC1[:, g, :], self.VC1.b)
            pv = ps2[0:n, 0:388].rearrange("p (h c) -> p h c", h=4)
            self.ts(sm[0:n, 208:212], sm.b, pv[:, :, 64], ps2.b, 1e-20, ALU.max)
            self.recip(sm[0:n, 212:216], sm.b, sm[0:n, 208:212], sm.b)
            self.tt(sm[0:n, 216:220], sm.b, sm[0:n, 212:216], sm.b, gv[:, 4 * g:4 * g + 4, 0], gates.b, ALU.mult)
            self.tt(oacc[0:n, g * 256:(g + 1) * 256].rearrange("p (h d) -> p h d", h=4), oacc.b, pv[:, :, 0:64], ps2.b,
                    sm[0:n, 216:220].unsqueeze(2).to_broadcast([n, 4, 64]), sm.b, ALU.mult)
            jv = self.junk[0:n, 0:128].rearrange("p (h s) -> p h s", h=4)
            self.tt(jv, self.junk.b, pv[:, :, 65:97], ps2.b, sm[0:n, 212:216].unsqueeze(2).to_broadcast([n, 4, 32]), sm.b, ALU.mult)
            self.red(imp[0:n, g, :], imp.b, jv.rearrange("p h s -> p s h"), self.junk.b)
        FH = self.selFH
        S.op("pool", lambda e: e.memset(FH[:, 0, :], 1e9), writes=[FH.b])
        S.op("pool", lambda e: e.memset(FH[:, 1, :], 3e38), writes=[FH.b])
        for half in range(2):
            vF = FH[64 * half:64 * half + 64, 0, :]
            vH = FH[64 * half:64 * half + 64, 1, :]
            cur = 2 * tau + half
            S.op("pool", lambda e: e.affine_select(out=vF, in_=vF, pattern=[[1, 32]], compare_op=ALU.is_equal, fill=self.fillreg(0.0),
                                                  base=-cur, channel_multiplier=0), reads=[FH.b], writes=[FH.b])
            S.op("pool", lambda e: e.affine_select(out=vH, in_=vH, pattern=[[-1, 32]], compare_op=ALU.is_ge, fill=self.fillreg(-1e30),
                                                  base=cur, channel_multiplier=0), reads=[FH.b], writes=[FH.b])
        S.op("pool", lambda e: e.memset(FH[:, 0, 0:1], 1e9), reads=[FH.b], writes=[FH.b])
        self.select_topk(n, FH[0:n, 0, :], FH[0:n, 1, :], 32)
        for g in range(4):
            for kap in range(tau + 1):
                ps = self.psum()
                pv4 = ps[:, 0:4 * n].rearrange("p (h t) -> p h t", h=4)
                self.mm(pv4, ps.b, self.kcache[:, g, kap * 128:(kap + 1) * 128], self.kcache.b, qT2[:, 0, 4 * g:4 * g + 4, 0:n], qT2.b,
                        start=True, stop=False)
                self.mm(pv4, ps.b, self.Eall[:, kap, :], self.Eall.b, self.selT[:, g, 0:n].unsqueeze(1).to_broadcast([128, 4, n]),
                        self.selT.b, start=False, stop=(kap != tau))
                if kap == tau:
                    self.mm(pv4, ps.b, self.ident[:], self.ident.b, self.causal_neg[:, 0:n].unsqueeze(1).to_broadcast([128, 4, n]),
                            self.causal_neg.b, start=False, stop=True)
                self.act(pT[:, kap, 0:4 * n], pT.b, ps[:, 0:4 * n], ps.b, AF.Exp)
            self.attn_pv(g, n, range(tau + 1), self.VS1, 1, first=False)
        for g in range(4):
            k0 = max(0, tau - 4)
            for kap in range(k0, tau + 1):
                ps = self.psum()
                pv4 = ps[:, 0:4 * n].rearrange("p (h t) -> p h t", h=4)
                extra = []
                if kap == tau:
                    extra.append(self.causal_neg)
                if kap == tau - 4:
                    extra.append(self.win_neg)
                self.mm(pv4, ps.b, self.kcache[:, g, kap * 128:(kap + 1) * 128], self.kcache.b, qT2[:, 1, 4 * g:4 * g + 4, 0:n], qT2.b,
                        start=True, stop=(len(extra) == 0))
                for i, m in enumerate(extra):
                    self.mm(pv4, ps.b, self.ident[:], self.ident.b, m[:, 0:n].unsqueeze(1).to_broadcast([128, 4, n]), m.b,
                            start=False, stop=(i == len(extra) - 1))
                self.act(pT[:, kap, 0:4 * n], pT.b, ps[:, 0:4 * n], ps.b, AF.Exp)
            self.attn_pv(g, n, range(k0, tau + 1), self.VW1, 2, first=False, slot=lambda k: k % 5)
        self.cp(big[0:n, 0:1024], big.b, oacc[0:n, :], oacc.b, eng="act")
        self.transpose_fm(big, n, 8, self.bigT)
        self.linear_tm(self.bigT, n, self.W_out1, 1024, 0, 1024,
                       lambda o, b, ps, pb_: self.tt(x[0:n, o:o + b], x.b, x[0:n, o:o + b], x.b, ps, pb_, ALU.add))

    def attn_pv(self, g, n, kaps, V, gi, first, slot=lambda k: k):
        sm, pT, oacc = self.sm, self.pT, self.oacc
        gv = self.gates[0:n, :].rearrange("p (h k) -> p h k", k=3)
        kaps = list(kaps)
        ps2 = self.psum()
        for hh in range(4):
            for i, kap in enumerate(kaps):
                self.mm(ps2[0:n, hh * 65:(hh + 1) * 65], ps2.b, pT[:, kap, hh * n:(hh + 1) * n], pT.b, V[:, slot(kap), g, :], V.b,
                        start=(i == 0), stop=(i == len(kaps) - 1))
        pv = ps2[0:n, 0:260].rearrange("p (h c) -> p h c", h=4)
        self.ts(sm[0:n, 208:212], sm.b, pv[:, :, 64], ps2.b, 1e-20, ALU.max)
        self.recip(sm[0:n, 212:216], sm.b, sm[0:n, 208:212], sm.b)
        self.tt(sm[0:n, 216:220], sm.b, sm[0:n, 212:216], sm.b, gv[:, 4 * g:4 * g + 4, gi], self.gates.b, ALU.mult)
        ov = oacc[0:n, g * 256:(g + 1) * 256].rearrange("p (h d) -> p h d", h=4)
        gb = sm[0:n, 216:220].unsqueeze(2).to_broadcast([n, 4, 64])
        if first:
            self.tt(ov, oacc.b, pv[:, :, 0:64], ps2.b, gb, sm.b, ALU.mult)
        else:
            jv = self.junk[0:n, 0:256].rearrange("p (h d) -> p h d", h=4)
            self.tt(jv, self.junk.b, pv[:, :, 0:64], ps2.b, gb, sm.b, ALU.mult)
            self.tt(ov, oacc.b, ov, oacc.b, jv, self.junk.b, ALU.add)

    def compress_finish(self, nchunk):
        Aall, pre = self.Aall, self.pre
        jv = self.junk[:, 0:512].rearrange("p (g t) -> p g t", g=4)
        self.tt(jv[:, :, 0:127], self.junk.b, Aall[:, 0, :, 0:127], Aall.b, Aall[:, 1, :, 1:128], Aall.b, ALU.add)
        self.act(pre[:, :, 0:127], pre.b, jv[:, :, 0:127], self.junk.b, AF.Silu)
        ps = self.psum()
        for g in range(4):
            self.mm(ps[:, g * 64:(g + 1) * 64], ps.b, pre[0:64, g, :], pre.b, self.w2[0:64, :], self.w2.b)
        kc = self.f2[:, 512:768]
        self.cp(kc, self.f2.b, ps[:, 0:256], ps.b, eng="act")
        kcv = kc.rearrange("p (g d) -> p g d", g=4)
        self.headnorm(kcv, self.f2.b, 128, 4, self.nsak_g[:, 0, :], self.nsak_g.b, 200)
        self.rope(kcv, self.f2.b, 128, 4, self.cosC[:, 0, :], self.sinC[:, 0, :], self.cosC.b)
        self.cp(self.hn[:, 0:256], self.hn.b, kc, self.f2.b)
        self.transpose_fm(self.hn, 128, 4, self.kcT, width=64, dview=self.kcT[0:64, :, 0:128])
        ps = self.psum()
        for g in range(4):
            self.mm(ps[:, g * 64:(g + 1) * 64], ps.b, pre[64:128, g, :], pre.b, self.w2[64:128, :], self.w2.b)
        self.cp(self.VC1[:, :, 0:64], self.VC1.b, ps[:, 0:256].rearrange("p (g d) -> p g d", g=4), ps.b)

    def select_topk(self, n, F, H, nslc):
        sm, imp = self.sm, self.imp
        iv = imp[0:n, :, 0:nslc]
        self.tt(iv, imp.b, iv, imp.b, F.unsqueeze(1).to_broadcast([n, 4, nslc]), self.selFH.b, ALU.max)
        self.tt(iv, imp.b, iv, imp.b, H.unsqueeze(1).to_broadcast([n, 4, nslc]), self.selFH.b, ALU.min)
        work = self.junk
        for g in range(4):
            S = self.S
            S.op("dve", lambda e: e.max(out=sm[0:n, 224:232], in_=imp[0:n, g, 0:nslc]), reads=[imp.b], writes=[sm.b])
            S.op("dve", lambda e: e.match_replace(out=work[0:n, 0:nslc], in_to_replace=sm[0:n, 224:232],
                                                  in_values=imp[0:n, g, 0:nslc], imm_value=-3e38),
                 reads=[imp.b, sm.b], writes=[work.b])
            S.op("dve", lambda e: e.max(out=sm[0:n, 232:240], in_=work[0:n, 0:nslc]), reads=[work.b], writes=[sm.b])
            self.ts(work[0:n, 64:64 + nslc], work.b, imp[0:n, g, 0:nslc], imp.b, sm[0:n, 239:240], ALU.is_ge, rd=[sm.b])
            self.stt(work[0:n, 128:128 + nslc], work.b, imp[0:n, g, 0:nslc], imp.b, -5e29, work[0:n, 64:64 + nslc], work.b,
                     ALU.is_gt, ALU.mult)
            self.ts(self.hn[0:n, 256 + g * nslc:256 + (g + 1) * nslc], self.hn.b, work[0:n, 128:128 + nslc], work.b, -1.0, ALU.add)
        pb = self.psumb()
        for g in range(4):
            self.tr(pb[0:nslc, g * n:(g + 1) * n], pb.b, self.hn[0:n, 256 + g * nslc:256 + (g + 1) * nslc], self.hn.b)
        self.cp(self.selT[0:nslc, :, 0:n], self.selT.b, pb[0:nslc, 0:4 * n].rearrange("p (g t) -> p g t", g=4), pb.b, eng="act")


    def sample_seq(self, b):
        S = self.S
        n = TS
        self.cur_prompt = False
        ms = self.memset
        ubuf, xbuf, hs, hsb = self.ubuf, self.xbuf, self.hstate, self.hstate_bf
        with self.nc.allow_non_contiguous_dma("tiny state loads"):
            for k in range(2):
                S.dma("pool", ubuf[:, :, k], self.state_sconv[b, k].rearrange("(c p) -> p c", p=128), writes=[ubuf.b])
            for k in range(3):
                S.dma("pool", xbuf[:, :, k], self.state_ssd_conv[b, k].rearrange("(c p) -> p c", p=128), writes=[xbuf.b])
        for c in range(8):
            t = self.f2
            S.dma("pool", t[:, 0:128], self.state_ssd[b, c * 128:(c + 1) * 128, :], writes=[t.b])
            ps = self.psum()
            S.op("pe", lambda e: e.transpose(ps[:, 0:128], t[:, 0:128], self.identf[:]), reads=[t.b, self.identf.b], writes=[ps.b])
            self.cp(hs[:, c * 128:(c + 1) * 128], hs.b, ps[:, 0:128], ps.b)
        self.cp(hsb[:], hsb.b, hs[:], hs.b, eng="act")
        for l in range(2):
            for mc in range(2):
                t = self.f1
                S.dma("pool", t[:], self.cache_mem_kv[l, b, mc * 128:(mc + 1) * 128, :], writes=[t.b])
                self.cp(self.big[:, 0:512], self.big.b, t[:, 0:512], t.b, eng="act")
                pb = self.psumb()
                for h in range(4):
                    self.tr(pb[:, h * 128:(h + 1) * 128], pb.b, self.big[:, h * 128:(h + 1) * 128], self.big.b)
                self.cp(self.KT[l][:, :, mc * 128:(mc + 1) * 128], self.KT[l].b, pb[:, 0:512].rearrange("p (h t) -> p h t", h=4), pb.b)
                self.cp(self.V1[l][:, mc, :, 0:128], self.V1[l].b, t[:, 512:1024].rearrange("p (h d) -> p h d", h=4), t.b, eng="pool")
        x = self.xres
        S.dma("pool", x[0:n, :], self.x_sample[b], writes=[x.b])
        self.layer0(b, 0, n, last=True)
        self.mem_attn(0, n)
        self.ffn(0, n)
        self.layer1_sample(b)
        self.mem_attn(1, n)
        self.ffn(1, n)
        S.dma("pool", self.y_sample[b], x[0:n, :], reads=[x.b], writes=[self.b_y_sample])

    def layer1_sample(self, b):
        S = self.S
        n = TS
        x, sm = self.xres, self.sm
        rows, winr, gates, q = self.rows, self.winr, self.gates, self.f1
        qv, kv_, kw_ = self.nsa_front(n, self.cosS[0:n, 0, :], self.sinS[0:n, 0, :], self.cosS.b)
        S.dma("pool", self.s_nsa_rows[b], rows[0:n, :], reads=[rows.b], writes=[self.b_s_nsa_rows])
        S.dma("pool", self.s_nsa_win[b, 0:504, :], self.cache_nsa_win[b, 8:512, :], writes=[self.b_s_nsa_win])
        S.dma("pool", self.s_nsa_win[b, 504:512, :], winr[0:n, :], reads=[winr.b], writes=[self.b_s_nsa_win])
        self.nsa_q_stage(n, qv)
        kst = self.hn[0:n, 0:512].rearrange("p (g c d) -> p g c d", g=4, c=2)
        self.cp(kst[:, :, 0, :], self.hn.b, kv_, rows.b)
        self.cp(kst[:, :, 1, :], self.hn.b, kw_, winr.b)
        S.barrier()
        pTf = self.pT.t[:].rearrange("p a b -> p (a b)")
        o = [0]

        def carve(name, cols, shape_str=None, **kw):
            v = pTf[:, o[0]:o[0] + cols]
            o[0] += cols
            if shape_str:
                v = v.rearrange(shape_str, **kw)
            return TT(v, name)
        eT = carve("eT", 512, "p (c t) -> p c t", c=16)
        pgb = [carve("pgb%d" % i, 512, "p (g c d) -> p g c d", g=4, c=2) for i in range(2)]
        kst2 = [carve("kst2_%d" % i, 512, "p (g c d) -> p g c d", g=4, c=2) for i in range(2)]
        kTp = [carve("kTp%d" % i, 512, "p (g t) -> p g t", g=4) for i in range(2)]
        pTs = [carve("pTs%d" % i, 128) for i in range(4)]
        VC1s = carve("VC1s", 8 * 4 * 65, "p (c g d) -> p c g d", c=8, g=4)
        selTs = carve("selTs", 256, "p (c g t) -> p c g t", c=8, g=4)
        kTn = carve("kTn", 32, "p (g t) -> p g t", g=4)
        eTg = carve("eTg", 64, "p (c t) -> p c t", c=8)
        cTs = [carve("cTs%d" % i, 576, "p (g t) -> p g t", g=4) for i in range(2)]
        pre_all = TT(self.kcache.t[:].rearrange("p g t -> p (g t)")[:, 0:4096].rearrange("p (g t) -> p g t", g=4), "pre_all")
        kcTs = TT(self.VS1.t[:].rearrange("p a g d -> p (a g d)")[:, 0:4096].rearrange("p (g t) -> p g t", g=4), "kcTs")
        Vpg = [TT(self.VW1.t[:, i], "Vpg%d" % i) for i in (0, 1, 4)]
        Vn_s = TT(self.VW1.t[:, 2], "Vn_s")
        Vn_w = TT(self.VW1.t[:, 3], "Vn_w")
        xbf = self.xbuf.t[:].rearrange("p c t -> p (c t)")
        pg = [TT(self.hstate.t[:, i * 512:(i + 1) * 512], "pg%d" % i) for i in range(2)] + \
             [TT(xbf[:, i * 512:(i + 1) * 512], "pgx%d" % i) for i in range(4)]
        NPG = len(pg)
        Af = self.Aall.t[:].rearrange("p a c d -> p (a c d)")
        otm = TT(Af[0:8, 0:1024], "otm")
        cbf = self.cbm.t[:].rearrange("p a b -> p (a b)")
        abc = TT(cbf[:, 0:72].rearrange("p (a g m) -> p a g m", a=2, g=4), "abc")
        Fs = TT(cbf[0:8, 128:384], "Fs")
        acc = TT(self.rmask.t[0:32].rearrange("p a b -> p (a b)")[:, 0:260], "acc")
        imp_s = TT(self.f1.t[0:8, 0:1024].rearrange("p (g s) -> p g s", g=4), "imp_s", self.f1.b)
        idxA = TT(self.seg.t[:].rearrange("p a b -> p (a b)")[:, 0:128].bitcast(I32), "idxA")
        idxB = TT(self.seg.t[:].rearrange("p a b -> p (a b)")[:, 128:256].bitcast(I32), "idxB")
        ptf = TT(self.seg.t[:].rearrange("p a b -> p (a b)")[:, 256:384], "ptf")
        cache2 = self.cache_nsa_kv.rearrange("r (h c) -> (r h) c", h=2)
        ms = self.memset
        it = self.itmp
        S.dma("pool", it[:, 0:128], self.page_table[b:b + 1, :].partition_broadcast(128), writes=[it.b])
        self.cp(ptf[:], ptf.b, it[:, 0:128], it.b)
        self.ts(ptf[:], ptf.b, ptf[:], ptf.b, 256.0, ALU.mult, self.piota2[:, 0:1], ALU.add, rd=[self.piota2.b])
        self.cp(idxA[:], idxA.b, ptf[:], ptf.b)
        self.ts(ptf[:], ptf.b, ptf[:], ptf.b, 1.0, ALU.add)
        self.cp(idxB[:], idxB.b, ptf[:], ptf.b)
        self.transpose_fm(self.hn, n, 4, kTn, width=128, dview=kTn[:, :, 0:n])
        self.cp(Vn_s[0:n, :, 0:64], Vn_s.b, rows[0:n, 768:1024].rearrange("p (g d) -> p g d", g=4), rows.b)
        self.cp(Vn_w[0:n, :, 0:64], Vn_w.b, winr[0:n, 256:512].rearrange("p (g d) -> p g d", g=4), winr.b)
        ms(pre_all[:], pre_all.b, 0.0)
        ms(kcTs[:], kcTs.b, 0.0)
        ms(VC1s[:], VC1s.b, 1.0)
        ms(selTs[:], selTs.b, 0.0)

        def gather(dst, idx, j):
            S.dma("pool", None, None, fn=lambda e: e.indirect_dma_start(
                out=dst[:], out_offset=None, in_=cache2, in_offset=bass.IndirectOffsetOnAxis(ap=idx[:, j:j + 1], axis=0)),
                reads=[idx.b], writes=[dst.b])
        for i in range(2):
            ms(cTs[i][:], cTs[i].b, 0.0)
        cb2 = self.junk[:, 1000:1001]
        self.tt(cb2, self.junk.b, self.cAB[:, 0:1], self.cAB.b, self.cAB[:, 1:2], self.cAB.b, ALU.add)
        csum = TT(cbf[:, 100:101], "csum")
        self.cp(csum[:], csum.b, cb2, self.junk.b)
        cT3 = cTs + [TT(self.big.t[:, 0:576].rearrange("p (g t) -> p g t", g=4), "cTs2", self.big.b)]
        ms(cT3[2][:], cT3[2].b, 0.0)

        def p1_a(j):
            p_ = pg[j % NPG]
            gather(p_, idxA, j)
            pb_ = pgb[j % 2]
            self.cp(pb_[:], pb_.b, p_[:].rearrange("p (c g d) -> p g c d", c=2, g=4), p_.b, eng=("act" if j % 2 else "dve"))
            ct = cT3[j % 3]
            pbk = self.psumb()
            for g in range(4):
                self.tr(pbk[:, g * 128:(g + 1) * 128], pbk.b, pb_[:, g].rearrange("p c d -> p (c d)"), pb_.b)
            self.cp(ct[:, :, 16:144], ct.b, pbk[:, 0:512].rearrange("p (g t) -> p g t", g=4), pbk.b, eng=("dve" if j % 2 else "act"))
            nxt = cT3[(j + 1) % 3]
            self.cp(nxt[:, :, 0:16], nxt.b, ct[:, :, 128:144], ct.b, eng=("dve" if j % 2 else "act"))

        def p1_b(j):
            ct = cT3[j % 3]
            ps = self.psum()
            ov = ps[:, 0:32].rearrange("p (g m) -> p g m", g=4)
            for jj in range(16):
                self.mm(ov, ps.b, self.w1[:, 0, jj, :], self.w1.b, ct[:, :, jj:128:16], ct.b, start=(jj == 0), stop=False)
            for jj in range(16):
                self.mm(ov, ps.b, self.w1[:, 1, jj, :], self.w1.b, ct[:, :, 16 + jj:144:16], ct.b, start=False, stop=(jj == 15))
            m0 = 1 if j == 0 else 0
            self.act(pre_all[:, :, 8 * j - 1 + m0:8 * j + 7], pre_all.b, ov[:, :, m0:8], ps.b, AF.Silu, bias=csum[:, 0:1], rd=[csum.b])
        for it in range(NPAGES + 1):
            if it < NPAGES:
                p1_a(it)
            if it >= 1:
                p1_b(it - 1)
        for c in range(8):
            ps = self.psum()
            for g in range(4):
                self.mm(ps[:, g * 64:(g + 1) * 64], ps.b, pre_all[0:64, g, c * 128:(c + 1) * 128], pre_all.b, self.w2[0:64, :], self.w2.b)
            kc = self.f2[:, 512:768]
            self.cp(kc, self.f2.b, ps[:, 0:256], ps.b, eng="act")
            kcv = kc.rearrange("p (g d) -> p g d", g=4)
            self.headnorm(kcv, self.f2.b, 128, 4, self.nsak_g[:, 0, :], self.nsak_g.b, 200)
            self.rope(kcv, self.f2.b, 128, 4, self.cosC[:, c, :], self.sinC[:, c, :], self.cosC.b)
            self.cp(self.hn[:, 512:768], self.hn.b, kc, self.f2.b)
            self.transpose_fm(self.hn, 128, 4, kcTs, width=64, col0=512, dview=kcTs[0:64, :, c * 128:(c + 1) * 128])
            ps = self.psum()
            for g in range(4):
                self.mm(ps[:, g * 64:(g + 1) * 64], ps.b, pre_all[64:128, g, c * 128:(c + 1) * 128], pre_all.b, self.w2[64:128, :], self.w2.b)
            self.cp(VC1s[:, c, :, 0:64], VC1s.b, ps[:, 0:256].rearrange("p (g d) -> p g d", g=4), ps.b)
        ms(self.maskc[:], self.maskc.b, 0.0)
        S.op("pool", lambda e: e.affine_select(out=self.maskc[:, 0:8], in_=self.maskc[:, 0:8], pattern=[[1, 8]], compare_op=ALU.is_ge,
                                              fill=self.fillreg(NEGM), base=2017, channel_multiplier=-16),
             reads=[self.maskc.b], writes=[self.maskc.b])
        qT2 = self.qT2
        savedpT = self.pT
        self.pT = eT
        for g in range(4):
            for c in range(8):
                ps = self.psum()
                pv4 = ps[:, 0:4 * n].rearrange("p (h t) -> p h t", h=4)
                self.mm(pv4, ps.b, kcTs[:, g, c * 128:(c + 1) * 128], kcTs.b, qT2[:, 0, 4 * g:4 * g + 4, 0:n], qT2.b, start=True, stop=(c != 7))
                if c == 7:
                    self.mm(pv4, ps.b, self.ident[:], self.ident.b, self.maskc[:, 0:n].unsqueeze(1).to_broadcast([128, 4, n]),
                            self.maskc.b, start=False, stop=True)
                self.act(eT[:, c, 0:4 * n], eT.b, ps[:, 0:4 * n], ps.b, AF.Exp)
            self.attn_pv(g, n, range(8), VC1s, 0, first=True)
            ps3 = self.psum()
            for c in range(8):
                self.mm(ps3[:, 0:32], ps3.b, self.ones_bf[:], self.ones_bf.b, eT[:, c, 0:32], eT.b, start=(c == 0), stop=(c == 7))
            rdb = self.junk[:, 0:32]
            self.ts(rdb, self.junk.b, ps3[:, 0:32], ps3.b, 1e-20, ALU.max)
            self.recip(rdb, self.junk.b, rdb, self.junk.b)
            self.tt(eT[:, 8:16, 0:32], eT.b, eT[:, 0:8, 0:32], eT.b, rdb.unsqueeze(1).to_broadcast([128, 8, 32]), self.junk.b, ALU.mult)
            etf = self.junk[:, 64:128].rearrange("p (c t) -> p c t", c=8)
            self.red(etf, self.junk.b, eT[:, 8:16, 0:32].rearrange("p c (h t) -> p c t h", h=4), eT.b)
            self.cp(eTg[:], eTg.b, etf, self.junk.b)
            ps4 = self.psum()
            for c in range(8):
                self.mm(ps4[0:n, c * 33:(c + 1) * 33], ps4.b, eTg[:, c, :], eTg.b, self.mov33[:], self.mov33.b)
            u = ps4[0:n, 0:264].rearrange("p (c s) -> p c s", c=8)
            self.cp(imp_s[:, g, 0:256].rearrange("p (c s) -> p c s", c=8), imp_s.b, u[:, :, 0:32], ps4.b)
            self.tt(imp_s[:, g, 32:256:32], imp_s.b, imp_s[:, g, 32:256:32], imp_s.b, u[:, 0:7, 32], ps4.b, ALU.add)
        self.pT = savedpT
        ms(Fs[:], Fs.b, 0.0)
        ms(Fs[:, 0:1], Fs.b, 1e9)
        work = self.junk
        stage = self.big
        for g in range(4):
            iv = imp_s[:, g, 0:256]
            self.tt(iv, imp_s.b, iv, imp_s.b, Fs[:], Fs.b, ALU.max)
            S.op("dve", lambda e: e.max(out=sm[0:n, 224:232], in_=iv), reads=[imp_s.b], writes=[sm.b])
            S.op("dve", lambda e: e.match_replace(out=work[0:n, 0:256], in_to_replace=sm[0:n, 224:232], in_values=iv, imm_value=-3e38),
                 reads=[imp_s.b, sm.b], writes=[work.b])
            S.op("dve", lambda e: e.max(out=sm[0:n, 232:240], in_=work[0:n, 0:256]), reads=[work.b], writes=[sm.b])
            self.ts(work[0:n, 512:768], work.b, iv, imp_s.b, sm[0:n, 238:239], ALU.is_ge, rd=[sm.b])
            self.ts(stage[0:n, g * 256:(g + 1) * 256], stage.b, work[0:n, 512:768], work.b, -1.0, ALU.add)
        for g in range(4):
            pb = self.psumb()
            for c8 in range(8):
                self.tr(pb[0:32, c8 * n:(c8 + 1) * n], pb.b, stage[0:n, g * 256 + c8 * 32:g * 256 + (c8 + 1) * 32], stage.b)
            self.cp(selTs[0:32, :, g, :], selTs.b, pb[0:32, 0:8 * n].rearrange("p (c t) -> p c t", c=8), pb.b, eng="act")

        def key_tile(kT, nk, qsel, V, masks, i):
            key_tile_c(key_tile_b(kT, nk, qsel, masks, i), nk, V)

        def key_tile_b(kT, nk, qsel, masks, i):
            ps = self.psum()
            for g in range(4):
                pv4 = ps[0:nk, g * 32:(g + 1) * 32].rearrange("p (h t) -> p h t", h=4)
                self.mm(pv4, ps.b, kT[:, g, 0:nk], kT.b, qT2[:, qsel, 4 * g:4 * g + 4, 0:n], qT2.b, start=True, stop=(len(masks) == 0))
                for mi, (ml, mlb, mr, mrb, per_g) in enumerate(masks):
                    rhs = mr(g) if per_g else mr
                    self.mm(pv4, ps.b, ml, mlb, rhs, mrb, start=False, stop=(mi == len(masks) - 1))
            pt_ = pTs[i % 4]
            self.act(pt_[0:nk, :], pt_.b, ps[0:nk, 0:128], ps.b, AF.Exp)
            return pt_

        def key_tile_c(pt_, nk, V):
            ps2 = self.psum()
            for g in range(4):
                self.mm(ps2[0:32, g * 65:(g + 1) * 65], ps2.b, pt_[0:nk, g * 32:(g + 1) * 32], pt_.b, V[0:nk, g, :], V.b)
            self.tt(acc[:], acc.b, acc[:], acc.b, ps2[0:32, 0:260], ps2.b, ALU.add)

        def branch_finish(gi):
            av = acc[:].rearrange("p (g c) -> p g c", g=4)
            self.ts(sm[0:32, 208:212], sm.b, av[:, :, 64], acc.b, 1e-20, ALU.max)
            self.recip(sm[0:32, 212:216], sm.b, sm[0:32, 208:212], sm.b)
            onb = self.junk[0:32, 0:256].rearrange("p (g d) -> p g d", g=4)
            self.tt(onb, self.junk.b, av[:, :, 0:64], acc.b, sm[0:32, 212:216].unsqueeze(2).to_broadcast([32, 4, 64]), sm.b, ALU.mult)
            ov = otm[:].rearrange("p (g h d) -> p g h d", g=4, h=4)
            for hh in range(4):
                S.dma("pool", ov[:, :, hh, :], onb[hh * 8:(hh + 1) * 8], reads=[self.junk.b], writes=[otm.b])
            gv = gates[0:n, :].rearrange("p (h k) -> p h k", k=3)
            o3 = otm[:].rearrange("p (h d) -> p h d", h=16)
            self.tt(o3, otm.b, o3, otm.b, gv[:, :, gi:gi + 1].to_broadcast([n, 16, 64]), gates.b, ALU.mult)
            self.tt(self.oacc[0:n, :], self.oacc.b, self.oacc[0:n, :], self.oacc.b, otm[:], otm.b, ALU.add)

        idb, cnb = self.ident, self.causal_neg
        new_masks = [(idb[:, 0:n], idb.b, cnb[:, 0:n].unsqueeze(1).to_broadcast([128, 4, n]), cnb.b, False)]
        kst4 = kst2 + pgb
        kT4 = kTp + [TT(c.t[:, :, 0:128], c.b.name + "v", c.b) for c in cTs]
        ms(acc[:], acc.b, 0.0)
        pts = {}

        def stage_a(kap):
            p_ = pg[kap % NPG]
            gather(p_, idxB, kap)
            ks = kst4[kap % 4]
            self.cp(ks[:], ks.b, p_[:].rearrange("p (c g d) -> p g c d", c=2, g=4), p_.b, eng=("act" if kap % 2 else "dve"))
            V = Vpg[kap % 3]
            self.cp(V[:, :, 0:64], V.b, p_[:, 256:512].rearrange("p (g d) -> p g d", g=4), p_.b, eng=("dve" if kap % 2 else "act"))
            kT = kT4[kap % 4]
            pb = self.psumb()
            for g in range(4):
                self.tr(pb[:, g * 128:(g + 1) * 128], pb.b, ks[:, g].rearrange("p c d -> p (c d)"), ks.b)
            self.cp(kT[:], kT.b, pb[:, 0:512].rearrange("p (g t) -> p g t", g=4), pb.b, eng=("dve" if kap % 2 else "act"))

        def stage_b(kap):
            msk = [(self.Eall[:, kap % 16, :], self.Eall.b,
                    (lambda g, kap=kap: selTs[:, kap // 16, g, :].unsqueeze(1).to_broadcast([128, 4, n])), selTs.b, True)]
            pts[kap] = key_tile_b(kT4[kap % 4], 128, 0, msk, kap)

        def stage_c(kap):
            key_tile_c(pts.pop(kap), 128, Vpg[kap % 3])
        for it in range(NPAGES + 2):
            if it < NPAGES:
                stage_a(it)
            if 0 <= it - 1 < NPAGES:
                stage_b(it - 1)
            if 0 <= it - 2 < NPAGES:
                stage_c(it - 2)
        key_tile(kTn, n, 0, Vn_s, new_masks, 0)
        branch_finish(1)
        ms(acc[:], acc.b, 0.0)
        wnb = self.win_neg
        for w in range(4):
            p_ = pg[w % 2]
            S.dma("pool", p_[:], self.cache_nsa_win[b, w * 128:(w + 1) * 128, :], writes=[p_.b])
            ks = kst2[w % 2]
            self.cp(ks[:], ks.b, p_[:].rearrange("p (c g d) -> p g c d", c=2, g=4), p_.b)
            V = Vpg[w % 2]
            self.cp(V[:, :, 0:64], V.b, p_[:, 256:512].rearrange("p (g d) -> p g d", g=4), p_.b, eng="pool")
            kT = kTp[w % 2]
            pb = self.psumb()
            for g in range(4):
                self.tr(pb[:, g * 128:(g + 1) * 128], pb.b, ks[:, g].rearrange("p c d -> p (c d)"), ks.b)
            self.cp(kT[:], kT.b, pb[:, 0:512].rearrange("p (g t) -> p g t", g=4), pb.b, eng="act")
            msk = []
            if w == 0:
                msk = [(idb[:], idb.b, wnb[:, 0:n].unsqueeze(1).to_broadcast([128, 4, n]), wnb.b, False)]
            key_tile(kT, 128, 0, V, msk, w)
        key_tile(kTn, n, 1, Vn_w, new_masks, 0)
        branch_finish(2)
        S.barrier()
        big = self.big
        self.cp(big[0:n, 0:1024], big.b, self.oacc[0:n, :], self.oacc.b, eng="act")
        self.transpose_fm(big, n, 8, self.bigT)
        self.linear_tm(self.bigT, n, self.W_out1, 1024, 0, 1024,
                       lambda o_, b_, ps, pb_: self.tt(x[0:n, o_:o_ + b_], x.b, x[0:n, o_:o_ + b_], x.b, ps, pb_, ALU.add))


def build_program(**kw):
    p = Prog(**kw)
    p.cur_prompt = True
    p.cl_b = None
    return p


_NAMES = ["y_prompt", "y_sample", "p_sconv", "p_ssd_conv", "p_ssd", "p_nsa_rows", "p_nsa_win", "p_mem_kv",
          "s_sconv", "s_ssd_conv", "s_ssd", "s_nsa_rows", "s_nsa_win"]


def shard_inputs(inp, c):
    f = np.ascontiguousarray
    m = {}
    m["x_prompt"] = f(inp["x_prompt"][NPB * c:NPB * (c + 1)])
    m["x_sample"] = f(inp["x_sample"][NSB * c:NSB * (c + 1)])
    m["mem_prompt"] = f(inp["mem_prompt"][NPB * c:NPB * (c + 1)])
    m["state_sconv"] = f(inp["state_sconv"][0, NSB * c:NSB * (c + 1)])
    m["state_ssd_conv"] = f(inp["state_ssd_conv"][0, NSB * c:NSB * (c + 1)])
    m["state_ssd"] = f(inp["state_ssd"][0, NSB * c:NSB * (c + 1)]).reshape(NSB, 1024, 128)
    m["cache_nsa_kv"] = inp["cache_nsa_kv"].reshape(-1, 1024)
    m["cache_nsa_win"] = f(inp["cache_nsa_win"][0, NSB * c:NSB * (c + 1)]).reshape(NSB, 512, 512)
    m["cache_mem_kv"] = f(inp["cache_mem_kv"][:, NSB * c:NSB * (c + 1)]).reshape(2, NSB, 256, 1024)
    m["page_table"] = f(inp["page_table"][NSB * c:NSB * (c + 1)])
    for k in ["norm_mix", "norm_mem", "norm_memsrc", "norm_ffn", "ab_ssd_conv_b", "ab_dt_bias", "ab_a_log", "ab_d",
              "ab_ssd_norm", "nsa_q_norm", "mem_w_q", "mem_q_norm", "mem_w_kv", "mem_k_norm", "mem_w_o",
              "ffn_w_gate", "ffn_w_up", "ffn_w_down"]:
        m[k] = f(inp[k])
    for k in ["ab_w_in", "ab_sconv_w", "ab_ssd_conv_w", "ab_w_out", "nsa_w_in", "nsa_k_norm", "nsa_cmp_pe", "nsa_cmp_w1",
              "nsa_cmp_w2", "nsa_w_out"]:
        m[k] = f(inp[k][0])
    return m


def gather_outputs(results):
    cat = lambda name, ax: np.concatenate([r[name] for r in results], axis=ax)
    y_prompt = cat("y_prompt", 0)
    y_sample = cat("y_sample", 0)
    p_sconv = cat("p_sconv", 0)[None]
    p_ssd_conv = cat("p_ssd_conv", 0)[None]
    p_ssd = cat("p_ssd", 0).reshape(1, 16, 16, 64, 128)
    p_rows = cat("p_nsa_rows", 0).reshape(16, SEQ, 1, 4, 4, 64)
    p_win = cat("p_nsa_win", 0).reshape(1, 16, 512, 2, 4, 64)
    p_mem = cat("p_mem_kv", 1).reshape(2, 16, 256, 2, 4, 128)
    s_sconv = cat("s_sconv", 0)[None]
    s_ssd_conv = cat("s_ssd_conv", 0)[None]
    s_ssd = cat("s_ssd", 0).reshape(1, 32, 16, 64, 128)
    s_rows = cat("s_nsa_rows", 0).reshape(32, TS, 1, 4, 4, 64)
    s_win = cat("s_nsa_win", 0).reshape(1, 32, 512, 2, 4, 64)
    return (y_prompt, y_sample, p_sconv, p_ssd_conv, p_ssd, p_rows, p_win, p_mem,
            s_sconv, s_ssd_conv, s_ssd, s_rows, s_win)


def kernel(**inputs):
    inp = {k: np.asarray(v) for k, v in inputs.items()}
    p = build_program()
    nc = p.build()
    in_maps = [shard_inputs(inp, c) for c in range(NCORES)]
    res = run_bass_kernel_spmd(nc, in_maps, core_ids=list(range(NCORES)))
    return gather_outputs(res.results)
```

```python
import math
from contextlib import ExitStack

import numpy as np
import concourse.bass as bass
import concourse.mybir as mybir
from concourse.bass_utils import run_bass_kernel_spmd

F32 = mybir.dt.float32
BF16 = mybir.dt.bfloat16
I32 = mybir.dt.int32
ALU = mybir.AluOpType
AF = mybir.ActivationFunctionType
AX = mybir.AxisListType

NCORES = 8
D = 1024
SEQ = 2048
NPB = 2
NSB = 4
TS = 8
PAST = 16384
NPAGES = 128
EPS = 1e-6
IN0 = 6160
IN1 = 2608
DFF = 2816
NEGM = -30000.0
THETA = 500000.0


class Buf:
    __slots__ = ("name", "w", "r", "excl")

    def __init__(self, name, excl=False):
        self.name = name
        self.w = None
        self.r = {}
        self.excl = excl


class TT:
    def __init__(self, t, name, b=None):
        self.t = t
        self.b = b if b is not None else Buf(name)

    def __getitem__(self, k):
        return self.t[k]


class Sched:
    def __init__(self, nc, es, same_engine_sync=True, n_dma_sems=10):
        self.nc = nc
        self.engs = {"pe": nc.tensor, "act": nc.scalar, "dve": nc.vector, "pool": nc.gpsimd, "sp": nc.sync}
        self.sem = {}
        self.cnt = {}
        for k in self.engs:
            self.sem[k] = es.enter_context(nc.semaphore("s_" + k))
            self.cnt[k] = 0
        self.dma_sems = {}
        self.dma_rr = {}
        for q in ("sp", "pool", "act"):
            self.dma_sems[q] = []
            for i in range(n_dma_sems):
                key = "d_%s%d" % (q, i)
                self.sem[key] = es.enter_context(nc.semaphore(key))
                self.cnt[key] = 0
                self.dma_sems[q].append(key)
            self.dma_rr[q] = 0
        self.waited = {k: {} for k in self.engs}
        self.same = same_engine_sync
        self.ninstr = 0

    def _wait(self, e, deps):
        eng = self.engs[e]
        for k, v in deps.items():
            if v <= 0:
                continue
            if k == e and (not self.same or e == "pe"):
                continue
            if self.waited[e].get(k, 0) >= v:
                continue
            eng.wait_ge(self.sem[k], v)
            self.waited[e][k] = v

    @staticmethod
    def _deps(reads, writes):
        deps = {}
        for b in reads:
            if b.w is not None and deps.get(b.w[0], 0) < b.w[1]:
                deps[b.w[0]] = b.w[1]
        for b in writes:
            if b.w is not None and deps.get(b.w[0], 0) < b.w[1]:
                deps[b.w[0]] = b.w[1]
            for k, v in b.r.items():
                if deps.get(k, 0) < v:
                    deps[k] = v
        return deps

    def op(self, e, fn, reads=(), writes=()):
        if any(b.excl for b in reads):
            writes = list(writes) + [b for b in reads if b.excl and b not in writes]
            reads = [b for b in reads if not b.excl]
        deps = self._deps(reads, writes)
        self._wait(e, deps)
        ins = fn(self.engs[e])
        self.cnt[e] += 1
        ins.then_inc(self.sem[e], 1)
        v = self.cnt[e]
        for b in reads:
            if b.r.get(e, 0) < v:
                b.r[e] = v
        for b in writes:
            b.w = (e, v)
            b.r = {}
        self.ninstr += 1
        return ins

    def dma(self, q, out, in_, reads=(), writes=(), fn=None, **kw):
        deps = self._deps(reads, writes)
        key = self.dma_sems[q][self.dma_rr[q] % len(self.dma_sems[q])]
        self.dma_rr[q] += 1
        if self.cnt[key] > 0:
            deps[key] = max(deps.get(key, 0), self.cnt[key])
        self._wait(q, deps)
        if fn is not None:
            ins = fn(self.engs[q])
        else:
            ins = self.engs[q].dma_start(out=out, in_=in_, **kw)
        self.cnt[key] += 16
        ins.then_inc(self.sem[key], 16)
        v = self.cnt[key]
        for b in reads:
            b.r[key] = v
        for b in writes:
            b.w = (key, v)
            b.r = {}
        self.ninstr += 1
        return ins

    def barrier(self):
        snap = {k: v for k, v in self.cnt.items() if v > 0}
        for e in self.engs:
            self._wait(e, dict(snap))

    def finish(self, bufs, e="sp"):
        self._wait(e, self._deps(bufs, ()))


class Prog:
    def __init__(self, do_sample=True, n_ptiles=16, debug=False, pool_rows=5120 * 128, stage=9, do_prompt=True, n_sample=NSB):
        self.pool_rows = pool_rows
        self.stage = stage
        self.do_prompt = do_prompt
        self.n_sample = n_sample
        self.same_sync = True
        self.do_sample = do_sample
        self.n_ptiles = n_ptiles
        self.debug = debug
        self.nc = bass.Bass("TRN2", target_bir_lowering=False)
        self.es = ExitStack()
        self.out_bufs = []

    def sb(self, name, shape, dt=F32):
        return TT(self.es.enter_context(self.nc.sbuf_tensor(name, list(shape), dt)), name)

    def din(self, name, shape, dt=F32):
        return self.nc.dram_tensor(name, list(shape), dt, kind="ExternalInput").ap()

    def dout(self, name, shape, dt=F32):
        ap = self.nc.dram_tensor(name, list(shape), dt, kind="ExternalOutput").ap()
        b = Buf(name)
        self.out_bufs.append(b)
        return ap, b

    def dscratch(self, name, shape, dt):
        return TT(self.nc.dram_tensor(name, list(shape), dt, kind="Internal").ap(), name)

    def fillreg(self, v):
        if not hasattr(self, "_fillregs"):
            self._fillregs = {}
        if v not in self._fillregs:
            self._fillregs[v] = self.nc.gpsimd.to_reg(float(v))
        return self._fillregs[v]

    def psum(self):
        i = self.ps_rr % len(self.psF)
        self.ps_rr += 1
        return self.psF[i]

    def psumb(self):
        i = self.psb_rr % len(self.psB)
        self.psb_rr += 1
        return self.psB[i]

    def mm(self, out, ob, lhsT, lb, rhs, rb, start=True, stop=True):
        self.S.op("pe", lambda e: e.matmul(out, lhsT, rhs, start=start, stop=stop), reads=[lb, rb], writes=[ob])

    def tr(self, out, ob, in_, ib, ident=None):
        idt = ident if ident is not None else self.ident
        n = in_.shape[0]
        self.S.op("pe", lambda e: e.transpose(out, in_, idt[0:n, 0:n]), reads=[ib, idt.b], writes=[ob])

    def act(self, out, ob, in_, ib, func, bias=None, scale=None, accum=None, rd=(), wr=()):
        kw = {}
        if bias is not None:
            kw["bias"] = bias
        if scale is not None:
            kw["scale"] = scale
        if accum is not None:
            kw["accum_out"] = accum
        self.S.op("act", lambda e: e.activation(out=out, in_=in_, func=func, **kw), reads=[ib] + list(rd),
                  writes=[ob] + list(wr))

    def tt(self, out, ob, in0, b0, in1, b1, op, eng="dve"):
        self.S.op(eng, lambda e: e.tensor_tensor(out=out, in0=in0, in1=in1, op=op), reads=[b0, b1], writes=[ob])

    def ts(self, out, ob, in0, b0, s1, op0, s2=None, op1=None, rd=(), eng="dve", accum=None):
        kw = {}
        if op1 is not None:
            kw["op1"] = op1
        if accum is not None:
            kw["accum_out"] = accum
        self.S.op(eng, lambda e: e.tensor_scalar(out=out, in0=in0, scalar1=s1, scalar2=s2, op0=op0, **kw),
                  reads=[b0] + list(rd), writes=[ob])

    def stt(self, out, ob, in0, b0, sc, in1, b1, op0, op1, rd=()):
        self.S.op("dve", lambda e: e.scalar_tensor_tensor(out=out, in0=in0, scalar=sc, in1=in1, op0=op0, op1=op1),
                  reads=[b0, b1] + list(rd), writes=[ob])

    def cp(self, out, ob, in_, ib, eng="dve"):
        if eng == "act":
            self.S.op("act", lambda e: e.copy(out, in_), reads=[ib], writes=[ob])
        else:
            self.S.op(eng, lambda e: e.tensor_copy(out, in_), reads=[ib], writes=[ob])

    def memset(self, ap, b, val, eng="pool"):
        self.S.op(eng, lambda e: e.memset(ap, val), writes=[b])

    def red(self, out, ob, in_, ib, op=ALU.add, eng="dve"):
        self.S.op(eng, lambda e: e.tensor_reduce(out=out, in_=in_, axis=AX.X, op=op), reads=[ib], writes=[ob])

    def recip(self, out, ob, in_, ib):
        self.S.op("dve", lambda e: e.reciprocal(out, in_), reads=[ib], writes=[ob])

    def rstd(self, out, ob, ss, sb_, n, inv_n):
        self.act(out, ob, ss, sb_, AF.Ln, scale=inv_n, bias=self.epsc[0:out.shape[0], 0:1], rd=[self.epsc.b])
        self.act(out, ob, out, ob, AF.Exp, scale=-0.5)

    def wblock(self, Wd, K, c0, ncols):
        i = self.w_rr % len(self.wbufs)
        self.w_rr += 1
        wb = self.wbufs[i]
        kc = K // 128
        src = Wd.t[:, c0:c0 + ncols].rearrange("(c p) n -> p c n", p=128)
        view = wb[:, 0:kc * ncols].rearrange("p (c n) -> p c n", c=kc)
        self.S.dma("sp", view, src, reads=[Wd.b], writes=[wb.b])
        return view, wb.b

    def linear_tm(self, xT, n, Wd, K, c0, ncols_total, consume):
        kc = K // 128
        off = 0
        while off < ncols_total:
            maxc = min(512, (5632 // kc) // 128 * 128)
            nb = min(maxc, ncols_total - off)
            wv, wbb = self.wblock(Wd, K, c0 + off, nb)
            ps = self.psum()
            for k in range(kc):
                self.mm(ps[0:n, 0:nb], ps.b, xT[:, k, 0:n], xT.b, wv[:, k, 0:nb], wbb, start=(k == 0), stop=(k == kc - 1))
            consume(off, nb, ps[0:n, 0:nb], ps.b)
            off += nb

    def linear_fm(self, xT, n, Wd, K, c0, nchunks, consume):
        kc = K // 128
        ch = 0
        while ch < nchunks:
            nch = min(4, nchunks - ch)
            wv, wbb = self.wblock(Wd, K, c0 + ch * 128, nch * 128)
            ps = self.psum()
            for j in range(nch):
                for k in range(kc):
                    self.mm(ps[:, j * n:(j + 1) * n], ps.b, wv[:, k, j * 128:(j + 1) * 128], wbb, xT[:, k, 0:n], xT.b,
                            start=(k == 0), stop=(k == kc - 1))
            consume(ch, nch, ps[:, 0:nch * n].rearrange("p (c t) -> p c t", c=nch), ps.b)
            ch += nch

    def rmsnorm_T(self, x, n, gcol, hT):
        junk, ss, hn = self.junk, self.ss, self.hn
        self.act(junk[0:n, :], junk.b, x[0:n, :], x.b, AF.Square, accum=ss[0:n, 0:1], wr=[ss.b])
        self.rstd(ss[0:n, 1:2], ss.b, ss[0:n, 0:1], ss.b, n, 1.0 / D)
        self.ts(hn[0:n, :], hn.b, x[0:n, :], x.b, ss[0:n, 1:2], ALU.mult, rd=[ss.b])
        self.transpose_fm(hn, n, 8, hT, gcol)

    def transpose_fm(self, src, n, nchunks, dst, gcol=None, width=128, col0=0, dview=None):
        dv = dview if dview is not None else dst[0:width, 0:nchunks, 0:n]
        per = min(8, 1024 // n)
        c = 0
        while c < nchunks:
            m = min(per, nchunks - c)
            pb = self.psumb()
            for j in range(m):
                a = col0 + (c + j) * width
                self.tr(pb[0:width, j * n:(j + 1) * n], pb.b, src[0:n, a:a + width], src.b)
            pv = pb[0:width, 0:m * n].rearrange("p (c t) -> p c t", c=m)
            if gcol is None:
                self.cp(dv[:, c:c + m, :], dst.b, pv, pb.b, eng="act")
            else:
                g1 = gcol[0:width, c:c + m].unsqueeze(2).to_broadcast([width, m, n])
                self.tt(dv[:, c:c + m, :], dst.b, pv, pb.b, g1, gcol.b, ALU.mult)
            c += m

    def build(self):
        nc = self.nc
        es = self.es
        with es:
            self.S = Sched(nc, es, same_engine_sync=self.same_sync)
            self.declare_io()
            self.alloc()
            st = self.stage
            self.setup_consts()
            if st >= 1:
                self.convert_weights()
            if st >= 2 and self.do_prompt:
                self.cur_prompt = True
                for s in range(NPB if st >= 9 else 1):
                    self.prompt_seq(s)
            if self.do_sample:
                for b in range(self.n_sample):
                    self.sample_seq(b)
            self.S.finish([b for b in self.out_bufs if b.w is not None])
        return nc

    def declare_io(self):
        d = self.din
        self.x_prompt = d("x_prompt", [NPB, SEQ, D])
        self.x_sample = d("x_sample", [NSB, TS, D])
        self.mem_prompt = d("mem_prompt", [NPB, 256, D])
        self.state_sconv = d("state_sconv", [NSB, 2, 1024])
        self.state_ssd_conv = d("state_ssd_conv", [NSB, 3, 2048])
        self.state_ssd = d("state_ssd", [NSB, 1024, 128])
        self.cache_nsa_kv = d("cache_nsa_kv", [self.pool_rows, 1024])
        self.cache_nsa_win = d("cache_nsa_win", [NSB, 512, 512])
        self.cache_mem_kv = d("cache_mem_kv", [2, NSB, 256, 1024])
        self.page_table = d("page_table", [NSB, 128], I32)
        self.norm_mix = d("norm_mix", [2, D])
        self.norm_mem = d("norm_mem", [2, D])
        self.norm_memsrc = d("norm_memsrc", [2, D])
        self.norm_ffn = d("norm_ffn", [2, D])
        self.ab_w_in = d("ab_w_in", [D, IN0])
        self.ab_sconv_w = d("ab_sconv_w", [3, 1024])
        self.ab_ssd_conv_w = d("ab_ssd_conv_w", [4, 2048])
        self.ab_ssd_conv_b = d("ab_ssd_conv_b", [1, 2048])
        self.ab_dt_bias = d("ab_dt_bias", [1, 16])
        self.ab_a_log = d("ab_a_log", [1, 16])
        self.ab_d = d("ab_d", [1, 16])
        self.ab_ssd_norm = d("ab_ssd_norm", [1, 1024])
        self.ab_w_out = d("ab_w_out", [2048, D])
        self.nsa_w_in = d("nsa_w_in", [D, IN1])
        self.nsa_q_norm = d("nsa_q_norm", [1, 64])
        self.nsa_k_norm = d("nsa_k_norm", [3, 64])
        self.nsa_cmp_pe = d("nsa_cmp_pe", [2, 32, 64])
        self.nsa_cmp_w1 = d("nsa_cmp_w1", [2, 32, 64, 64])
        self.nsa_cmp_w2 = d("nsa_cmp_w2", [2, 64, 64])
        self.nsa_w_out = d("nsa_w_out", [D, D])
        self.mem_w_q = d("mem_w_q", [2, D, 512])
        self.mem_q_norm = d("mem_q_norm", [2, 128])
        self.mem_w_kv = d("mem_w_kv", [2, D, 1024])
        self.mem_k_norm = d("mem_k_norm", [2, 128])
        self.mem_w_o = d("mem_w_o", [2, 512, D])
        self.ffn_w_gate = d("ffn_w_gate", [2, D, DFF])
        self.ffn_w_up = d("ffn_w_up", [2, D, DFF])
        self.ffn_w_down = d("ffn_w_down", [2, DFF, D])
        o = self.dout
        self.y_prompt, self.b_y_prompt = o("y_prompt", [NPB, SEQ, D])
        self.y_sample, self.b_y_sample = o("y_sample", [NSB, TS, D])
        self.p_sconv, self.b_p_sconv = o("p_sconv", [NPB, 2, 1024])
        self.p_ssd_conv, self.b_p_ssd_conv = o("p_ssd_conv", [NPB, 3, 2048])
        self.p_ssd, self.b_p_ssd = o("p_ssd", [NPB, 1024, 128])
        self.p_nsa_rows, self.b_p_nsa_rows = o("p_nsa_rows", [NPB, SEQ, 1024])
        self.p_nsa_win, self.b_p_nsa_win = o("p_nsa_win", [NPB, 512, 512])
        self.p_mem_kv, self.b_p_mem_kv = o("p_mem_kv", [2, NPB, 256, 1024])
        self.s_sconv, self.b_s_sconv = o("s_sconv", [NSB, 2, 1024])
        self.s_ssd_conv, self.b_s_ssd_conv = o("s_ssd_conv", [NSB, 3, 2048])
        self.s_ssd, self.b_s_ssd = o("s_ssd", [NSB, 1024, 128])
        self.s_nsa_rows, self.b_s_nsa_rows = o("s_nsa_rows", [NSB, TS, 1024])
        self.s_nsa_win, self.b_s_nsa_win = o("s_nsa_win", [NSB, 512, 512])
        ds = self.dscratch
        self.W_in0 = ds("W_in0", [D, IN0], BF16)
        self.W_out0 = ds("W_out0", [2048, D], BF16)
        self.W_in1 = ds("W_in1", [D, IN1], BF16)
        self.W_out1 = ds("W_out1", [D, D], BF16)
        self.W_q = [ds("W_q%d" % l, [D, 512], BF16) for l in range(2)]
        self.W_kv = [ds("W_kv%d" % l, [D, 1024], BF16) for l in range(2)]
        self.W_o = [ds("W_o%d" % l, [512, D], BF16) for l in range(2)]
        self.W_g = [ds("W_g%d" % l, [D, DFF], BF16) for l in range(2)]
        self.W_u = [ds("W_u%d" % l, [D, DFF], BF16) for l in range(2)]
        self.W_d = [ds("W_d%d" % l, [DFF, D], BF16) for l in range(2)]

    def convert_weights(self):
        S = self.S
        pairs = [(self.W_in0, self.ab_w_in), (self.W_out0, self.ab_w_out), (self.W_in1, self.nsa_w_in),
                 (self.W_out1, self.nsa_w_out)]
        for l in range(2):
            pairs += [(self.W_q[l], self.mem_w_q[l]), (self.W_kv[l], self.mem_w_kv[l]), (self.W_o[l], self.mem_w_o[l]),
                      (self.W_g[l], self.ffn_w_gate[l]), (self.W_u[l], self.ffn_w_up[l]), (self.W_d[l], self.ffn_w_down[l])]
        for dst, src in pairs:
            K, N = src.shape[0], src.shape[1]
            for r in range(0, K, 256):
                r1 = min(K, r + 256)
                for c in range(0, N, 2048):
                    c1 = min(N, c + 2048)
                    S.dma("pool", dst.t[r:r1, c:c1], src[r:r1, c:c1], writes=[dst.b])

    def alloc(self):
        nc, es, sb = self.nc, self.es, self.sb
        self.psF = [TT(es.enter_context(nc.psum_tensor("psF%d" % i, [128, 512], F32)), "psF%d" % i, Buf("psF%d" % i, True))
                    for i in range(6)]
        self.psB = [TT(es.enter_context(nc.psum_tensor("psB%d" % i, [128, 1024], BF16)), "psB%d" % i, Buf("psB%d" % i, True))
                    for i in range(2)]
        self.ps_rr = 0
        self.psb_rr = 0
        self.wbufs = [sb("wbuf%d" % i, [128, 5632], BF16) for i in range(3)]
        self.w_rr = 0
        self.ident = sb("ident", [128, 128], BF16)
        self.identf = sb("identf", [128, 128], F32)
        self.U = sb("U", [128, 128], F32)
        self.onesf = sb("onesf", [128, 128], F32)
        self.causal_neg = sb("causal_neg", [128, 128], BF16)
        self.win_neg = sb("win_neg", [128, 128], BF16)
        self.maskc = sb("maskc", [128, 128], BF16)
        self.Eall = sb("Eall", [128, 16, 128], BF16)
        self.mov = sb("mov", [128, 33], F32)
        self.itmp = sb("itmp", [128, 256], I32)
        self.piota2 = sb("piota2", [128, 1], F32)
        self.ones_bf = sb("ones_bf", [128, 128], BF16)
        self.mov33 = sb("mov33", [128, 33], BF16)
        self.epsc = sb("epsc", [128, 1], F32)
        self.g_mix = [sb("g_mix%d" % l, [128, 8]) for l in range(2)]
        self.g_mem = [sb("g_mem%d" % l, [128, 8]) for l in range(2)]
        self.g_src = [sb("g_src%d" % l, [128, 8]) for l in range(2)]
        self.g_ffn = [sb("g_ffn%d" % l, [128, 8]) for l in range(2)]
        self.g_ssdn = sb("g_ssdn", [128, 8])
        self.sconv_w = sb("sconv_w", [128, 8, 3])
        self.ssdc_w = sb("ssdc_w", [128, 16, 4])
        self.ssdc_b = sb("ssdc_b", [128, 16])
        self.dt_bias = sb("dt_bias", [128, 16])
        self.Aneg = sb("Aneg", [128, 16])
        self.Dsk = sb("Dsk", [128, 16])
        self.memq_g = [sb("memq_g%d" % l, [128, 1]) for l in range(2)]
        self.memk_g = [sb("memk_g%d" % l, [128, 128]) for l in range(2)]
        self.memk_gc = [sb("memk_gc%d" % l, [128, 1]) for l in range(2)]
        self.nsaq_g = sb("nsaq_g", [128, 64])
        self.nsak_g = sb("nsak_g", [128, 3, 64])
        self.w1 = sb("w1", [128, 2, 16, 128], BF16)
        self.w2 = sb("w2", [128, 64], BF16)
        self.peT = sb("peT", [128, 2, 16], BF16)
        self.cAB = sb("cAB", [128, 2])
        self.cosT = sb("cosT", [128, 16, 8])
        self.sinT = sb("sinT", [128, 16, 8])
        self.cosC = sb("cosC", [128, 8, 8])
        self.sinC = sb("sinC", [128, 8, 8])
        self.cosS = sb("cosS", [8, 1, 8])
        self.sinS = sb("sinS", [8, 1, 8])
        self.selFH = sb("selFH", [128, 2, 32])
        self.xres2 = sb("xres2", [128, 1024])
        self.xres = sb("xres", [128, 1024])
        self.junk = sb("junk", [128, 1024])
        self.ss = sb("ss", [128, 2])
        self.hn = sb("hn", [128, 1024], BF16)
        self.hT = sb("hT", [128, 8, 128], BF16)
        self.big = sb("big", [128, 2816], BF16)
        self.bigT = sb("bigT", [128, 22, 128], BF16)
        self.f1 = sb("f1", [128, 1024])
        self.f2 = sb("f2", [128, 1024])
        self.f3 = sb("f3", [128, 1024])
        self.sm = sb("sm", [128, 256])
        self.smB = sb("smB", [128, 16])
        self.ubuf = sb("ubuf", [128, 8, 130])
        self.xbuf = sb("xbuf", [128, 16, 131])
        self.xcs = sb("xcs", [128, 16, 128], BF16)
        self.rmask = sb("rmask", [128, 4, 128])
        self.seg = sb("seg", [128, 4, 128])
        self.cbm = sb("cbm", [128, 4, 128])
        self.xdt = sb("xdt", [128, 1024], BF16)
        self.xdte = sb("xdte", [128, 1024], BF16)
        self.hstate = sb("hstate", [128, 1024])
        self.hstate_bf = sb("hstate_bf", [128, 1024], BF16)
        self.KT = [sb("KT%d" % l, [128, 4, 256], BF16) for l in range(2)]
        self.V1 = [sb("V1%d" % l, [128, 2, 4, 129], BF16) for l in range(2)]
        self.qT = sb("qT", [128, 4, 128], BF16)
        self.qT2 = sb("qT2", [128, 2, 16, 128], BF16)
        self.pT = sb("pT", [128, 16, 512], BF16)
        self.kcache = sb("kcache", [128, 4, 2048], BF16)
        self.VS1 = sb("VS1", [128, 16, 4, 65], BF16)
        self.VW1 = sb("VW1", [128, 5, 4, 65], BF16)
        self.Aall = sb("Aall", [128, 2, 4, 129])
        self.pre = sb("pre", [128, 4, 128], BF16)
        self.kcT = sb("kcT", [128, 4, 128], BF16)
        self.VC1 = sb("VC1", [128, 4, 97], BF16)
        self.cT = sb("cT", [128, 4, 128], BF16)
        self.selT = sb("selT", [128, 4, 128], BF16)
        self.rows = sb("rows", [128, 1024])
        self.winr = sb("winr", [128, 512])
        self.gates = sb("gates", [128, 48])
        self.imp = sb("imp", [128, 4, 32])
        self.xr = [self.xres, self.xres2]
        self.s_h = TT(self.f1[:, 0:1024].rearrange("p (c t) -> p c t", c=8), "s_h", self.f1.b)
        self.cacc = TT(self.f3[:, 0:1024].rearrange("p (c t) -> p c t", c=8), "cacc", self.f3.b)
        self.x_tm = TT(self.big[:, 1024:2048], "x_tm", self.big.b)
        self.B_tm = TT(self.big[:, 2048:2560], "B_tm", self.big.b)
        self.Mh = TT(self.pT[:, 0:4, :].rearrange("p a (b t) -> p (a b) t", b=4), "Mh", self.pT.b)
        self.oacc = TT(self.f3[:, :], "oacc", self.f3.b)

    def setup_consts(self):
        S = self.S
        P = self
        ms = self.memset

        def asel(t, pattern, op, fill, base, cm, rows=128, view=None):
            v = view if view is not None else t[0:rows]
            S.op("pool", lambda e: e.affine_select(out=v, in_=v, pattern=pattern, compare_op=op, fill=self.fillreg(fill),
                                                  base=base, channel_multiplier=cm), reads=[t.b], writes=[t.b])
        ms(self.epsc[:], self.epsc.b, EPS)
        ms(self.ident[:], self.ident.b, 1.0)
        asel(self.ident, [[-1, 128]], ALU.is_equal, 0.0, 0, 1)
        ms(self.identf[:], self.identf.b, 1.0)
        asel(self.identf, [[-1, 128]], ALU.is_equal, 0.0, 0, 1)
        ms(self.onesf[:], self.onesf.b, 1.0)
        ms(self.U[:], self.U.b, 1.0)
        asel(self.U, [[1, 128]], ALU.is_ge, 0.0, 0, -1)
        ms(self.causal_neg[:], self.causal_neg.b, 0.0)
        asel(self.causal_neg, [[1, 128]], ALU.is_ge, NEGM, 0, -1)
        ms(self.win_neg[:], self.win_neg.b, 0.0)
        asel(self.win_neg, [[-1, 128]], ALU.is_gt, NEGM, 0, 1)
        ms(self.Eall[:], self.Eall.b, -NEGM)
        asel(self.Eall, [[128, 16], [1, 128]], ALU.is_ge, 0.0, 0, -64)
        asel(self.Eall, [[-128, 16], [-1, 128]], ALU.is_ge, 0.0, 63, 64)
        ms(self.qT2[:], self.qT2.b, 0.0)
        ms(self.kcT[:], self.kcT.b, 0.0)
        ms(self.selT[:], self.selT.b, 0.0)
        def col(t, src_row):
            S.dma("pool", t[:], src_row.rearrange("(c p) -> p c", p=128), writes=[t.b])
        nc = self.nc
        with nc.allow_non_contiguous_dma("tiny constant loads"):
            for l in range(2):
                col(self.g_mix[l], self.norm_mix[l])
                col(self.g_mem[l], self.norm_mem[l])
                col(self.g_src[l], self.norm_memsrc[l])
                col(self.g_ffn[l], self.norm_ffn[l])
            col(self.g_ssdn, self.ab_ssd_norm[0])
            for k in range(3):
                S.dma("pool", self.sconv_w[:, :, k], self.ab_sconv_w[k].rearrange("(c p) -> p c", p=128), writes=[self.sconv_w.b])
            for k in range(4):
                S.dma("pool", self.ssdc_w[:, :, k], self.ab_ssd_conv_w[k].rearrange("(c p) -> p c", p=128), writes=[self.ssdc_w.b])
            S.dma("pool", self.ssdc_b[:], self.ab_ssd_conv_b[0].rearrange("(c p) -> p c", p=128), writes=[self.ssdc_b.b])
            for l in range(2):
                S.dma("pool", self.memq_g[l][:], self.mem_q_norm[l].rearrange("(p o) -> p o", o=1), writes=[self.memq_g[l].b])
                S.dma("pool", self.memk_gc[l][:], self.mem_k_norm[l].rearrange("(p o) -> p o", o=1), writes=[self.memk_gc[l].b])
            ms(self.w1[:], self.w1.b, 0.0)
            for kv in range(2):
                r0 = 64 * kv
                for ab in range(2):
                    S.dma("pool", self.peT[r0:r0 + 64, ab, :], self.nsa_cmp_pe[kv, 16 * ab:16 * ab + 16].rearrange("j d -> d j"),
                          writes=[self.peT.b])
                    S.dma("pool", self.w1[r0:r0 + 64, ab, :, r0:r0 + 64],
                          self.nsa_cmp_w1[kv, 16 * ab:16 * ab + 16].rearrange("j d e -> d j e"), writes=[self.w1.b])
                S.dma("pool", self.w2[r0:r0 + 64, :], self.nsa_cmp_w2[kv], writes=[self.w2.b])
        S.dma("pool", self.dt_bias[:], self.ab_dt_bias.partition_broadcast(128), writes=[self.dt_bias.b])
        S.dma("pool", self.Aneg[:], self.ab_a_log.partition_broadcast(128), writes=[self.Aneg.b])
        S.dma("pool", self.Dsk[:], self.ab_d.partition_broadcast(128), writes=[self.Dsk.b])
        for l in range(2):
            S.dma("pool", self.memk_g[l][:], self.mem_k_norm[l:l + 1, :].partition_broadcast(128), writes=[self.memk_g[l].b])
            self.ts(self.memq_g[l][:], self.memq_g[l].b, self.memq_g[l][:], self.memq_g[l].b, 128 ** -0.5, ALU.mult)
        S.dma("pool", self.nsaq_g[:], self.nsa_q_norm.partition_broadcast(128), writes=[self.nsaq_g.b])
        self.ts(self.nsaq_g[:], self.nsaq_g.b, self.nsaq_g[:], self.nsaq_g.b, 64 ** -0.5, ALU.mult)
        for i in range(3):
            S.dma("pool", self.nsak_g[:, i, :], self.nsa_k_norm[i:i + 1, :].partition_broadcast(128), writes=[self.nsak_g.b])
        self.act(self.Aneg[:], self.Aneg.b, self.Aneg[:], self.Aneg.b, AF.Exp)
        self.ts(self.Aneg[:], self.Aneg.b, self.Aneg[:], self.Aneg.b, -1.0, ALU.mult)
        ps = self.psum()
        for ab in range(2):
            for j in range(16):
                self.mm(ps[:, ab:ab + 1], ps.b, self.w1[:, ab, j, :], self.w1.b, self.peT[:, ab, j:j + 1], self.peT.b,
                        start=(j == 0), stop=(j == 15))
        self.cp(self.cAB[:], self.cAB.b, ps[:, 0:2], ps.b)
        t2 = self.junk
        it = self.itmp
        S.op("pool", lambda e: e.iota(it[:, 0:33], pattern=[[64, 33]], base=0, channel_multiplier=0), writes=[it.b])
        S.op("pool", lambda e: e.iota(it[:, 64:97], pattern=[[0, 33]], base=0, channel_multiplier=16), writes=[it.b])
        S.op("pool", lambda e: e.iota(it[:, 128:129], pattern=[[0, 1]], base=0, channel_multiplier=2), writes=[it.b])
        self.cp(t2[:, 0:129], t2.b, it[:, 0:129], it.b)
        self.cp(self.piota2[:], self.piota2.b, t2[:, 128:129], t2.b)
        self.ts(t2[:, 256:289], t2.b, t2[:, 0:33], t2.b, 64.0, ALU.add)
        self.stt(t2[:, 320:353], t2.b, t2[:, 64:97], t2.b, 32.0, t2[:, 256:289], t2.b, ALU.add, ALU.min)
        self.tt(t2[:, 384:417], t2.b, t2[:, 64:97], t2.b, t2[:, 0:33], t2.b, ALU.max)
        self.tt(t2[:, 448:481], t2.b, t2[:, 320:353], t2.b, t2[:, 384:417], t2.b, ALU.subtract)
        self.ts(self.mov[:], self.mov.b, t2[:, 448:481], t2.b, 0.0, ALU.max, 1.0 / 32, ALU.mult)
        self.cp(self.mov33[:], self.mov33.b, self.mov[:], self.mov.b)
        ms(self.ones_bf[:], self.ones_bf.b, 1.0)
        self.rope_table(self.cosT, self.sinT, 16, lambda e, v: e.iota(v, pattern=[[128, 16], [0, 8]], base=0, channel_multiplier=1), 128)
        self.rope_table(self.cosC, self.sinC, 8, lambda e, v: e.iota(v, pattern=[[2048, 8], [0, 8]], base=31, channel_multiplier=16), 128)
        self.rope_table(self.cosS, self.sinS, 1, lambda e, v: e.iota(v, pattern=[[0, 1], [0, 8]], base=PAST, channel_multiplier=1), 8)
        ms(self.VS1[:], self.VS1.b, 1.0)
        ms(self.VW1[:], self.VW1.b, 1.0)
        ms(self.VC1[:], self.VC1.b, 1.0)
        for l in range(2):
            ms(self.V1[l][:], self.V1[l].b, 1.0)
        for g in range(4):
            self.cp(self.VC1[:, g, 65:97], self.VC1.b, self.mov[:, 0:32], self.mov.b)

    def rope_table(self, cosT, sinT, ncol, iota_fn, rows):
        S = self.S
        t = self.f1
        v = t[0:rows, 0:ncol * 8].rearrange("p (c k) -> p c k", k=8)
        iv = self.itmp[0:rows, 0:ncol * 8].rearrange("p (c k) -> p c k", k=8)
        S.op("pool", lambda e: iota_fn(e, iv), writes=[self.itmp.b])
        self.cp(v, t.b, iv, self.itmp.b)
        inv = self.f2
        for k in range(8):
            self.memset(inv[0:rows, k:k + 1], inv.b, THETA ** (-k / 8.0))
        invb = inv[0:rows, 0:8].unsqueeze(1).to_broadcast([rows, ncol, 8])
        ang = self.f3
        av = ang[0:rows, 0:ncol * 8].rearrange("p (c k) -> p c k", k=8)
        self.tt(av, ang.b, v, t.b, invb, inv.b, ALU.mult)
        a2 = ang[0:rows, 0:ncol * 8]
        w = self.junk[0:rows, 0:ncol * 8]
        self.range_reduce(w, a2, ang.b, rows, ncol * 8, 0.0)
        self.act(sinT[0:rows].rearrange("p c k -> p (c k)"), sinT.b, w, self.junk.b, AF.Sin)
        self.range_reduce(w, a2, ang.b, rows, ncol * 8, math.pi / 2)
        self.act(cosT[0:rows].rearrange("p c k -> p (c k)"), cosT.b, w, self.junk.b, AF.Sin)

    def range_reduce(self, w, a, ab, rows, m, shift):
        jb = self.junk.b
        two_pi = 2 * math.pi
        hi = 6.28125
        lo = two_pi - hi
        t = self.junk[0:rows, 512:512 + m]
        it = self.itmp[0:rows, 0:m]
        kf = self.junk[0:rows, 768:768 + m]
        self.ts(w, jb, a, ab, shift, ALU.add)
        self.ts(t, jb, w, jb, 1.0 / two_pi, ALU.mult)
        self.cp(it, self.itmp.b, t, jb)
        self.cp(kf, jb, it, self.itmp.b)
        self.stt(w, jb, kf, jb, -hi, w, jb, ALU.mult, ALU.add)
        self.stt(w, jb, kf, jb, -lo, w, jb, ALU.mult, ALU.add)
        self.ts(t, jb, w, jb, math.pi, ALU.is_gt)
        self.stt(w, jb, t, jb, -two_pi, w, jb, ALU.mult, ALU.add)
        self.ts(t, jb, w, jb, -math.pi, ALU.is_lt)
        self.stt(w, jb, t, jb, two_pi, w, jb, ALU.mult, ALU.add)
        self.ts(w, jb, w, jb, math.pi - 1e-5, ALU.min, -(math.pi - 1e-5), ALU.max)

    def prompt_seq(self, s):
        self.mem_kv_prompt(s)
        if self.stage < 3:
            return
        self.seq_reset()
        for tau in range(0, self.n_ptiles, 2):
            self.tile_pair(s, tau)
        self.xres = self.xr[0]

    def seq_reset(self):
        ms = self.memset
        ms(self.ubuf[:], self.ubuf.b, 0.0)
        ms(self.xbuf[:], self.xbuf.b, 0.0)
        ms(self.hstate[:], self.hstate.b, 0.0)
        ms(self.hstate_bf[:], self.hstate_bf.b, 0.0)
        ms(self.Aall[:], self.Aall.b, 0.0)
        ms(self.pre[:], self.pre.b, 0.0)

    def mem_kv_prompt(self, s):
        S = self.S
        for l in range(2):
            for mc in range(2):
                x = self.xres
                S.dma("pool", x[:], self.mem_prompt[s, mc * 128:(mc + 1) * 128, :], writes=[x.b])
                sub = 9
                if sub >= 1:
                    self.rmsnorm_T(x, 128, self.g_src[l], self.hT)
                if sub < 2:
                    S.dma("pool", self.p_mem_kv[l, s, mc * 128:(mc + 1) * 128, :], x[:], reads=[x.b, self.hT.b], writes=[self.b_p_mem_kv])
                    continue
                out = self.f1

                def consume(off, nb, ps, pb, out=out, l=l, mc=mc):
                    if sub < 3:
                        self.cp(out[:, off:off + nb], out.b, ps, pb, eng="act")
                        return
                    if off == 0:
                        sq = self.junk
                        self.act(sq[:, 0:512], sq.b, ps, pb, AF.Square)
                        self.red(self.sm[:, 0:4], self.sm.b, sq[:, 0:512].rearrange("p (h d) -> p h d", h=4), sq.b)
                        self.rstd(self.sm[:, 4:8], self.sm.b, self.sm[:, 0:4], self.sm.b, 128, 1.0 / 128)
                        rb = self.sm[:, 4:8].unsqueeze(2).to_broadcast([128, 4, 128])
                        ov = out[:, 0:512].rearrange("p (h d) -> p h d", h=4)
                        self.tt(ov, out.b, ps.rearrange("p (h d) -> p h d", h=4), pb, rb, self.sm.b, ALU.mult)
                        self.cp(self.big[:, 0:512], self.big.b, out[:, 0:512], out.b, eng="act")
                        gb = self.memk_g[l][:].unsqueeze(1).to_broadcast([128, 4, 128])
                        self.tt(ov, out.b, ov, out.b, gb, self.memk_g[l].b, ALU.mult)
                    else:
                        self.cp(out[:, 512:1024], out.b, ps, pb, eng="act")
                        self.cp(self.V1[l][:, mc, :, 0:128], self.V1[l].b, ps.rearrange("p (h d) -> p h d", h=4), pb)
                self.linear_tm(self.hT, 128, self.W_kv[l], 1024, 0, 1024, consume)
                S.dma("pool", self.p_mem_kv[l, s, mc * 128:(mc + 1) * 128, :], out[:], reads=[out.b], writes=[self.b_p_mem_kv])
                if sub < 4:
                    continue
                pb = self.psumb()
                for h in range(4):
                    self.tr(pb[:, h * 128:(h + 1) * 128], pb.b, self.big[:, h * 128:(h + 1) * 128], self.big.b)
                self.ts(self.KT[l][:, :, mc * 128:(mc + 1) * 128], self.KT[l].b, pb[:, 0:512].rearrange("p (h t) -> p h t", h=4),
                        pb.b, self.memk_gc[l][:, 0:1], ALU.mult, rd=[self.memk_gc[l].b])

    def tile(self, s, tau):
        S = self.S
        n = 128
        x = self.xres
        S.dma("pool", x[:], self.x_prompt[s, tau * 128:(tau + 1) * 128, :], writes=[x.b])
        st = self.stage
        if st >= 4:
            self.layer0(s, tau, n)
        if st >= 5:
            self.mem_attn(0, n)
        if st >= 6:
            self.ffn(0, n)
        if st >= 7:
            self.layer1(s, tau, n)
        if st >= 8:
            self.mem_attn(1, n)
            self.ffn(1, n)
        S.dma("pool", self.y_prompt[s, tau * 128:(tau + 1) * 128, :], x[:], reads=[x.b], writes=[self.b_y_prompt])

    def tile_pair(self, s, tau):
        S = self.S
        n = 128
        xr = self.xr
        for i in range(2):
            S.dma("sp", xr[i][:], self.x_prompt[s, (tau + i) * 128:(tau + i + 1) * 128, :], writes=[xr[i].b])
        for i in range(2):
            self.xres = xr[i]
            self.layer0(s, tau + i, n)
        self.mem_pair(0, n)
        self.ffn_pair(0, n)
        for i in range(2):
            self.xres = xr[i]
            self.layer1(s, tau + i, n)
        self.mem_pair(1, n)
        self.ffn_pair(1, n)
        for i in range(2):
            S.dma("pool", self.y_prompt[s, (tau + i) * 128:(tau + i + 1) * 128, :], xr[i][:], reads=[xr[i].b],
                  writes=[self.b_y_prompt])

    def wrows(self, Wd, r0, m, ncols):
        i = self.w_rr % len(self.wbufs)
        self.w_rr += 1
        wb = self.wbufs[i]
        src = Wd.t[r0 * 128:(r0 + m) * 128, 0:ncols].rearrange("(c p) n -> p c n", p=128)
        view = wb[:, 0:m * ncols].rearrange("p (c n) -> p c n", c=m)
        self.S.dma("sp", view, src, reads=[Wd.b], writes=[wb.b])
        return view, wb.b

    def ffn_pair(self, l, n):
        xr = self.xr
        pTf = self.pT.t[:].rearrange("p a b -> p (a b)")
        hTs = [self.hT, TT(pTf[:, 0:1024].rearrange("p (c t) -> p c t", c=8), "hTB", self.pT.b)]
        acts = [self.big, TT(pTf[:, 1024:1024 + DFF], "actB", self.pT.b)]
        sgs = [TT(self.f1.t[:, 0:512], "sgA", self.f1.b), TT(self.f2.t[:, 0:512], "sgB", self.f2.b)]
        for i in range(2):
            self.rmsnorm_T(xr[i], n, self.g_ffn[l], hTs[i])
        off = 0
        while off < DFF:
            nb = min(512, DFF - off)
            wv, wbb = self.wblock(self.W_g[l], 1024, off, nb)
            for i in range(2):
                ps = self.psum()
                for k in range(8):
                    self.mm(ps[0:n, 0:nb], ps.b, hTs[i][:, k, 0:n], hTs[i].b, wv[:, k, 0:nb], wbb, start=(k == 0), stop=(k == 7))
                self.act(sgs[i][0:n, 0:nb], sgs[i].b, ps[0:n, 0:nb], ps.b, AF.Silu)
            wv, wbb = self.wblock(self.W_u[l], 1024, off, nb)
            for i in range(2):
                ps = self.psum()
                for k in range(8):
                    self.mm(ps[0:n, 0:nb], ps.b, hTs[i][:, k, 0:n], hTs[i].b, wv[:, k, 0:nb], wbb, start=(k == 0), stop=(k == 7))
                self.tt(acts[i][0:n, off:off + nb], acts[i].b, ps[0:n, 0:nb], ps.b, sgs[i][0:n, 0:nb], sgs[i].b, ALU.mult)
            off += nb
        po = [[self.psum(), self.psum()] for _ in range(2)]
        bT = self.bigT
        aTs = [[TT(bT.t[:, 4 * (2 * par + i):4 * (2 * par + i) + 4, :], "aT%d%d" % (par, i)) for i in range(2)] for par in range(2)]
        nk = DFF // 128
        groups = [(kg, min(4, nk - kg)) for kg in range(0, nk, 4)]

        def dn_t(gi):
            kg, m = groups[gi]
            for i in range(2):
                aT = aTs[gi % 2][i]
                pb = self.psumb()
                for j in range(m):
                    self.tr(pb[:, j * n:(j + 1) * n], pb.b, acts[i][0:n, (kg + j) * 128:(kg + j + 1) * 128], acts[i].b)
                self.cp(aT[:, 0:m, 0:n], aT.b, pb[:, 0:m * n].rearrange("p (c t) -> p c t", c=m), pb.b, eng=("act" if i == 0 else "dve"))

        def dn_mm(gi):
            kg, m = groups[gi]
            wv, wbb = self.wrows(self.W_d[l], kg, m, 1024)
            for i in range(2):
                aT = aTs[gi % 2][i]
                for half in range(2):
                    p_ = po[i][half]
                    for j in range(m):
                        self.mm(p_[0:n, :], p_.b, aT[:, j, 0:n], aT.b, wv[:, j, half * 512:(half + 1) * 512], wbb,
                                start=(kg == 0 and j == 0), stop=(kg + j == nk - 1))
        dn_t(0)
        for gi in range(len(groups)):
            if gi + 1 < len(groups):
                dn_t(gi + 1)
            dn_mm(gi)
        for i in range(2):
            x = xr[i]
            for half in range(2):
                p_ = po[i][half]
                self.tt(x[0:n, half * 512:(half + 1) * 512], x.b, x[0:n, half * 512:(half + 1) * 512], x.b, p_[0:n, :], p_.b, ALU.add)

    def ffn(self, l, n):
        x = self.xres
        self.rmsnorm_T(x, n, self.g_ffn[l], self.hT)
        act = self.big
        sg = self.f1

        def c_gate(off, nb, ps, pb):
            self.act(sg[0:n, off:off + nb], sg.b, ps, pb, AF.Silu)
        off = 0
        while off < DFF:
            nb = min(512, DFF - off)
            self.linear_tm(self.hT, n, self.W_g[l], 1024, off, nb, lambda o, b, ps, pb: self.act(sg[0:n, 0:nb], sg.b, ps, pb, AF.Silu))
            self.linear_tm(self.hT, n, self.W_u[l], 1024, off, nb,
                           lambda o, b, ps, pb: self.tt(act[0:n, off:off + nb], act.b, ps, pb, sg[0:n, 0:nb], sg.b, ALU.mult))
            off += nb
        self.transpose_fm(act, n, 22, self.bigT)
        self.linear_tm(self.bigT, n, self.W_d[l], DFF, 0, 1024,
                       lambda o, b, ps, pb: self.tt(x[0:n, o:o + b], x.b, x[0:n, o:o + b], x.b, ps, pb, ALU.add))

    def mem_pair(self, l, n):
        S = self.S
        S.barrier()
        xr = self.xr
        pTf = self.pT.t[:].rearrange("p a b -> p (a b)")

        class C:
            pass
        A, B = C(), C()
        A.x, A.hT, A.big, A.qT, A.sm, A.sq = xr[0], self.hT, self.big, self.qT, self.sm, self.junk
        A.pT = TT(pTf[:, 0:1024].rearrange("p (m t) -> p m t", m=2), "mpTA")
        A.oT = TT(self.bigT.t[:, 0:4, :], "moTA", self.bigT.b)
        B.x = xr[1]
        B.hT = TT(pTf[:, 2048:3072].rearrange("p (c t) -> p c t", c=8), "mhTB")
        B.pT = TT(pTf[:, 3072:4096].rearrange("p (m t) -> p m t", m=2), "mpTB")
        B.big = TT(pTf[:, 4096:4608], "mbigB")
        B.qT = TT(pTf[:, 4608:5120].rearrange("p (h t) -> p h t", h=4), "mqTB")
        B.oT = TT(pTf[:, 5120:5632].rearrange("p (c t) -> p c t", c=4), "moTB")
        B.sm, B.sq = self.smB, self.f2
        cs = [A, B]
        for c in cs:
            self.rmsnorm_T(c.x, n, self.g_mem[l], c.hT)
        wv, wbb = self.wblock(self.W_q[l], 1024, 0, 512)
        for c in cs:
            ps = self.psum()
            for k in range(8):
                self.mm(ps[0:n, 0:512], ps.b, c.hT[:, k, 0:n], c.hT.b, wv[:, k, 0:512], wbb, start=(k == 0), stop=(k == 7))
            sm, sq = c.sm, c.sq
            self.act(sq[0:n, 0:512], sq.b, ps[0:n, 0:512], ps.b, AF.Square)
            self.red(sm[0:n, 0:4], sm.b, sq[0:n, 0:512].rearrange("p (h d) -> p h d", h=4), sq.b)
            self.rstd(sm[0:n, 4:8], sm.b, sm[0:n, 0:4], sm.b, 128, 1.0 / 128)
            self.tt(c.big[0:n, 0:512].rearrange("p (h d) -> p h d", h=4), c.big.b,
                    ps[0:n, 0:512].rearrange("p (h d) -> p h d", h=4), ps.b,
                    sm[0:n, 4:8].unsqueeze(2).to_broadcast([n, 4, 128]), sm.b, ALU.mult)
        for c in cs:
            pb = self.psumb()
            for h in range(4):
                self.tr(pb[:, h * n:(h + 1) * n], pb.b, c.big[0:n, h * 128:(h + 1) * 128], c.big.b)
            self.ts(c.qT[:, 0:4, 0:n], c.qT.b, pb[:, 0:4 * n].rearrange("p (h t) -> p h t", h=4), pb.b,
                    self.memq_g[l][:, 0:1], ALU.mult, rd=[self.memq_g[l].b])
        for c in cs:
            for mc in range(2):
                ps = self.psum()
                for h in range(4):
                    self.mm(ps[:, h * n:(h + 1) * n], ps.b, self.KT[l][:, h, mc * 128:(mc + 1) * 128], self.KT[l].b,
                            c.qT[:, h, 0:n], c.qT.b)
                self.act(c.pT[:, mc, 0:4 * n], c.pT.b, ps[:, 0:4 * n], ps.b, AF.Exp)
        for c in cs:
            sm = c.sm
            for h in range(4):
                ps = self.psum()
                for mc in range(2):
                    self.mm(ps[0:n, 0:129], ps.b, c.pT[:, mc, h * n:(h + 1) * n], c.pT.b, self.V1[l][:, mc, h, :], self.V1[l].b,
                            start=(mc == 0), stop=(mc == 1))
                self.recip(sm[0:n, 8:9], sm.b, ps[0:n, 128:129], ps.b)
                self.ts(c.big[0:n, h * 128:(h + 1) * 128], c.big.b, ps[0:n, 0:128], ps.b, sm[0:n, 8:9], ALU.mult, rd=[sm.b])
        for c in cs:
            pb = self.psumb()
            for h in range(4):
                self.tr(pb[:, h * n:(h + 1) * n], pb.b, c.big[0:n, h * 128:(h + 1) * 128], c.big.b)
            self.cp(c.oT[:, 0:4, 0:n], c.oT.b, pb[:, 0:4 * n].rearrange("p (h t) -> p h t", h=4), pb.b, eng="act")
        for off in (0, 512):
            wv, wbb = self.wblock(self.W_o[l], 512, off, 512)
            for c in cs:
                ps = self.psum()
                for k in range(4):
                    self.mm(ps[0:n, 0:512], ps.b, c.oT[:, k, 0:n], c.oT.b, wv[:, k, 0:512], wbb, start=(k == 0), stop=(k == 3))
                self.tt(c.x[0:n, off:off + 512], c.x.b, c.x[0:n, off:off + 512], c.x.b, ps[0:n, 0:512], ps.b, ALU.add)
        S.barrier()

    def mem_attn(self, l, n):
        x = self.xres
        self.rmsnorm_T(x, n, self.g_mem[l], self.hT)
        q = self.f1
        sm = self.sm

        def cq(off, nb, ps, pb):
            sq = self.junk
            self.act(sq[0:n, 0:512], sq.b, ps, pb, AF.Square)
            self.red(sm[0:n, 0:4], sm.b, sq[0:n, 0:512].rearrange("p (h d) -> p h d", h=4), sq.b)
            self.rstd(sm[0:n, 4:8], sm.b, sm[0:n, 0:4], sm.b, 128, 1.0 / 128)
            rb = sm[0:n, 4:8].unsqueeze(2).to_broadcast([n, 4, 128])
            self.tt(self.big[0:n, 0:512].rearrange("p (h d) -> p h d", h=4), self.big.b,
                    ps.rearrange("p (h d) -> p h d", h=4), pb, rb, sm.b, ALU.mult)
        self.linear_tm(self.hT, n, self.W_q[l], 1024, 0, 512, cq)
        pb = self.psumb()
        for h in range(4):
            self.tr(pb[:, h * n:(h + 1) * n], pb.b, self.big[0:n, h * 128:(h + 1) * 128], self.big.b)
        self.ts(self.qT[:, 0:4, 0:n], self.qT.b, pb[:, 0:4 * n].rearrange("p (h t) -> p h t", h=4), pb.b,
                self.memq_g[l][:, 0:1], ALU.mult, rd=[self.memq_g[l].b])
        pT = self.pT
        for mc in range(2):
            ps = self.psum()
            for h in range(4):
                self.mm(ps[:, h * n:(h + 1) * n], ps.b, self.KT[l][:, h, mc * 128:(mc + 1) * 128], self.KT[l].b,
                        self.qT[:, h, 0:n], self.qT.b)
            self.act(pT[:, mc, 0:4 * n], pT.b, ps[:, 0:4 * n], ps.b, AF.Exp)
        o = self.big
        for h in range(4):
            ps = self.psum()
            for mc in range(2):
                self.mm(ps[0:n, 0:129], ps.b, pT[:, mc, h * n:(h + 1) * n], pT.b, self.V1[l][:, mc, h, :], self.V1[l].b,
                        start=(mc == 0), stop=(mc == 1))
            self.recip(sm[0:n, 8:9], sm.b, ps[0:n, 128:129], ps.b)
            self.ts(o[0:n, h * 128:(h + 1) * 128], o.b, ps[0:n, 0:128], ps.b, sm[0:n, 8:9], ALU.mult, rd=[sm.b])
        self.transpose_fm(o, n, 4, self.bigT)
        self.linear_tm(self.bigT, n, self.W_o[l], 512, 0, 1024,
                       lambda o_, b, ps, pb: self.tt(x[0:n, o_:o_ + b], x.b, x[0:n, o_:o_ + b], x.b, ps, pb, ALU.add))

    def layer0(self, s, tau, n, last=None):
        S = self.S
        x = self.xres
        last = (tau == self.n_ptiles - 1) if last is None else last
        self.rmsnorm_T(x, n, self.g_mix[0], self.hT)
        ubuf, xbuf, s_h = self.ubuf, self.xbuf, self.s_h
        self.linear_fm(self.hT, n, self.W_in0, 1024, 0, 8,
                       lambda c, m, ps, pb: self.cp(s_h[:, c:c + m, 0:n], s_h.b, ps, pb, eng="act"))
        sbv = self.f2[:, 0:8 * n].rearrange("p (c t) -> p c t", c=8)
        self.linear_fm(self.hT, n, self.W_in0, 1024, 1024, 8,
                       lambda c, m, ps, pb: self.cp(sbv[:, c:c + m, :], self.f2.b, ps, pb, eng="act"))
        self.linear_fm(self.hT, n, self.W_in0, 1024, 2048, 8,
                       lambda c, m, ps, pb: self.tt(ubuf[:, c:c + m, 2:2 + n], ubuf.b, ps, pb, s_h[:, c:c + m, 0:n], s_h.b, ALU.mult))
        ca = self.cacc
        for k in range(3):
            wk = self.sconv_w[:, :, k:k + 1].to_broadcast([128, 8, n])
            if k == 0:
                self.tt(ca[:, 0:8, 0:n], ca.b, ubuf[:, :, 0:n], ubuf.b, wk, self.sconv_w.b, ALU.mult)
            else:
                self.tt(self.junk[:, 0:8 * n].rearrange("p (c t) -> p c t", c=8), self.junk.b, ubuf[:, :, k:k + n], ubuf.b,
                        wk, self.sconv_w.b, ALU.mult)
                self.tt(ca[:, 0:8, 0:n], ca.b, ca[:, 0:8, 0:n], ca.b, self.junk[:, 0:8 * n].rearrange("p (c t) -> p c t", c=8),
                        self.junk.b, ALU.add)
        ycat = self.bigT
        self.tt(ycat[:, 0:8, 0:n], ycat.b, ca[:, 0:8, 0:n], ca.b, sbv, self.f2.b, ALU.mult)
        if last:
            with self.nc.allow_non_contiguous_dma("tiny state out"):
                dst, db = (self.p_sconv, self.b_p_sconv) if self.cur_prompt else (self.s_sconv, self.b_s_sconv)
                for k in range(2):
                    S.dma("pool", dst[s, k].rearrange("(c p) -> p c", p=128), ubuf[:, :, n + k], reads=[ubuf.b], writes=[db])
        self.cp(self.sm[:, 0:16].rearrange("p (c k) -> p c k", k=2), self.sm.b, ubuf[:, :, n:n + 2], ubuf.b)
        self.cp(ubuf[:, :, 0:2], ubuf.b, self.sm[:, 0:16].rearrange("p (c k) -> p c k", k=2), self.sm.b)
        self.linear_fm(self.hT, n, self.W_in0, 1024, 4096, 16,
                       lambda c, m, ps, pb: self.cp(xbuf[:, c:c + m, 3:3 + n], xbuf.b, ps, pb, eng="act"))
        if last:
            with self.nc.allow_non_contiguous_dma("tiny state out"):
                dst, db = (self.p_ssd_conv, self.b_p_ssd_conv) if self.cur_prompt else (self.s_ssd_conv, self.b_s_ssd_conv)
                for k in range(3):
                    S.dma("pool", dst[s, k].rearrange("(c p) -> p c", p=128), xbuf[:, :, n + k], reads=[xbuf.b], writes=[db])
        xcs = self.xcs
        for hh in range(2):
            c0 = 8 * hh
            acc = ca[:, 0:8, 0:n]
            jv2 = self.junk[:, 0:8 * n].rearrange("p (c t) -> p c t", c=8)
            for k in range(4):
                wk2 = self.ssdc_w[:, c0:c0 + 8, k:k + 1].to_broadcast([128, 8, n])
                if k == 0:
                    self.tt(acc, ca.b, xbuf[:, c0:c0 + 8, 0:n], xbuf.b, wk2, self.ssdc_w.b, ALU.mult)
                else:
                    self.tt(jv2, self.junk.b, xbuf[:, c0:c0 + 8, k:k + n], xbuf.b, wk2, self.ssdc_w.b, ALU.mult)
                    self.tt(acc, ca.b, acc, ca.b, jv2, self.junk.b, ALU.add)
            bb = self.ssdc_b[:, c0:c0 + 8].unsqueeze(2).to_broadcast([128, 8, n])
            self.tt(acc, ca.b, acc, ca.b, bb, self.ssdc_b.b, ALU.add)
            self.act(xcs[:, c0:c0 + 8, 0:n], xcs.b, acc, ca.b, AF.Silu)
        self.cp(self.sm[:, 16:64].rearrange("p (c k) -> p c k", k=3), self.sm.b, xbuf[:, :, n:n + 3], xbuf.b)
        self.cp(xbuf[:, :, 0:3], xbuf.b, self.sm[:, 16:64].rearrange("p (c k) -> p c k", k=3), self.sm.b)
        x_tm, B_tm = self.x_tm, self.B_tm
        pb = self.psumb()
        for c in range(8):
            self.tr(pb[0:n, c * 128:(c + 1) * 128], pb.b, xcs[:, c, 0:n], xcs.b)
        self.cp(x_tm[0:n, :], x_tm.b, pb[0:n, :], pb.b, eng="act")
        pb = self.psumb()
        for c in range(4):
            self.tr(pb[0:n, c * 128:(c + 1) * 128], pb.b, xcs[:, 8 + c, 0:n], xcs.b)
        self.cp(B_tm[0:n, :], B_tm.b, pb[0:n, 0:512], pb.b, eng="act")
        zs = self.f1
        self.linear_tm(self.hT, n, self.W_in0, 1024, 3072, 1024,
                       lambda o, b, ps, pb_: self.act(zs[0:n, o:o + b], zs.b, ps, pb_, AF.Silu))
        sm = self.sm
        DT, DA, CUM, ECUM, DTE, CL = (sm[0:n, 64:80], sm[0:n, 80:96], sm[0:n, 96:112], sm[0:n, 112:128], sm[0:n, 128:144],
                                     sm[0:n, 144:160])

        def cdt(o, b, ps, pb_):
            self.tt(DT, sm.b, ps, pb_, self.dt_bias[0:n, :], self.dt_bias.b, ALU.add)
            self.act(DT, sm.b, DT, sm.b, AF.Exp)
            self.act(DT, sm.b, DT, sm.b, AF.Ln, bias=1.0)
        self.linear_tm(self.hT, n, self.W_in0, 1024, 6144, 16, cdt)
        self.tt(DA, sm.b, DT, sm.b, self.Aneg[0:n, :], self.Aneg.b, ALU.mult)
        ps = self.psum()
        psx = self.psum()
        self.mm(ps[0:n, 0:16], ps.b, self.U[0:n, 0:n], self.U.b, DA, sm.b)
        self.mm(psx[:, 16:32], psx.b, self.onesf[0:n, :], self.onesf.b, DA, sm.b)
        self.cp(CUM, sm.b, ps[0:n, 0:16], ps.b)
        self.cp(sm[:, 144:160], sm.b, psx[:, 16:32], psx.b)
        self.act(ECUM, sm.b, CUM, sm.b, AF.Exp)
        self.tt(DTE, sm.b, CL, sm.b, CUM, sm.b, ALU.subtract)
        self.act(DTE, sm.b, DTE, sm.b, AF.Exp)
        self.act(sm[:, 144:160], sm.b, sm[:, 144:160], sm.b, AF.Exp)
        xdt, xdte = self.xdt, self.xdte
        self.tt(xdt[0:n, :].rearrange("p (h d) -> p h d", h=16), xdt.b, x_tm[0:n, :].rearrange("p (h d) -> p h d", h=16), x_tm.b,
                DT.unsqueeze(2).to_broadcast([n, 16, 64]), sm.b, ALU.mult)
        self.tt(xdte[0:n, :].rearrange("p (h d) -> p h d", h=16), xdte.b, xdt[0:n, :].rearrange("p (h d) -> p h d", h=16), xdt.b,
                DTE.unsqueeze(2).to_broadcast([n, 16, 64]), sm.b, ALU.mult)
        rmask = self.rmask
        cbm = self.cbm
        psc = self.psum()
        for g in range(4):
            self.mm(psc[0:n, g * n:(g + 1) * n], psc.b, xcs[:, 8 + g, 0:n], xcs.b, xcs[:, 12 + g, 0:n], xcs.b)
        self.tt(cbm[0:n, :, 0:n], cbm.b, psc[0:n, 0:4 * n].rearrange("p (g t) -> p g t", g=4), psc.b,
                self.U[0:n, 0:n].unsqueeze(1).to_broadcast([n, 4, n]), self.U.b, ALU.mult)
        seg, Mh = self.seg, self.Mh
        for g in range(4):
            self.tt(rmask[0:n, :, 0:n], rmask.b, self.U[0:n, 0:n].unsqueeze(1).to_broadcast([n, 4, n]), self.U.b,
                    sm[0:n, 80 + 4 * g:84 + 4 * g].unsqueeze(2).to_broadcast([n, 4, n]), sm.b, ALU.mult)
            ps = self.psum()
            for hh in range(4):
                h = 4 * g + hh
                self.mm(ps[0:n, hh * n:(hh + 1) * n], ps.b, self.onesf[0:n, 0:n], self.onesf.b, rmask[0:n, hh, 0:n], rmask.b)
            for hh in range(4):
                h = 4 * g + hh
                self.ts(seg[0:n, hh, 0:n], seg.b, ps[0:n, hh * n:(hh + 1) * n], ps.b, sm[0:n, 96 + h:97 + h], ALU.min,
                        sm[0:n, 96 + h:97 + h], ALU.subtract, rd=[sm.b])
            self.act(seg[0:n, :, 0:n], seg.b, seg[0:n, :, 0:n], seg.b, AF.Exp)
            self.tt(Mh[0:n, 4 * g:4 * g + 4, 0:n], Mh.b, seg[0:n, :, 0:n], seg.b,
                    cbm[0:n, g:g + 1, 0:n].to_broadcast([n, 4, n]), cbm.b, ALU.mult)
        y = self.f3
        hs, hsb = self.hstate, self.hstate_bf
        for half in range(2):
            psd = self.psum()
            pso = self.psum()
            for hh in range(8):
                h = 8 * half + hh
                self.mm(psd[0:n, hh * 64:(hh + 1) * 64], psd.b, Mh[0:n, h, 0:n], Mh.b, xdt[0:n, h * 64:(h + 1) * 64], xdt.b)
            for gg in range(2):
                g = 2 * half + gg
                self.mm(pso[0:n, gg * 256:(gg + 1) * 256], pso.b, xcs[:, 12 + g, 0:n], xcs.b, hsb[:, g * 256:(g + 1) * 256], hsb.b)
            yv = y[0:n, half * 512:(half + 1) * 512].rearrange("p (h d) -> p h d", h=8)
            self.tt(yv, y.b, pso[0:n, :].rearrange("p (h d) -> p h d", h=8), pso.b,
                    ECUM[:, 8 * half:8 * half + 8].unsqueeze(2).to_broadcast([n, 8, 64]), sm.b, ALU.mult)
            self.tt(y[0:n, half * 512:(half + 1) * 512], y.b, y[0:n, half * 512:(half + 1) * 512], y.b, psd[0:n, :], psd.b, ALU.add)
        jv = self.junk[0:n, :].rearrange("p (h d) -> p h d", h=16)
        self.tt(jv, self.junk.b, x_tm[0:n, :].rearrange("p (h d) -> p h d", h=16), x_tm.b,
                self.Dsk[0:n, :].unsqueeze(2).to_broadcast([n, 16, 64]), self.Dsk.b, ALU.mult)
        self.tt(y[0:n, :], y.b, y[0:n, :], y.b, self.junk[0:n, :], self.junk.b, ALU.add)
        self.tt(y[0:n, :], y.b, y[0:n, :], y.b, zs[0:n, :], zs.b, ALU.mult)
        self.tt(self.junk[0:n, :], self.junk.b, y[0:n, :], y.b, y[0:n, :], y.b, ALU.mult)
        self.red(sm[0:n, 160:164], sm.b, self.junk[0:n, :].rearrange("p (g d) -> p g d", g=4), self.junk.b)
        self.rstd(sm[0:n, 164:168], sm.b, sm[0:n, 160:164], sm.b, 256, 1.0 / 256)
        yb = self.big
        self.tt(yb[0:n, 0:1024].rearrange("p (g d) -> p g d", g=4), yb.b, y[0:n, :].rearrange("p (g d) -> p g d", g=4), y.b,
                sm[0:n, 164:168].unsqueeze(2).to_broadcast([n, 4, 256]), sm.b, ALU.mult)
        for c0 in (0, 4):
            pb = self.psumb()
            for j in range(4):
                self.tr(pb[:, j * n:(j + 1) * n], pb.b, yb[0:n, (c0 + j) * 128:(c0 + j + 1) * 128], yb.b)
            self.tt(ycat[:, 8 + c0:12 + c0, 0:n], ycat.b, pb[:, 0:4 * n].rearrange("p (c t) -> p c t", c=4), pb.b,
                    self.g_ssdn[:, c0:c0 + 4].unsqueeze(2).to_broadcast([128, 4, n]), self.g_ssdn.b, ALU.mult)
        for half in range(2):
            ps = self.psum()
            for gg in range(2):
                g = 2 * half + gg
                self.mm(ps[:, gg * 256:(gg + 1) * 256], ps.b, B_tm[0:n, g * 128:(g + 1) * 128], B_tm.b,
                        xdte[0:n, g * 256:(g + 1) * 256], xdte.b)
            hv = hs[:, half * 512:(half + 1) * 512].rearrange("p (h d) -> p h d", h=8)
            self.tt(hv, hs.b, hv, hs.b, sm[:, 144 + 8 * half:152 + 8 * half].unsqueeze(2).to_broadcast([128, 8, 64]), sm.b, ALU.mult)
            self.tt(hs[:, half * 512:(half + 1) * 512], hs.b, hs[:, half * 512:(half + 1) * 512], hs.b, ps[:, :], ps.b, ALU.add)
        self.cp(hsb[:], hsb.b, hs[:], hs.b, eng="act")
        if last:
            for c in range(8):
                ps = self.psum()
                self.S.op("pe", lambda e: e.transpose(ps[:, 0:128], hs[:, c * 128:(c + 1) * 128], self.identf[:]),
                          reads=[hs.b, self.identf.b], writes=[ps.b])
                self.cp(self.f2[:, 0:128], self.f2.b, ps[:, 0:128], ps.b)
                dst, db = (self.p_ssd, self.b_p_ssd) if self.cur_prompt else (self.s_ssd, self.b_s_ssd)
                self.S.dma("pool", dst[s, c * 128:(c + 1) * 128, :], self.f2[:, 0:128], reads=[self.f2.b], writes=[db])
        self.linear_tm(ycat, n, self.W_out0, 2048, 0, 1024,
                       lambda o, b, ps, pb_: self.tt(x[0:n, o:o + b], x.b, x[0:n, o:o + b], x.b, ps, pb_, ALU.add))

    def headnorm(self, v, vb, n, H, gain, gb, so):
        sm = self.sm
        jv = self.junk[0:n, 0:H * 64].rearrange("p (h d) -> p h d", h=H)
        self.tt(jv, self.junk.b, v, vb, v, vb, ALU.mult)
        self.red(sm[0:n, so:so + H], sm.b, jv, self.junk.b)
        self.rstd(sm[0:n, so + H:so + 2 * H], sm.b, sm[0:n, so:so + H], sm.b, 64, 1.0 / 64)
        self.tt(v, vb, v, vb, sm[0:n, so + H:so + 2 * H].unsqueeze(2).to_broadcast([n, H, 64]), sm.b, ALU.mult)
        self.tt(v, vb, v, vb, gain.unsqueeze(1).to_broadcast([n, H, 64]), gb, ALU.mult)

    def rope(self, v, vb, n, H, cos, sin, cb):
        f2 = self.f2
        t = [f2[0:n, i * 128:i * 128 + H * 8].rearrange("p (h k) -> p h k", k=8) for i in range(4)]
        cb_ = cos.unsqueeze(1).to_broadcast([n, H, 8])
        sb_ = sin.unsqueeze(1).to_broadcast([n, H, 8])
        x1, x2 = v[:, :, 0:8], v[:, :, 8:16]
        self.tt(t[0], f2.b, x1, vb, cb_, cb, ALU.mult)
        self.tt(t[1], f2.b, x2, vb, sb_, cb, ALU.mult)
        self.tt(t[2], f2.b, x2, vb, cb_, cb, ALU.mult)
        self.tt(t[3], f2.b, x1, vb, sb_, cb, ALU.mult)
        self.tt(x1, vb, t[0], f2.b, t[1], f2.b, ALU.subtract)
        self.tt(x2, vb, t[2], f2.b, t[3], f2.b, ALU.add)

    def nsa_front(self, n, cos, sin, cb):
        x, sm = self.xres, self.sm
        self.rmsnorm_T(x, n, self.g_mix[1], self.hT)
        rows, winr, gates, q = self.rows, self.winr, self.gates, self.f1

        def cons(off, nb, ps, pb):
            if off < 1024:
                self.cp(q[0:n, off:off + nb], q.b, ps, pb, eng="act")
            elif off == 1024:
                self.cp(rows[0:n, 0:512], rows.b, ps, pb, eng="act")
            elif off == 1536:
                self.cp(rows[0:n, 512:1024], rows.b, ps, pb, eng="act")
            elif off == 2048:
                self.cp(winr[0:n, :], winr.b, ps, pb, eng="act")
            else:
                self.act(gates[0:n, :], gates.b, ps, pb, AF.Exp, scale=-1.0)
                self.ts(gates[0:n, :], gates.b, gates[0:n, :], gates.b, 1.0, ALU.add)
                self.recip(gates[0:n, :], gates.b, gates[0:n, :], gates.b)
        self.linear_tm(self.hT, n, self.W_in1, 1024, 0, IN1, cons)
        qv = q[0:n, :].rearrange("p (h d) -> p h d", h=16)
        self.headnorm(qv, q.b, n, 16, self.nsaq_g[0:n, :], self.nsaq_g.b, 168)
        self.rope(qv, q.b, n, 16, cos, sin, cb)
        kv_ = rows[0:n, 512:768].rearrange("p (h d) -> p h d", h=4)
        self.headnorm(kv_, rows.b, n, 4, self.nsak_g[0:n, 1, :], self.nsak_g.b, 200)
        self.rope(kv_, rows.b, n, 4, cos, sin, cb)
        kw_ = winr[0:n, 0:256].rearrange("p (h d) -> p h d", h=4)
        self.headnorm(kw_, winr.b, n, 4, self.nsak_g[0:n, 2, :], self.nsak_g.b, 200)
        self.rope(kw_, winr.b, n, 4, cos, sin, cb)
        return qv, kv_, kw_

    def nsa_q_stage(self, n, qv):
        big, q, qT2 = self.big, self.f1, self.qT2
        qd = big[0:n, 0:2048].rearrange("p (h c d) -> p h c d", h=16, c=2)
        self.cp(qd[:, :, 0, :], big.b, qv, q.b, eng="act")
        self.cp(qd[:, :, 1, :], big.b, qv, q.b, eng="pool")
        for c0 in range(0, 16, 8):
            pb = self.psumb()
            for j in range(8):
                a = (c0 + j) * 128
                self.tr(pb[:, j * n:(j + 1) * n], pb.b, big[0:n, a:a + 128], big.b)
            pv = pb[:, 0:8 * n].rearrange("p (c t) -> p c t", c=8)
            self.cp(qT2[0:64, 0, c0:c0 + 8, 0:n], qT2.b, pv[0:64], pb.b, eng="act")
            self.cp(qT2[64:128, 1, c0:c0 + 8, 0:n], qT2.b, pv[64:128], pb.b)

    def layer1(self, s, tau, n):
        S = self.S
        x, sm = self.xres, self.sm
        rows, winr, gates, q = self.rows, self.winr, self.gates, self.f1
        qv, kv_, kw_ = self.nsa_front(n, self.cosT[0:n, tau, :], self.sinT[0:n, tau, :], self.cosT.b)
        S.dma("pool", self.p_nsa_rows[s, tau * 128:tau * 128 + n, :], rows[0:n, :], reads=[rows.b], writes=[self.b_p_nsa_rows])
        w0 = self.n_ptiles - 4
        if tau >= w0:
            S.dma("pool", self.p_nsa_win[s, (tau - w0) * 128:(tau - w0) * 128 + n, :], winr[0:n, :], reads=[winr.b],
                  writes=[self.b_p_nsa_win])
        big = self.big
        self.nsa_q_stage(n, qv)
        self.cp(big[0:n, 2048:2560].rearrange("p (g c d) -> p g c d", g=4, c=2), big.b,
                rows[0:n, 0:512].rearrange("p (c g d) -> p g c d", c=2, g=4), rows.b, eng="pool")
        kst = self.hn[0:n, 0:512].rearrange("p (g c d) -> p g c d", g=4, c=2)
        self.cp(kst[:, :, 0, :], self.hn.b, kv_, rows.b)
        self.cp(kst[:, :, 1, :], self.hn.b, kw_, winr.b)
        self.cp(self.VS1[0:n, tau, :, 0:64], self.VS1.b, rows[0:n, 768:1024].rearrange("p (g d) -> p g d", g=4), rows.b, eng="pool")
        self.cp(self.VW1[0:n, tau % 5, :, 0:64], self.VW1.b, winr[0:n, 256:512].rearrange("p (g d) -> p g d", g=4), winr.b, eng="pool")
        qT2 = self.qT2
        self.transpose_fm(big, n, 4, self.cT, width=128, col0=2048, dview=self.cT[:, :, 0:n])
        self.transpose_fm(self.hn, n, 4, self.kcache, width=128, col0=0, dview=self.kcache[:, :, tau * 128:tau * 128 + n])
        Aall = self.Aall
        ps = self.psum()
        for ab in range(2):
            ov = ps[:, ab * 32:(ab + 1) * 32].rearrange("p (g m) -> p g m", g=4)
            for j in range(16):
                self.mm(ov, ps.b, self.w1[:, ab, j, :], self.w1.b, self.cT[:, :, j:128:16], self.cT.b, start=(j == 0), stop=(j == 15))
        for ab in range(2):
            ov = ps[:, ab * 32:(ab + 1) * 32].rearrange("p (g m) -> p g m", g=4)
            self.ts(Aall[:, ab, :, 8 * tau:8 * tau + 8], Aall.b, ov, ps.b, self.cAB[:, ab:ab + 1], ALU.add, rd=[self.cAB.b])
        self.compress_finish(1)
        self.memset(self.maskc[:], self.maskc.b, 0.0)
        S.op("pool", lambda e: e.affine_select(out=self.maskc[:], in_=self.maskc[:], pattern=[[1, 128]], compare_op=ALU.is_ge,
                                              fill=self.fillreg(NEGM), base=128 * tau - 31, channel_multiplier=-16),
             reads=[self.maskc.b], writes=[self.maskc.b])
        pT, oacc, imp = self.pT, self.oacc, self.imp
        gv = gates[0:n, :].rearrange("p (h k) -> p h k", k=3)
        for g in range(4):
            ps = self.psum()
            pv4 = ps[:, 0:4 * n].rearrange("p (h t) -> p h t", h=4)
            self.mm(pv4, ps.b, self.kcT[:, g, :], self.kcT.b, qT2[:, 0, 4 * g:4 * g + 4, 0:n], qT2.b, start=True, stop=False)
            self.mm(pv4, ps.b, self.ident[:], self.ident.b, self.maskc[:, 0:n].unsqueeze(1).to_broadcast([128, 4, n]), self.maskc.b,
                    start=False, stop=True)
            self.act(pT[:, 0, 0:4 * n], pT.b, ps[:, 0:4 * n], ps.b, AF.Exp)
            ps2 = self.psum()
            for hh in range(4):
                self.mm(ps2[0:n, hh * 97:(hh + 1) * 97], ps2.b, pT[:, 0, hh * n:(hh + 1) * n], pT.b, self.VC1[:, g, :], self.VC1.b)
            pv = ps2[0:n, 0:388].rearrange("p (h c) -> p h c", h=4)
            self.ts(sm[0:n, 208:212], sm.b, pv[:, :, 64], ps2.b, 1e-20, ALU.max)
            self.recip(sm[0:n, 212:216], sm.b, sm[0:n, 208:212], sm.b)
            self.tt(sm[0:n, 216:220], sm.b, sm[0:n, 212:216], sm.b, gv[:, 4 * g:4 * g + 4, 0], gates.b, ALU.mult)
            self.tt(oacc[0:n, g * 256:(g + 1) * 256].rearrange("p (h d) -> p h d", h=4), oacc.b, pv[:, :, 0:64], ps2.b,
                    sm[0:n, 216:220].unsqueeze(2).to_broadcast([n, 4, 64]), sm.b, ALU.mult)
            jv = self.junk[0:n, 0:128].rearrange("p (h s) -> p h s", h=4)
            self.tt(jv, self.junk.b, pv[:, :, 65:97], ps2.b, sm[0:n, 212:216].unsqueeze(2).to_broadcast([n, 4, 32]), sm.b, ALU.mult)
            self.red(imp[0:n, g, :], imp.b, jv.rearrange("p h s -> p s h"), self.junk.b)
        FH = self.selFH
        S.op("pool", lambda e: e.memset(FH[:, 0, :], 1e9), writes=[FH.b])
        S.op("pool", lambda e: e.memset(FH[:, 1, :], 3e38), writes=[FH.b])
        for half in range(2):
            vF = FH[64 * half:64 * half + 64, 0, :]
            vH = FH[64 * half:64 * half + 64, 1, :]
            cur = 2 * tau + half
            S.op("pool", lambda e: e.affine_select(out=vF, in_=vF, pattern=[[1, 32]], compare_op=ALU.is_equal, fill=self.fillreg(0.0),
                                                  base=-cur, channel_multiplier=0), reads=[FH.b], writes=[FH.b])
            S.op("pool", lambda e: e.affine_select(out=vH, in_=vH, pattern=[[-1, 32]], compare_op=ALU.is_ge, fill=self.fillreg(-1e30),
                                                  base=cur, channel_multiplier=0), reads=[FH.b], writes=[FH.b])
        S.op("pool", lambda e: e.memset(FH[:, 0, 0:1], 1e9), reads=[FH.b], writes=[FH.b])
        self.select_topk(n, FH[0:n, 0, :], FH[0:n, 1, :], 32)
        for g in range(4):
            for kap in range(tau + 1):
                ps = self.psum()
                pv4 = ps[:, 0:4 * n].rearrange("p (h t) -> p h t", h=4)
                self.mm(pv4, ps.b, self.kcache[:, g, kap * 128:(kap + 1) * 128], self.kcache.b, qT2[:, 0, 4 * g:4 * g + 4, 0:n], qT2.b,
                        start=True, stop=False)
                self.mm(pv4, ps.b, self.Eall[:, kap, :], self.Eall.b, self.selT[:, g, 0:n].unsqueeze(1).to_broadcast([128, 4, n]),
                        self.selT.b, start=False, stop=(kap != tau))
                if kap == tau:
                    self.mm(pv4, ps.b, self.ident[:], self.ident.b, self.causal_neg[:, 0:n].unsqueeze(1).to_broadcast([128, 4, n]),
                            self.causal_neg.b, start=False, stop=True)
                self.act(pT[:, kap, 0:4 * n], pT.b, ps[:, 0:4 * n], ps.b, AF.Exp)
            self.attn_pv(g, n, range(tau + 1), self.VS1, 1, first=False)
        for g in range(4):
            k0 = max(0, tau - 4)
            for kap in range(k0, tau + 1):
                ps = self.psum()
                pv4 = ps[:, 0:4 * n].rearrange("p (h t) -> p h t", h=4)
                extra = []
                if kap == tau:
                    extra.append(self.causal_neg)
                if kap == tau - 4:
                    extra.append(self.win_neg)
                self.mm(pv4, ps.b, self.kcache[:, g, kap * 128:(kap + 1) * 128], self.kcache.b, qT2[:, 1, 4 * g:4 * g + 4, 0:n], qT2.b,
                        start=True, stop=(len(extra) == 0))
                for i, m in enumerate(extra):
                    self.mm(pv4, ps.b, self.ident[:], self.ident.b, m[:, 0:n].unsqueeze(1).to_broadcast([128, 4, n]), m.b,
                            start=False, stop=(i == len(extra) - 1))
                self.act(pT[:, kap, 0:4 * n], pT.b, ps[:, 0:4 * n], ps.b, AF.Exp)
            self.attn_pv(g, n, range(k0, tau + 1), self.VW1, 2, first=False, slot=lambda k: k % 5)
        self.cp(big[0:n, 0:1024], big.b, oacc[0:n, :], oacc.b, eng="act")
        self.transpose_fm(big, n, 8, self.bigT)
        self.linear_tm(self.bigT, n, self.W_out1, 1024, 0, 1024,
                       lambda o, b, ps, pb_: self.tt(x[0:n, o:o + b], x.b, x[0:n, o:o + b], x.b, ps, pb_, ALU.add))

    def attn_pv(self, g, n, kaps, V, gi, first, slot=lambda k: k):
        sm, pT, oacc = self.sm, self.pT, self.oacc
        gv = self.gates[0:n, :].rearrange("p (h k) -> p h k", k=3)
        kaps = list(kaps)
        ps2 = self.psum()
        for hh in range(4):
            for i, kap in enumerate(kaps):
                self.mm(ps2[0:n, hh * 65:(hh + 1) * 65], ps2.b, pT[:, kap, hh * n:(hh + 1) * n], pT.b, V[:, slot(kap), g, :], V.b,
                        start=(i == 0), stop=(i == len(kaps) - 1))
        pv = ps2[0:n, 0:260].rearrange("p (h c) -> p h c", h=4)
        self.ts(sm[0:n, 208:212], sm.b, pv[:, :, 64], ps2.b, 1e-20, ALU.max)
        self.recip(sm[0:n, 212:216], sm.b, sm[0:n, 208:212], sm.b)
        self.tt(sm[0:n, 216:220], sm.b, sm[0:n, 212:216], sm.b, gv[:, 4 * g:4 * g + 4, gi], self.gates.b, ALU.mult)
        ov = oacc[0:n, g * 256:(g + 1) * 256].rearrange("p (h d) -> p h d", h=4)
        gb = sm[0:n, 216:220].unsqueeze(2).to_broadcast([n, 4, 64])
        if first:
            self.tt(ov, oacc.b, pv[:, :, 0:64], ps2.b, gb, sm.b, ALU.mult)
        else:
            jv = self.junk[0:n, 0:256].rearrange("p (h d) -> p h d", h=4)
            self.tt(jv, self.junk.b, pv[:, :, 0:64], ps2.b, gb, sm.b, ALU.mult)
            self.tt(ov, oacc.b, ov, oacc.b, jv, self.junk.b, ALU.add)

    def compress_finish(self, nchunk):
        Aall, pre = self.Aall, self.pre
        jv = self.junk[:, 0:512].rearrange("p (g t) -> p g t", g=4)
        self.tt(jv[:, :, 0:127], self.junk.b, Aall[:, 0, :, 0:127], Aall.b, Aall[:, 1, :, 1:128], Aall.b, ALU.add)
        self.act(pre[:, :, 0:127], pre.b, jv[:, :, 0:127], self.junk.b, AF.Silu)
        ps = self.psum()
        for g in range(4):
            self.mm(ps[:, g * 64:(g + 1) * 64], ps.b, pre[0:64, g, :], pre.b, self.w2[0:64, :], self.w2.b)
        kc = self.f2[:, 512:768]
        self.cp(kc, self.f2.b, ps[:, 0:256], ps.b, eng="act")
        kcv = kc.rearrange("p (g d) -> p g d", g=4)
        self.headnorm(kcv, self.f2.b, 128, 4, self.nsak_g[:, 0, :], self.nsak_g.b, 200)
        self.rope(kcv, self.f2.b, 128, 4, self.cosC[:, 0, :], self.sinC[:, 0, :], self.cosC.b)
        self.cp(self.hn[:, 0:256], self.hn.b, kc, self.f2.b)
        self.transpose_fm(self.hn, 128, 4, self.kcT, width=64, dview=self.kcT[0:64, :, 0:128])
        ps = self.psum()
        for g in range(4):
            self.mm(ps[:, g * 64:(g + 1) * 64], ps.b, pre[64:128, g, :], pre.b, self.w2[64:128, :], self.w2.b)
        self.cp(self.VC1[:, :, 0:64], self.VC1.b, ps[:, 0:256].rearrange("p (g d) -> p g d", g=4), ps.b)

    def select_topk(self, n, F, H, nslc):
        sm, imp = self.sm, self.imp
        iv = imp[0:n, :, 0:nslc]
        self.tt(iv, imp.b, iv, imp.b, F.unsqueeze(1).to_broadcast([n, 4, nslc]), self.selFH.b, ALU.max)
        self.tt(iv, imp.b, iv, imp.b, H.unsqueeze(1).to_broadcast([n, 4, nslc]), self.selFH.b, ALU.min)
        work = self.junk
        for g in range(4):
            S = self.S
            S.op("dve", lambda e: e.max(out=sm[0:n, 224:232], in_=imp[0:n, g, 0:nslc]), reads=[imp.b], writes=[sm.b])
            S.op("dve", lambda e: e.match_replace(out=work[0:n, 0:nslc], in_to_replace=sm[0:n, 224:232],
                                                  in_values=imp[0:n, g, 0:nslc], imm_value=-3e38),
                 reads=[imp.b, sm.b], writes=[work.b])
            S.op("dve", lambda e: e.max(out=sm[0:n, 232:240], in_=work[0:n, 0:nslc]), reads=[work.b], writes=[sm.b])
            self.ts(work[0:n, 64:64 + nslc], work.b, imp[0:n, g, 0:nslc], imp.b, sm[0:n, 239:240], ALU.is_ge, rd=[sm.b])
            self.stt(work[0:n, 128:128 + nslc], work.b, imp[0:n, g, 0:nslc], imp.b, -5e29, work[0:n, 64:64 + nslc], work.b,
                     ALU.is_gt, ALU.mult)
            self.ts(self.hn[0:n, 256 + g * nslc:256 + (g + 1) * nslc], self.hn.b, work[0:n, 128:128 + nslc], work.b, -1.0, ALU.add)
        pb = self.psumb()
        for g in range(4):
            self.tr(pb[0:nslc, g * n:(g + 1) * n], pb.b, self.hn[0:n, 256 + g * nslc:256 + (g + 1) * nslc], self.hn.b)
        self.cp(self.selT[0:nslc, :, 0:n], self.selT.b, pb[0:nslc, 0:4 * n].rearrange("p (g t) -> p g t", g=4), pb.b, eng="act")


    def sample_seq(self, b):
        S = self.S
        n = TS
        self.cur_prompt = False
        ms = self.memset
        ubuf, xbuf, hs, hsb = self.ubuf, self.xbuf, self.hstate, self.hstate_bf
        with self.nc.allow_non_contiguous_dma("tiny state loads"):
            for k in range(2):
                S.dma("pool", ubuf[:, :, k], self.state_sconv[b, k].rearrange("(c p) -> p c", p=128), writes=[ubuf.b])
            for k in range(3):
                S.dma("pool", xbuf[:, :, k], self.state_ssd_conv[b, k].rearrange("(c p) -> p c", p=128), writes=[xbuf.b])
        for c in range(8):
            t = self.f2
            S.dma("pool", t[:, 0:128], self.state_ssd[b, c * 128:(c + 1) * 128, :], writes=[t.b])
            ps = self.psum()
            S.op("pe", lambda e: e.transpose(ps[:, 0:128], t[:, 0:128], self.identf[:]), reads=[t.b, self.identf.b], writes=[ps.b])
            self.cp(hs[:, c * 128:(c + 1) * 128], hs.b, ps[:, 0:128], ps.b)
        self.cp(hsb[:], hsb.b, hs[:], hs.b, eng="act")
        for l in range(2):
            for mc in range(2):
                t = self.f1
                S.dma("pool", t[:], self.cache_mem_kv[l, b, mc * 128:(mc + 1) * 128, :], writes=[t.b])
                self.cp(self.big[:, 0:512], self.big.b, t[:, 0:512], t.b, eng="act")
                pb = self.psumb()
                for h in range(4):
                    self.tr(pb[:, h * 128:(h + 1) * 128], pb.b, self.big[:, h * 128:(h + 1) * 128], self.big.b)
                self.cp(self.KT[l][:, :, mc * 128:(mc + 1) * 128], self.KT[l].b, pb[:, 0:512].rearrange("p (h t) -> p h t", h=4), pb.b)
                self.cp(self.V1[l][:, mc, :, 0:128], self.V1[l].b, t[:, 512:1024].rearrange("p (h d) -> p h d", h=4), t.b, eng="pool")
        x = self.xres
        S.dma("pool", x[0:n, :], self.x_sample[b], writes=[x.b])
        self.layer0(b, 0, n, last=True)
        self.mem_attn(0, n)
        self.ffn(0, n)
        self.layer1_sample(b)
        self.mem_attn(1, n)
        self.ffn(1, n)
        S.dma("pool", self.y_sample[b], x[0:n, :], reads=[x.b], writes=[self.b_y_sample])

    def layer1_sample(self, b):
        S = self.S
        n = TS
        x, sm = self.xres, self.sm
        rows, winr, gates, q = self.rows, self.winr, self.gates, self.f1
        qv, kv_, kw_ = self.nsa_front(n, self.cosS[0:n, 0, :], self.sinS[0:n, 0, :], self.cosS.b)
        S.dma("pool", self.s_nsa_rows[b], rows[0:n, :], reads=[rows.b], writes=[self.b_s_nsa_rows])
        S.dma("pool", self.s_nsa_win[b, 0:504, :], self.cache_nsa_win[b, 8:512, :], writes=[self.b_s_nsa_win])
        S.dma("pool", self.s_nsa_win[b, 504:512, :], winr[0:n, :], reads=[winr.b], writes=[self.b_s_nsa_win])
        self.nsa_q_stage(n, qv)
        kst = self.hn[0:n, 0:512].rearrange("p (g c d) -> p g c d", g=4, c=2)
        self.cp(kst[:, :, 0, :], self.hn.b, kv_, rows.b)
        self.cp(kst[:, :, 1, :], self.hn.b, kw_, winr.b)
        S.barrier()
        pTf = self.pT.t[:].rearrange("p a b -> p (a b)")
        o = [0]

        def carve(name, cols, shape_str=None, **kw):
            v = pTf[:, o[0]:o[0] + cols]
            o[0] += cols
            if shape_str:
                v = v.rearrange(shape_str, **kw)
            return TT(v, name)
        eT = carve("eT", 512, "p (c t) -> p c t", c=16)
        pgb = [carve("pgb%d" % i, 512, "p (g c d) -> p g c d", g=4, c=2) for i in range(2)]
        kst2 = [carve("kst2_%d" % i, 512, "p (g c d) -> p g c d", g=4, c=2) for i in range(2)]
        kTp = [carve("kTp%d" % i, 512, "p (g t) -> p g t", g=4) for i in range(2)]
        pTs = [carve("pTs%d" % i, 128) for i in range(4)]
        VC1s = carve("VC1s", 8 * 4 * 65, "p (c g d) -> p c g d", c=8, g=4)
        selTs = carve("selTs", 256, "p (c g t) -> p c g t", c=8, g=4)
        kTn = carve("kTn", 32, "p (g t) -> p g t", g=4)
        eTg = carve("eTg", 64, "p (c t) -> p c t", c=8)
        cTs = [carve("cTs%d" % i, 576, "p (g t) -> p g t", g=4) for i in range(2)]
        pre_all = TT(self.kcache.t[:].rearrange("p g t -> p (g t)")[:, 0:4096].rearrange("p (g t) -> p g t", g=4), "pre_all")
        kcTs = TT(self.VS1.t[:].rearrange("p a g d -> p (a g d)")[:, 0:4096].rearrange("p (g t) -> p g t", g=4), "kcTs")
        Vpg = [TT(self.VW1.t[:, i], "Vpg%d" % i) for i in (0, 1, 4)]
        Vn_s = TT(self.VW1.t[:, 2], "Vn_s")
        Vn_w = TT(self.VW1.t[:, 3], "Vn_w")
        xbf = self.xbuf.t[:].rearrange("p c t -> p (c t)")
        pg = [TT(self.hstate.t[:, i * 512:(i + 1) * 512], "pg%d" % i) for i in range(2)] + \
             [TT(xbf[:, i * 512:(i + 1) * 512], "pgx%d" % i) for i in range(4)]
        NPG = len(pg)
        Af = self.Aall.t[:].rearrange("p a c d -> p (a c d)")
        otm = TT(Af[0:8, 0:1024], "otm")
        cbf = self.cbm.t[:].rearrange("p a b -> p (a b)")
        abc = TT(cbf[:, 0:72].rearrange("p (a g m) -> p a g m", a=2, g=4), "abc")
        Fs = TT(cbf[0:8, 128:384], "Fs")
        acc = TT(self.rmask.t[0:32].rearrange("p a b -> p (a b)")[:, 0:260], "acc")
        imp_s = TT(self.f1.t[0:8, 0:1024].rearrange("p (g s) -> p g s", g=4), "imp_s", self.f1.b)
        idxA = TT(self.seg.t[:].rearrange("p a b -> p (a b)")[:, 0:128].bitcast(I32), "idxA")
        idxB = TT(self.seg.t[:].rearrange("p a b -> p (a b)")[:, 128:256].bitcast(I32), "idxB")
        ptf = TT(self.seg.t[:].rearrange("p a b -> p (a b)")[:, 256:384], "ptf")
        cache2 = self.cache_nsa_kv.rearrange("r (h c) -> (r h) c", h=2)
        ms = self.memset
        it = self.itmp
        S.dma("pool", it[:, 0:128], self.page_table[b:b + 1, :].partition_broadcast(128), writes=[it.b])
        self.cp(ptf[:], ptf.b, it[:, 0:128], it.b)
        self.ts(ptf[:], ptf.b, ptf[:], ptf.b, 256.0, ALU.mult, self.piota2[:, 0:1], ALU.add, rd=[self.piota2.b])
        self.cp(idxA[:], idxA.b, ptf[:], ptf.b)
        self.ts(ptf[:], ptf.b, ptf[:], ptf.b, 1.0, ALU.add)
        self.cp(idxB[:], idxB.b, ptf[:], ptf.b)
        self.transpose_fm(self.hn, n, 4, kTn, width=128, dview=kTn[:, :, 0:n])
        self.cp(Vn_s[0:n, :, 0:64], Vn_s.b, rows[0:n, 768:1024].rearrange("p (g d) -> p g d", g=4), rows.b)
        self.cp(Vn_w[0:n, :, 0:64], Vn_w.b, winr[0:n, 256:512].rearrange("p (g d) -> p g d", g=4), winr.b)
        ms(pre_all[:], pre_all.b, 0.0)
        ms(kcTs[:], kcTs.b, 0.0)
        ms(VC1s[:], VC1s.b, 1.0)
        ms(selTs[:], selTs.b, 0.0)

        def gather(dst, idx, j):
            S.dma("pool", None, None, fn=lambda e: e.indirect_dma_start(
                out=dst[:], out_offset=None, in_=cache2, in_offset=bass.IndirectOffsetOnAxis(ap=idx[:, j:j + 1], axis=0)),
                reads=[idx.b], writes=[dst.b])
        for i in range(2):
            ms(cTs[i][:], cTs[i].b, 0.0)
        cb2 = self.junk[:, 1000:1001]
        self.tt(cb2, self.junk.b, self.cAB[:, 0:1], self.cAB.b, self.cAB[:, 1:2], self.cAB.b, ALU.add)
        csum = TT(cbf[:, 100:101], "csum")
        self.cp(csum[:], csum.b, cb2, self.junk.b)
        cT3 = cTs + [TT(self.big.t[:, 0:576].rearrange("p (g t) -> p g t", g=4), "cTs2", self.big.b)]
        ms(cT3[2][:], cT3[2].b, 0.0)

        def p1_a(j):
            p_ = pg[j % NPG]
            gather(p_, idxA, j)
            pb_ = pgb[j % 2]
            self.cp(pb_[:], pb_.b, p_[:].rearrange("p (c g d) -> p g c d", c=2, g=4), p_.b, eng=("act" if j % 2 else "dve"))
            ct = cT3[j % 3]
            pbk = self.psumb()
            for g in range(4):
                self.tr(pbk[:, g * 128:(g + 1) * 128], pbk.b, pb_[:, g].rearrange("p c d -> p (c d)"), pb_.b)
            self.cp(ct[:, :, 16:144], ct.b, pbk[:, 0:512].rearrange("p (g t) -> p g t", g=4), pbk.b, eng=("dve" if j % 2 else "act"))
            nxt = cT3[(j + 1) % 3]
            self.cp(nxt[:, :, 0:16], nxt.b, ct[:, :, 128:144], ct.b, eng=("dve" if j % 2 else "act"))

        def p1_b(j):
            ct = cT3[j % 3]
            ps = self.psum()
            ov = ps[:, 0:32].rearrange("p (g m) -> p g m", g=4)
            for jj in range(16):
                self.mm(ov, ps.b, self.w1[:, 0, jj, :], self.w1.b, ct[:, :, jj:128:16], ct.b, start=(jj == 0), stop=False)
            for jj in range(16):
                self.mm(ov, ps.b, self.w1[:, 1, jj, :], self.w1.b, ct[:, :, 16 + jj:144:16], ct.b, start=False, stop=(jj == 15))
            m0 = 1 if j == 0 else 0
            self.act(pre_all[:, :, 8 * j - 1 + m0:8 * j + 7], pre_all.b, ov[:, :, m0:8], ps.b, AF.Silu, bias=csum[:, 0:1], rd=[csum.b])
        for it in range(NPAGES + 1):
            if it < NPAGES:
                p1_a(it)
            if it >= 1:
                p1_b(it - 1)
        for c in range(8):
            ps = self.psum()
            for g in range(4):
                self.mm(ps[:, g * 64:(g + 1) * 64], ps.b, pre_all[0:64, g, c * 128:(c + 1) * 128], pre_all.b, self.w2[0:64, :], self.w2.b)
            kc = self.f2[:, 512:768]
            self.cp(kc, self.f2.b, ps[:, 0:256], ps.b, eng="act")
            kcv = kc.rearrange("p (g d) -> p g d", g=4)
            self.headnorm(kcv, self.f2.b, 128, 4, self.nsak_g[:, 0, :], self.nsak_g.b, 200)
            self.rope(kcv, self.f2.b, 128, 4, self.cosC[:, c, :], self.sinC[:, c, :], self.cosC.b)
            self.cp(self.hn[:, 512:768], self.hn.b, kc, self.f2.b)
            self.transpose_fm(self.hn, 128, 4, kcTs, width=64, col0=512, dview=kcTs[0:64, :, c * 128:(c + 1) * 128])
            ps = self.psum()
            for g in range(4):
                self.mm(ps[:, g * 64:(g + 1) * 64], ps.b, pre_all[64:128, g, c * 128:(c + 1) * 128], pre_all.b, self.w2[64:128, :], self.w2.b)
            self.cp(VC1s[:, c, :, 0:64], VC1s.b, ps[:, 0:256].rearrange("p (g d) -> p g d", g=4), ps.b)
        ms(self.maskc[:], self.maskc.b, 0.0)
        S.op("pool", lambda e: e.affine_select(out=self.maskc[:, 0:8], in_=self.maskc[:, 0:8], pattern=[[1, 8]], compare_op=ALU.is_ge,
                                              fill=self.fillreg(NEGM), base=2017, channel_multiplier=-16),
             reads=[self.maskc.b], writes=[self.maskc.b])
        qT2 = self.qT2
        savedpT = self.pT
        self.pT = eT
        for g in range(4):
            for c in range(8):
                ps = self.psum()
                pv4 = ps[:, 0:4 * n].rearrange("p (h t) -> p h t", h=4)
                self.mm(pv4, ps.b, kcTs[:, g, c * 128:(c + 1) * 128], kcTs.b, qT2[:, 0, 4 * g:4 * g + 4, 0:n], qT2.b, start=True, stop=(c != 7))
                if c == 7:
                    self.mm(pv4, ps.b, self.ident[:], self.ident.b, self.maskc[:, 0:n].unsqueeze(1).to_broadcast([128, 4, n]),
                            self.maskc.b, start=False, stop=True)
                self.act(eT[:, c, 0:4 * n], eT.b, ps[:, 0:4 * n], ps.b, AF.Exp)
            self.attn_pv(g, n, range(8), VC1s, 0, first=True)
            ps3 = self.psum()
            for c in range(8):
                self.mm(ps3[:, 0:32], ps3.b, self.ones_bf[:], self.ones_bf.b, eT[:, c, 0:32], eT.b, start=(c == 0), stop=(c == 7))
            rdb = self.junk[:, 0:32]
            self.ts(rdb, self.junk.b, ps3[:, 0:32], ps3.b, 1e-20, ALU.max)
            self.recip(rdb, self.junk.b, rdb, self.junk.b)
            self.tt(eT[:, 8:16, 0:32], eT.b, eT[:, 0:8, 0:32], eT.b, rdb.unsqueeze(1).to_broadcast([128, 8, 32]), self.junk.b, ALU.mult)
            etf = self.junk[:, 64:128].rearrange("p (c t) -> p c t", c=8)
            self.red(etf, self.junk.b, eT[:, 8:16, 0:32].rearrange("p c (h t) -> p c t h", h=4), eT.b)
            self.cp(eTg[:], eTg.b, etf, self.junk.b)
            ps4 = self.psum()
            for c in range(8):
                self.mm(ps4[0:n, c * 33:(c + 1) * 33], ps4.b, eTg[:, c, :], eTg.b, self.mov33[:], self.mov33.b)
            u = ps4[0:n, 0:264].rearrange("p (c s) -> p c s", c=8)
            self.cp(imp_s[:, g, 0:256].rearrange("p (c s) -> p c s", c=8), imp_s.b, u[:, :, 0:32], ps4.b)
            self.tt(imp_s[:, g, 32:256:32], imp_s.b, imp_s[:, g, 32:256:32], imp_s.b, u[:, 0:7, 32], ps4.b, ALU.add)
        self.pT = savedpT
        ms(Fs[:], Fs.b, 0.0)
        ms(Fs[:, 0:1], Fs.b, 1e9)
        work = self.junk
        stage = self.big
        for g in range(4):
            iv = imp_s[:, g, 0:256]
            self.tt(iv, imp_s.b, iv, imp_s.b, Fs[:], Fs.b, ALU.max)
            S.op("dve", lambda e: e.max(out=sm[0:n, 224:232], in_=iv), reads=[imp_s.b], writes=[sm.b])
            S.op("dve", lambda e: e.match_replace(out=work[0:n, 0:256], in_to_replace=sm[0:n, 224:232], in_values=iv, imm_value=-3e38),
                 reads=[imp_s.b, sm.b], writes=[work.b])
            S.op("dve", lambda e: e.max(out=sm[0:n, 232:240], in_=work[0:n, 0:256]), reads=[work.b], writes=[sm.b])
            self.ts(work[0:n, 512:768], work.b, iv, imp_s.b, sm[0:n, 238:239], ALU.is_ge, rd=[sm.b])
            self.ts(stage[0:n, g * 256:(g + 1) * 256], stage.b, work[0:n, 512:768], work.b, -1.0, ALU.add)
        for g in range(4):
            pb = self.psumb()
            for c8 in range(8):
                self.tr(pb[0:32, c8 * n:(c8 + 1) * n], pb.b, stage[0:n, g * 256 + c8 * 32:g * 256 + (c8 + 1) * 32], stage.b)
            self.cp(selTs[0:32, :, g, :], selTs.b, pb[0:32, 0:8 * n].rearrange("p (c t) -> p c t", c=8), pb.b, eng="act")

        def key_tile(kT, nk, qsel, V, masks, i):
            key_tile_c(key_tile_b(kT, nk, qsel, masks, i), nk, V)

        def key_tile_b(kT, nk, qsel, masks, i):
            ps = self.psum()
            for g in range(4):
                pv4 = ps[0:nk, g * 32:(g + 1) * 32].rearrange("p (h t) -> p h t", h=4)
                self.mm(pv4, ps.b, kT[:, g, 0:nk], kT.b, qT2[:, qsel, 4 * g:4 * g + 4, 0:n], qT2.b, start=True, stop=(len(masks) == 0))
                for mi, (ml, mlb, mr, mrb, per_g) in enumerate(masks):
                    rhs = mr(g) if per_g else mr
                    self.mm(pv4, ps.b, ml, mlb, rhs, mrb, start=False, stop=(mi == len(masks) - 1))
            pt_ = pTs[i % 4]
            self.act(pt_[0:nk, :], pt_.b, ps[0:nk, 0:128], ps.b, AF.Exp)
            return pt_

        def key_tile_c(pt_, nk, V):
            ps2 = self.psum()
            for g in range(4):
                self.mm(ps2[0:32, g * 65:(g + 1) * 65], ps2.b, pt_[0:nk, g * 32:(g + 1) * 32], pt_.b, V[0:nk, g, :], V.b)
            self.tt(acc[:], acc.b, acc[:], acc.b, ps2[0:32, 0:260], ps2.b, ALU.add)

        def branch_finish(gi):
            av = acc[:].rearrange("p (g c) -> p g c", g=4)
            self.ts(sm[0:32, 208:212], sm.b, av[:, :, 64], acc.b, 1e-20, ALU.max)
            self.recip(sm[0:32, 212:216], sm.b, sm[0:32, 208:212], sm.b)
            onb = self.junk[0:32, 0:256].rearrange("p (g d) -> p g d", g=4)
            self.tt(onb, self.junk.b, av[:, :, 0:64], acc.b, sm[0:32, 212:216].unsqueeze(2).to_broadcast([32, 4, 64]), sm.b, ALU.mult)
            ov = otm[:].rearrange("p (g h d) -> p g h d", g=4, h=4)
            for hh in range(4):
                S.dma("pool", ov[:, :, hh, :], onb[hh * 8:(hh + 1) * 8], reads=[self.junk.b], writes=[otm.b])
            gv = gates[0:n, :].rearrange("p (h k) -> p h k", k=3)
            o3 = otm[:].rearrange("p (h d) -> p h d", h=16)
            self.tt(o3, otm.b, o3, otm.b, gv[:, :, gi:gi + 1].to_broadcast([n, 16, 64]), gates.b, ALU.mult)
            self.tt(self.oacc[0:n, :], self.oacc.b, self.oacc[0:n, :], self.oacc.b, otm[:], otm.b, ALU.add)

        idb, cnb = self.ident, self.causal_neg
        new_masks = [(idb[:, 0:n], idb.b, cnb[:, 0:n].unsqueeze(1).to_broadcast([128, 4, n]), cnb.b, False)]
        kst4 = kst2 + pgb
        kT4 = kTp + [TT(c.t[:, :, 0:128], c.b.name + "v", c.b) for c in cTs]
        ms(acc[:], acc.b, 0.0)
        pts = {}

        def stage_a(kap):
            p_ = pg[kap % NPG]
            gather(p_, idxB, kap)
            ks = kst4[kap % 4]
            self.cp(ks[:], ks.b, p_[:].rearrange("p (c g d) -> p g c d", c=2, g=4), p_.b, eng=("act" if kap % 2 else "dve"))
            V = Vpg[kap % 3]
            self.cp(V[:, :, 0:64], V.b, p_[:, 256:512].rearrange("p (g d) -> p g d", g=4), p_.b, eng=("dve" if kap % 2 else "act"))
            kT = kT4[kap % 4]
            pb = self.psumb()
            for g in range(4):
                self.tr(pb[:, g * 128:(g + 1) * 128], pb.b, ks[:, g].rearrange("p c d -> p (c d)"), ks.b)
            self.cp(kT[:], kT.b, pb[:, 0:512].rearrange("p (g t) -> p g t", g=4), pb.b, eng=("dve" if kap % 2 else "act"))

        def stage_b(kap):
            msk = [(self.Eall[:, kap % 16, :], self.Eall.b,
                    (lambda g, kap=kap: selTs[:, kap // 16, g, :].unsqueeze(1).to_broadcast([128, 4, n])), selTs.b, True)]
            pts[kap] = key_tile_b(kT4[kap % 4], 128, 0, msk, kap)

        def stage_c(kap):
            key_tile_c(pts.pop(kap), 128, Vpg[kap % 3])
        for it in range(NPAGES + 2):
            if it < NPAGES:
                stage_a(it)
            if 0 <= it - 1 < NPAGES:
                stage_b(it - 1)
            if 0 <= it - 2 < NPAGES:
                stage_c(it - 2)
        key_tile(kTn, n, 0, Vn_s, new_masks, 0)
        branch_finish(1)
        ms(acc[:], acc.b, 0.0)
        wnb = self.win_neg
        for w in range(4):
            p_ = pg[w % 2]
            S.dma("pool", p_[:], self.cache_nsa_win[b, w * 128:(w + 1) * 128, :], writes=[p_.b])
            ks = kst2[w % 2]
            self.cp(ks[:], ks.b, p_[:].rearrange("p (c g d) -> p g c d", c=2, g=4), p_.b)
            V = Vpg[w % 2]
            self.cp(V[:, :, 0:64], V.b, p_[:, 256:512].rearrange("p (g d) -> p g d", g=4), p_.b, eng="pool")
            kT = kTp[w % 2]
            pb = self.psumb()
            for g in range(4):
                self.tr(pb[:, g * 128:(g + 1) * 128], pb.b, ks[:, g].rearrange("p c d -> p (c d)"), ks.b)
            self.cp(kT[:], kT.b, pb[:, 0:512].rearrange("p (g t) -> p g t", g=4), pb.b, eng="act")
            msk = []
            if w == 0:
                msk = [(idb[:], idb.b, wnb[:, 0:n].unsqueeze(1).to_broadcast([128, 4, n]), wnb.b, False)]
            key_tile(kT, 128, 0, V, msk, w)
        key_tile(kTn, n, 1, Vn_w, new_masks, 0)
        branch_finish(2)
        S.barrier()
        big = self.big
        self.cp(big[0:n, 0:1024], big.b, self.oacc[0:n, :], self.oacc.b, eng="act")
        self.transpose_fm(big, n, 8, self.bigT)
        self.linear_tm(self.bigT, n, self.W_out1, 1024, 0, 1024,
                       lambda o_, b_, ps, pb_: self.tt(x[0:n, o_:o_ + b_], x.b, x[0:n, o_:o_ + b_], x.b, ps, pb_, ALU.add))


def build_program(**kw):
    p = Prog(**kw)
    p.cur_prompt = True
    p.cl_b = None
    return p


_NAMES = ["y_prompt", "y_sample", "p_sconv", "p_ssd_conv", "p_ssd", "p_nsa_rows", "p_nsa_win", "p_mem_kv",
          "s_sconv", "s_ssd_conv", "s_ssd", "s_nsa_rows", "s_nsa_win"]


def shard_inputs(inp, c):
    f = np.ascontiguousarray
    m = {}
    m["x_prompt"] = f(inp["x_prompt"][NPB * c:NPB * (c + 1)])
    m["x_sample"] = f(inp["x_sample"][NSB * c:NSB * (c + 1)])
    m["mem_prompt"] = f(inp["mem_prompt"][NPB * c:NPB * (c + 1)])
    m["state_sconv"] = f(inp["state_sconv"][0, NSB * c:NSB * (c + 1)])
    m["state_ssd_conv"] = f(inp["state_ssd_conv"][0, NSB * c:NSB * (c + 1)])
    m["state_ssd"] = f(inp["state_ssd"][0, NSB * c:NSB * (c + 1)]).reshape(NSB, 1024, 128)
    m["cache_nsa_kv"] = inp["cache_nsa_kv"].reshape(-1, 1024)
    m["cache_nsa_win"] = f(inp["cache_nsa_win"][0, NSB * c:NSB * (c + 1)]).reshape(NSB, 512, 512)
    m["cache_mem_kv"] = f(inp["cache_mem_kv"][:, NSB * c:NSB * (c + 1)]).reshape(2, NSB, 256, 1024)
    m["page_table"] = f(inp["page_table"][NSB * c:NSB * (c + 1)])
    for k in ["norm_mix", "norm_mem", "norm_memsrc", "norm_ffn", "ab_ssd_conv_b", "ab_dt_bias", "ab_a_log", "ab_d",
              "ab_ssd_norm", "nsa_q_norm", "mem_w_q", "mem_q_norm", "mem_w_kv", "mem_k_norm", "mem_w_o",
              "ffn_w_gate", "ffn_w_up", "ffn_w_down"]:
        m[k] = f(inp[k])
    for k in ["ab_w_in", "ab_sconv_w", "ab_ssd_conv_w", "ab_w_out", "nsa_w_in", "nsa_k_norm", "nsa_cmp_pe", "nsa_cmp_w1",
              "nsa_cmp_w2", "nsa_w_out"]:
        m[k] = f(inp[k][0])
    return m


def gather_outputs(results):
    cat = lambda name, ax: np.concatenate([r[name] for r in results], axis=ax)
    y_prompt = cat("y_prompt", 0)
    y_sample = cat("y_sample", 0)
    p_sconv = cat("p_sconv", 0)[None]
    p_ssd_conv = cat("p_ssd_conv", 0)[None]
    p_ssd = cat("p_ssd", 0).reshape(1, 16, 16, 64, 128)
    p_rows = cat("p_nsa_rows", 0).reshape(16, SEQ, 1, 4, 4, 64)
    p_win = cat("p_nsa_win", 0).reshape(1, 16, 512, 2, 4, 64)
    p_mem = cat("p_mem_kv", 1).reshape(2, 16, 256, 2, 4, 128)
    s_sconv = cat("s_sconv", 0)[None]
    s_ssd_conv = cat("s_ssd_conv", 0)[None]
    s_ssd = cat("s_ssd", 0).reshape(1, 32, 16, 64, 128)
    s_rows = cat("s_nsa_rows", 0).reshape(32, TS, 1, 4, 4, 64)
    s_win = cat("s_nsa_win", 0).reshape(1, 32, 512, 2, 4, 64)
    return (y_prompt, y_sample, p_sconv, p_ssd_conv, p_ssd, p_rows, p_win, p_mem,
            s_sconv, s_ssd_conv, s_ssd, s_rows, s_win)


def kernel(**inputs):
    inp = {k: np.asarray(v) for k, v in inputs.items()}
    p = build_program()
    nc = p.build()
    in_maps = [shard_inputs(inp, c) for c in range(NCORES)]
    res = run_bass_kernel_spmd(nc, in_maps, core_ids=list(range(NCORES)))
    return gather_outputs(res.results)
```
